# Optimizing a Trainium2 kernel written in Bass

```python
import jax, jax.numpy as jnp
from jax import lax
import numpy as np

D_MODEL = 1024
BATCH = 8
SEQ = 2048
DEPTH = 1
DEC_BATCH = 32
DEC_SEQ = 1
PAST_LEN = 8192
PAGE_SIZE = 128

N_META = 16
SB_HEADS = 8
HEAD_DIM = 64
D_ATT = SB_HEADS * HEAD_DIM
D_CONV = D_MODEL // 2
CONV_K = 3
D_FF = -(-8 * D_MODEL // (3 * 256)) * 256
BLOCK = 128
EPS = 1e-6
SB_BIAS_HI = 5.0
SB_BIAS_LO = 9.0
SPLIT_SIZES = (D_ATT, D_ATT, D_ATT, D_CONV, D_CONV, D_CONV, D_MODEL, D_MODEL)
N_IN = sum(SPLIT_SIZES)

kernel_name = 'hybrid_stickbreak_shortconv_step'


def rms_norm(x, g):
    xf = x.astype(jnp.float32)
    y = xf * lax.rsqrt(jnp.mean(xf * xf, axis=-1, keepdims=True) + EPS)
    return (y * g.astype(jnp.float32)).astype(x.dtype)


def split_projection(h, w_in):
    b, t, _ = h.shape
    p = jnp.einsum('btd,dn->btn', h, w_in)
    cuts = [int(c) for c in np.cumsum(SPLIT_SIZES)[:-1]]
    q, k, v, b_gate, c_gate, x_conv, g_att, g_conv = jnp.split(p, cuts, axis=-1)
    heads = lambda a: a.reshape(b, t, SB_HEADS, HEAD_DIM)
    return heads(q), heads(k), heads(v), b_gate, c_gate, x_conv, g_att, g_conv


def stick_breaking(q, k, v, bias, q_pos, k_lo):
    L = k.shape[1]
    z = jnp.einsum('bqhd,bkhd->bhqk', q.astype(jnp.float32), k.astype(jnp.float32)) * (HEAD_DIM ** -0.5)
    z = z + bias.astype(jnp.float32)[None, :, None, None]
    s_idx = jnp.arange(L)
    mask = (s_idx[None, :] < q_pos[:, None]) & (s_idx[None, :] >= k_lo)
    log_beta = jax.nn.log_sigmoid(z)
    log_1m = jnp.where(mask, jax.nn.log_sigmoid(-z), 0.0)
    suffix = lax.cumsum(log_1m, axis=3, reverse=True) - log_1m
    w = jnp.where(mask, jnp.exp(log_beta + suffix), 0.0)
    return jnp.einsum('bhqk,bkhd->bqhd', w, v.astype(jnp.float32))


def prompt_stick_breaking(q, k, v, bias):
    b, t, h, d = q.shape
    pad = (-N_META) % BLOCK
    pw = ((0, 0), (pad, 0), (0, 0), (0, 0))
    qp, kp, vp = jnp.pad(q, pw), jnp.pad(k, pw), jnp.pad(v, pw)
    L = t + pad
    n_blocks = L // BLOCK

    def one_block(i):
        qb = lax.dynamic_slice_in_dim(qp, i * BLOCK, BLOCK, axis=1)
        q_pos = i * BLOCK + jnp.arange(BLOCK)
        return stick_breaking(qb, kp, vp, bias, q_pos, pad)

    o = lax.map(one_block, jnp.arange(n_blocks))
    o = jnp.moveaxis(o, 0, 1).reshape(b, L, h, d)
    return o[:, pad:]


def short_conv(u_ext, w, t):
    return sum(w[i] * u_ext[:, i:i + t] for i in range(CONV_K))


def gated_merge(o_att, y_conv, g_att, g_conv, w_att_out, w_conv_out, w_o):
    b, t, _ = y_conv.shape
    dt = y_conv.dtype
    y_a = jnp.einsum('bte,ed->btd', o_att.reshape(b, t, D_ATT).astype(dt), w_att_out)
    y_b = jnp.einsum('bte,ed->btd', y_conv, w_conv_out)
    m = jax.nn.sigmoid(g_att) * y_a + jax.nn.sigmoid(g_conv) * y_b
    return jnp.einsum('btd,de->bte', m, w_o)


def swiglu(h, w_gate, w_up, w_down):
    a = jax.nn.silu(jnp.einsum('btd,df->btf', h, w_gate)) * jnp.einsum('btd,df->btf', h, w_up)
    return jnp.einsum('btf,fd->btd', a, w_down)


def setup_inputs(seed: int = 0) -> dict:
    key = jax.random.key(seed)
    ks = jax.random.split(key, 20)
    nrm = lambda k, shape, scale: jax.random.normal(k, shape, jnp.float32) * scale
    n_pages = PAST_LEN // PAGE_SIZE
    used = DEC_BATCH * n_pages
    n_phys = used + max(1, used // 4)
    page_table = jax.random.permutation(ks[5], n_phys)[:used].reshape(DEC_BATCH, n_pages).astype(jnp.int32)
    sb_bias = (-jnp.linspace(SB_BIAS_HI, SB_BIAS_LO, SB_HEADS, dtype=jnp.float32)[None, :]
               + nrm(ks[18], (DEPTH, SB_HEADS), 0.1))
    return {
        'x_prompt': nrm(ks[0], (BATCH, SEQ, D_MODEL), 1.0),
        'x_sample': nrm(ks[1], (DEC_BATCH, DEC_SEQ, D_MODEL), 1.0),
        'cache_k': nrm(ks[2], (DEPTH, n_phys, PAGE_SIZE, SB_HEADS, HEAD_DIM), 1.0),
        'cache_v': nrm(ks[3], (DEPTH, n_phys, PAGE_SIZE, SB_HEADS, HEAD_DIM), 1.0),
        'state_conv': nrm(ks[4], (DEPTH, DEC_BATCH, CONV_K - 1, D_CONV), 1.0),
        'page_table': page_table,
        'meta_tokens': nrm(ks[6], (N_META, D_MODEL), 1.0),
        'norm_mix': 1.0 + nrm(ks[7], (DEPTH, D_MODEL), 0.05),
        'w_in': nrm(ks[8], (DEPTH, D_MODEL, N_IN), D_MODEL ** -0.5),
        'sb_bias': sb_bias,
        'conv_w': nrm(ks[9], (DEPTH, CONV_K, D_CONV), CONV_K ** -0.5),
        'w_att_out': nrm(ks[10], (DEPTH, D_ATT, D_MODEL), D_ATT ** -0.5),
        'w_conv_out': nrm(ks[11], (DEPTH, D_CONV, D_MODEL), D_CONV ** -0.5),
        'w_o': nrm(ks[12], (DEPTH, D_MODEL, D_MODEL), D_MODEL ** -0.5),
        'norm_ffn': 1.0 + nrm(ks[13], (DEPTH, D_MODEL), 0.05),
        'w_gate': nrm(ks[14], (DEPTH, D_MODEL, D_FF), D_MODEL ** -0.5),
        'w_up': nrm(ks[15], (DEPTH, D_MODEL, D_FF), D_MODEL ** -0.5),
        'w_down': nrm(ks[16], (DEPTH, D_FF, D_MODEL), D_FF ** -0.5),
        'norm_final': 1.0 + nrm(ks[17], (D_MODEL,), 0.05),
    }


def reference(x_prompt, x_sample, cache_k, cache_v, state_conv, page_table, meta_tokens,
              norm_mix, w_in, sb_bias, conv_w, w_att_out, w_conv_out, w_o, norm_ffn,
              w_gate, w_up, w_down, norm_final):
    n_prompt = x_prompt.shape[0]
    n_dec, t_dec = x_sample.shape[0], x_sample.shape[1]
    meta = jnp.broadcast_to(meta_tokens[None].astype(x_prompt.dtype), (n_prompt, N_META, D_MODEL))
    xp = jnp.concatenate([meta, x_prompt], axis=1)
    xs = x_sample
    kp_l, vp_l, cp_l, ks_l, vs_l, cs_l = [], [], [], [], [], []
    for l in range(DEPTH):
        h = rms_norm(xp, norm_mix[l])
        q, k, v, bg, cg, xc, ga, gb = split_projection(h, w_in[l])
        o_att = prompt_stick_breaking(q, k, v, sb_bias[l])
        u = cg * xc
        u_ext = jnp.concatenate([jnp.zeros((n_prompt, CONV_K - 1, D_CONV), u.dtype), u], axis=1)
        y_conv = bg * short_conv(u_ext, conv_w[l], u.shape[1])
        xp = xp + gated_merge(o_att, y_conv, ga, gb, w_att_out[l], w_conv_out[l], w_o[l])
        xp = xp + swiglu(rms_norm(xp, norm_ffn[l]), w_gate[l], w_up[l], w_down[l])
        kp_l.append(k)
        vp_l.append(v)
        cp_l.append(u_ext[:, -(CONV_K - 1):])
        h = rms_norm(xs, norm_mix[l])
        q, k, v, bg, cg, xc, ga, gb = split_projection(h, w_in[l])
        k_past = cache_k[l][page_table].reshape(n_dec, -1, SB_HEADS, HEAD_DIM)
        v_past = cache_v[l][page_table].reshape(n_dec, -1, SB_HEADS, HEAD_DIM)
        past = k_past.shape[1]
        k_all = jnp.concatenate([k_past.astype(k.dtype), k], axis=1)
        v_all = jnp.concatenate([v_past.astype(v.dtype), v], axis=1)
        o_att = stick_breaking(q, k_all, v_all, sb_bias[l], past + jnp.arange(t_dec), 0)
        u = cg * xc
        u_ext = jnp.concatenate([state_conv[l].astype(u.dtype), u], axis=1)
        y_conv = bg * short_conv(u_ext, conv_w[l], t_dec)
        xs = xs + gated_merge(o_att, y_conv, ga, gb, w_att_out[l], w_conv_out[l], w_o[l])
        xs = xs + swiglu(rms_norm(xs, norm_ffn[l]), w_gate[l], w_up[l], w_down[l])
        ks_l.append(k)
        vs_l.append(v)
        cs_l.append(u_ext[:, -(CONV_K - 1):])
    y_prompt = rms_norm(xp, norm_final)[:, N_META:]
    y_sample = rms_norm(xs, norm_final)
    return (y_prompt, y_sample, jnp.stack(kp_l), jnp.stack(vp_l), jnp.stack(cp_l),
            jnp.stack(ks_l), jnp.stack(vs_l), jnp.stack(cs_l))
```

```python
import contextlib
import numpy as np
import concourse.bass as bass
import concourse.mybir as mybir
from concourse.bass_utils import run_bass_kernel_spmd

F32 = mybir.dt.float32
BF16 = mybir.dt.bfloat16
I32 = mybir.dt.int32
AF = mybir.ActivationFunctionType
ALU = mybir.AluOpType
AX = mybir.AxisListType

D = 1024
T = 2048
NM = 16
ND = 4
NT = T // 128
DFF = 2816
NF = DFF // 128
NPHYS = 2560
EPS = 1e-6
NCORES = 8
ARENA_F32 = 52480

DMA_K = {"sp": 12, "pool": 12, "act": 4}
ENGS = ["pe", "act", "dve", "pool", "sp"]


class Buf:
    __slots__ = ("writer", "readers", "excl")

    def __init__(self, excl=False):
        self.writer = None
        self.readers = []
        self.excl = excl


class Ins:
    __slots__ = ("eng", "fn", "deps", "needs_inc", "semval", "is_dma", "dsem", "dval", "didx")

    def __init__(self, eng, fn, is_dma):
        self.eng = eng
        self.fn = fn
        self.is_dma = is_dma
        self.deps = []
        self.needs_inc = False
        self.semval = 0
        self.dsem = None
        self.dval = 0
        self.didx = 0


class Sched:
    def __init__(self):
        self.ins = []
        self.last = {}
        self.recent_dma = {q: [] for q in DMA_K}
        self.pending = {}
        self.enabled = True

    def emit(self, eng, fn, reads=(), writes=(), dma=False):
        ins = Ins(eng, fn, dma)
        if not self.enabled:
            return ins
        deps = []
        for b in reads:
            if b.writer is not None:
                deps.append((b.writer, True))
            if b.excl:
                for r in b.readers:
                    if r.eng != eng:
                        deps.append((r, False))
        for b in writes:
            if b.writer is not None:
                deps.append((b.writer, False))
            for r in b.readers:
                deps.append((r, False))
        if eng in self.pending:
            deps.extend(self.pending.pop(eng))
        ins.deps = deps
        for b in reads:
            if not dma:
                b.readers = [r for r in b.readers if r.is_dma or r.eng != eng]
            b.readers.append(ins)
        for b in writes:
            b.writer = ins
            b.readers = []
        self.ins.append(ins)
        if dma:
            lst = self.recent_dma[eng]
            lst.append(ins)
            if len(lst) > DMA_K[eng]:
                lst.pop(0)
        else:
            self.last[eng] = ins
        return ins

    def barrier(self):
        deps = [(i, True) for i in self.last.values()]
        for q in DMA_K:
            deps.extend((i, True) for i in self.recent_dma[q])
        for e in ENGS:
            self.pending[e] = list(deps) + self.pending.get(e, [])

    @staticmethod
    def _need(ins, d, raw):
        if d is ins:
            return False
        if d.is_dma or ins.is_dma:
            return True
        if d.eng != ins.eng:
            return True
        if d.eng == "pe":
            return False
        return raw

    def finalize(self, sems, dsems):
        for ins in self.ins:
            for (d, raw) in ins.deps:
                if not d.is_dma and self._need(ins, d, raw):
                    d.needs_inc = True
        cnt = {}
        dcnt = {}
        for ins in self.ins:
            if ins.is_dma:
                i = dcnt.get(ins.eng, 0)
                K = DMA_K[ins.eng]
                ins.didx = i
                ins.dsem = dsems[ins.eng][i % K]
                ins.dval = 16 * (i // K + 1)
                dcnt[ins.eng] = i + 1
            elif ins.needs_inc:
                cnt[ins.eng] = cnt.get(ins.eng, 0) + 1
                ins.semval = cnt[ins.eng]
        per = {e: [] for e in ENGS}
        for ins in self.ins:
            per[ins.eng].append(ins)

        def replay(ename, handle):
            waited = {}
            for ins in per[ename]:
                waits = {}
                for (d, raw) in ins.deps:
                    if not self._need(ins, d, raw):
                        continue
                    if d.is_dma:
                        key = ("d", d.eng, d.didx % DMA_K[d.eng])
                        sem = d.dsem
                        val = d.dval
                    else:
                        key = ("e", d.eng)
                        sem = sems[d.eng]
                        val = d.semval
                    if key not in waits or waits[key][1] < val:
                        waits[key] = (sem, val)
                if ins.is_dma and ins.dval > 16:
                    key = ("d", ins.eng, ins.didx % DMA_K[ins.eng])
                    val = ins.dval - 16
                    if key not in waits or waits[key][1] < val:
                        waits[key] = (ins.dsem, val)
                for key, (sem, val) in waits.items():
                    if waited.get(key, 0) < val:
                        handle.wait_ge(sem, val)
                        waited[key] = val
                r = ins.fn(handle)
                if ins.is_dma:
                    r.then_inc(ins.dsem, 16)
                elif ins.needs_inc:
                    r.then_inc(sems[ename], 1)
            if ename in dcnt:
                K = DMA_K[ename]
                n = dcnt[ename]
                for k in range(min(K, n)):
                    uses = (n - 1 - k) // K + 1
                    handle.wait_ge(dsems[ename][k], 16 * uses)

        return replay


NWARM = (1, 1, 1)
ALL_PHASES = ("P1", "P2", "P3", "PD", "P4", "P5")


def build_program(phases=ALL_PHASES, cache_rows=NPHYS * 64):
    nc = bass.Bass("TRN2", target_bir_lowering=False)
    S = Sched()

    def din(name, shape, dtype=F32):
        return nc.dram_tensor(name, list(shape), dtype, kind="ExternalInput").ap()

    def dout(name, shape):
        return nc.dram_tensor(name, list(shape), F32, kind="ExternalOutput").ap()

    x_in = din("x", [T, D])
    xe_in = din("xe", [NM + ND, D])
    ck_in = din("cache_k", [cache_rows, 1024])
    cv_in = din("cache_v", [cache_rows, 1024])
    sc_in = din("state_conv", [ND, 2, 512])
    pt_in = din("pt", [128, 2], I32)
    w_in = din("w_in", [D, 5120])
    w_ao = din("w_att_out", [512, D])
    w_co = din("w_conv_out", [512, D])
    w_o = din("w_o", [D, D])
    w_g = din("w_gate", [D, DFF])
    w_u = din("w_up", [D, DFF])
    w_d = din("w_down", [DFF, D])
    nmix_in = din("nmix", [128, 8])
    nffn_in = din("nffn", [128, 8])
    nfin_in = din("nfin", [1, D])
    sbb_in = din("sbb", [1, 8])
    cw_in = din("cw", [128, 12])
    ident_in = din("ident", [128, 128])
    ntri_in = din("ntri", [128, 128])
    dmask_in = din("dmask", [128, 128])
    sel_in = din("sel", [NM + ND, 256])
    ustr_in = din("ustr", [128, 128])

    y_out = dout("y", [T, D])
    ys_out = dout("ys", [ND, D])
    kp_out = dout("kp", [NM + T, 512])
    vp_out = dout("vp", [NM + T, 512])
    cp_out = dout("cp", [2, 512])
    ks_out = dout("ks", [ND, 512])
    vs_out = dout("vs", [ND, 512])
    cs_out = dout("cs", [ND, 2, 512])
    x1s = nc.dram_tensor("x1s", [T + ND, D], F32, kind="Internal").ap()

    es = contextlib.ExitStack()
    with es:
        sems = {e: es.enter_context(nc.semaphore("s_" + e)) for e in ["pe", "act", "dve", "pool"]}
        dsems = {q: [es.enter_context(nc.semaphore("d_%s%d" % (q, i))) for i in range(k)]
                 for q, k in DMA_K.items()}
        arena = es.enter_context(nc.sbuf_tensor("arena", [128, ARENA_F32], F32))
        PB = [es.enter_context(nc.psum_tensor("pb%d" % i, [128, 512], F32))[:, :] for i in range(8)]
        PBb = [Buf(True) for _ in range(8)]
        PT = PB[7].bitcast(BF16).rearrange("p (a b) -> p a b", a=8)
        PTb = PBb[7]

        class Reg:
            def __init__(self, start, end):
                self.start = start
                self.end = end
                self.cur = start

            def take(self, shape, dtype=F32):
                esz = 4 if dtype in (F32, I32) else 2
                n = 1
                for s in shape[1:]:
                    n *= s
                nb = (n * esz + 63) // 64 * 64
                assert self.cur + nb <= self.end, ("arena overflow", shape, self.cur, nb, self.end)
                o4 = self.cur // 4
                v = arena[0:shape[0], o4:o4 + nb // 4]
                if dtype != F32:
                    v = v.bitcast(dtype)
                v = v[:, 0:n]
                if len(shape) == 3:
                    v = v.rearrange("p (a b) -> p a b", a=shape[1])
                elif len(shape) == 4:
                    v = v.rearrange("p (a b c) -> p a b c", a=shape[1], b=shape[2])
                self.cur += nb
                return v

            def sub(self, nbytes):
                r = Reg(self.cur, self.cur + nbytes)
                self.cur += nbytes
                assert self.cur <= self.end
                return r

        TOTAL = ARENA_F32 * 4
        top = Reg(0, TOTAL)
        CONST = top.sub(14 * 1024)
        GEN = top.sub(19 * 1024)
        SZ_HT = 8 * 2068 * 2
        SZ_YC = 4 * 2052 * 2
        tail_bytes = SZ_HT + 2 * SZ_YC + 192
        BIG = top.sub(TOTAL - top.cur - tail_bytes)
        TAIL = top.sub(tail_bytes)
        big_start, big_end = BIG.start, BIG.end

        def dma(q, out, in_, r=(), w=(), **kw):
            import os
            if os.environ.get("KDBG", "") == "noout" and getattr(out.tensor, "name", "") in ("kp", "vp", "ks", "vs"):
                return None
            return S.emit(q, lambda e: e.dma_start(out=out, in_=in_, **kw), reads=r, writes=w, dma=True)

        def mm(out, lhsT, rhs, start, stop, r=(), w=(), skip=False):
            return S.emit("pe", lambda e: e.matmul(out, lhsT, rhs, start=start, stop=stop, skip_group_check=skip),
                          reads=r, writes=w)

        def tr(out, in_, ident, r=(), w=()):
            return S.emit("pe", lambda e: e.transpose(out, in_, ident), reads=r, writes=w)

        def act(out, in_, func, r=(), w=(), **kw):
            return S.emit("act", lambda e: e.activation(out, in_, func, **kw), reads=r, writes=w)

        def tt(eng, out, in0, in1, op, r=(), w=()):
            return S.emit(eng, lambda e: e.tensor_tensor(out, in0, in1, op), reads=r, writes=w)

        def ts(eng, out, in0, s1, s2, op0, op1=None, r=(), w=()):
            if op1 is None:
                return S.emit(eng, lambda e: e.tensor_scalar(out, in0, s1, None, op0), reads=r, writes=w)
            return S.emit(eng, lambda e: e.tensor_scalar(out, in0, s1, s2, op0, op1), reads=r, writes=w)

        def stt(out, in0, scalar, in1, op0, op1, r=(), w=()):
            return S.emit("dve", lambda e: e.scalar_tensor_tensor(out, in0, scalar, in1, op0, op1), reads=r, writes=w)

        def cpy(eng, out, in_, r=(), w=()):
            if eng == "act":
                return S.emit("act", lambda e: e.copy(out, in_), reads=r, writes=w)
            return S.emit(eng, lambda e: e.tensor_copy(out, in_), reads=r, writes=w)

        def mset(eng, ap, val, w=()):
            return S.emit(eng, lambda e: e.memset(ap, val), writes=w)

        KB = Buf()
        ident_f = CONST.take([128, 128])
        ident_b = CONST.take([128, 128], BF16)
        ntri_b = CONST.take([128, 128], BF16)
        nones_b = CONST.take([128, 128], BF16)
        dmask_b = CONST.take([128, 128], BF16)
        zeros_b = CONST.take([128, 512], BF16)
        gmix = CONST.take([128, 8])
        gffn = CONST.take([128, 8])
        gfin = CONST.take([128, D])
        sbb = CONST.take([128, 8])
        cw = CONST.take([128, 12])
        sel_f = CONST.take([NM + ND, 256])
        ustr_f = CONST.take([128, 128])
        ptab = CONST.take([128, 2], I32)
        qtok = CONST.take([NM + ND, 512])
        scT = CONST.take([128, 4, ND, 2])
        udec = CONST.take([128, 4, ND])
        cbufs = []

        def cdma(q, out, in_, **kw):
            b = Buf()
            cbufs.append(b)
            dma(q, out, in_, w=[b], **kw)

        cdma("sp", ident_f, ident_in)
        cdma("pool", ident_b, ident_in)
        cdma("pool", ntri_b, ntri_in)
        cdma("pool", dmask_b, dmask_in)
        cdma("sp", gmix, nmix_in)
        cdma("sp", gffn, nffn_in)
        cdma("sp", gfin, nfin_in.broadcast_to([128, D]))
        cdma("sp", sbb, sbb_in.broadcast_to([128, 8]))
        cdma("sp", cw, cw_in)
        cdma("sp", sel_f, sel_in)
        cdma("sp", ustr_f, ustr_in)
        cdma("sp", ptab, pt_in)
        for n_ in range(4):
            cdma("sp", scT[:, n_].rearrange("p s r -> p (s r)"),
                 sc_in[:, :, n_ * 128:(n_ + 1) * 128].rearrange("s r p -> p (s r)"), allow_slow_non_contiguous=True)
        b = Buf()
        cbufs.append(b)
        mset("dve", nones_b, -1.0, w=[b])
        mset("dve", zeros_b, 0.0, w=[b])
        for e in ["pe", "act", "dve", "pool"]:
            S.pending[e] = [(bb.writer, True) for bb in cbufs]

        xt = [GEN.take([128, D]) for _ in range(2)]
        xtb = [Buf(), Buf()]
        xn = [GEN.take([128, D], BF16) for _ in range(2)]
        xnb = [Buf(), Buf()]
        junk = GEN.take([128, D], BF16)
        junkb = Buf()
        stat = [GEN.take([128, 4]) for _ in range(2)]
        statb = [Buf(), Buf()]
        stage = [GEN.take([128, 512]) for _ in range(2)]
        stageb = [Buf(), Buf()]
        rr = {"n": 0, "st": 0, "pb": 0}

        def bank():
            i = rr["pb"] % 6
            rr["pb"] += 1
            return PB[i], PBb[i]

        hT = TAIL.take([128, 8, 2068], BF16)
        ycT = TAIL.take([128, 4, 2052], BF16)
        oT = TAIL.take([128, 4, 2052], BF16)
        hTb = [Buf() for _ in range(5)]
        ycTb = [Buf() for _ in range(5)]
        oTb = [Buf() for _ in range(5)]

        def norm_transpose(src, srcb, rows, g, dst, dstb):
            i = rr["n"] % 2
            rr["n"] += 1
            st = stat[i]
            act(junk[0:rows, :], src, AF.Square, r=[srcb], w=[junkb, statb[i]], accum_out=st[0:rows, 0:1])
            act(st[0:rows, 1:2], st[0:rows, 0:1], AF.Sqrt, r=[statb[i]], w=[statb[i]], scale=1.0 / D, bias=EPS)
            S.emit("dve", lambda e: e.reciprocal(st[0:rows, 2:3], st[0:rows, 1:2]), reads=[statb[i]], writes=[statb[i]])
            ts("dve", xn[i][0:rows, :], src, st[0:rows, 2:3], None, ALU.mult, r=[srcb, statb[i]], w=[xnb[i]])
            for c in range(8):
                tr(PT[:, c, 0:rows], xn[i][0:rows, c * 128:(c + 1) * 128], ident_b[0:rows, 0:rows],
                   r=[xnb[i]], w=[PTb])
            tt("dve", dst, PT[:, :, 0:rows], g.unsqueeze(2).broadcast_to([128, 8, rows]), ALU.mult,
               r=[PTb], w=[dstb])
            return i

        S.enabled = "P1" in phases
        for i in range(NT + 1):
            k = i % 2
            if i < NT:
                rows = 128
                dma("sp", xt[k], x_in[i * 128:(i + 1) * 128, :], w=[xtb[k]])
                dst = hT[:, :, i * 128:(i + 1) * 128]
                db = hTb[i // 4]
            else:
                rows = NM + ND
                dma("sp", xt[k][0:rows, :], xe_in, w=[xtb[k]])
                dst = hT[:, :, 2048:2068]
                db = hTb[4]
            norm_transpose(xt[k][0:rows, :], xtb[k], rows, gmix, dst, db)

        S.enabled = "P2" in phases or "P2a" in phases
        P2 = Reg(big_start, big_end)
        qT = P2.take([128, 4, 2048], BF16)
        kT = P2.take([128, 4, 2064], BF16)
        v_sb = P2.take([128, 16, 512], BF16)
        vmeta = P2.take([NM, 512], BF16)
        qTb = [Buf() for _ in range(4)]
        kTb = [Buf() for _ in range(5)]
        vb = [Buf() for _ in range(17)]
        TK = 2
        NCH = 128 // TK
        kch = [P2.take([128, TK * 512]) for _ in range(3)]
        idxa = P2.take([128, 2, NCH], I32)
        qb_t = P2.take([128, 512])
        p3_start = P2.cur
        P3 = Reg(p3_start, big_end)
        run_p3 = "P3" in phases
        run_pd = "PD" in phases
        wsl = [P2.take([128, 8, 512], BF16) for _ in range(3)]
        wslb = [Buf() for _ in range(3)]
        ubuf = [P2.take([128, 2 + NM + T])] * 2
        ubufb = [Buf()] * 2
        cgt = [P2.take([128, 512]) for _ in range(2)]
        cgtb = [Buf(), Buf()]
        cvt = [P2.take([128, 512]) for _ in range(2)]
        cvtb = [Buf(), Buf()]
        w_in_v = w_in.rearrange("(c p) n -> p c n", p=128)

        def load_w(slot, g):
            dma("pool", wsl[slot], w_in_v[:, :, g * 512:(g + 1) * 512], w=[wslb[slot]])

        CH = [(c * 512, 512, c) for c in range(4)]

        def feat_mm(slot, nsub, col0, n, hb):
            pb, pbb = bank()
            for c in range(8):
                mm(pb[:, 0:n], wsl[slot][:, c, nsub * 128:(nsub + 1) * 128], hT[:, c, col0:col0 + n],
                   c == 0, c == 7, r=[wslb[slot], hb], w=[pbb])
            return pb, pbb

        def tok_mm(slot, col0, rows, hb):
            pb, pbb = bank()
            for c in range(8):
                mm(pb[0:rows, :], hT[:, c, col0:col0 + rows], wsl[slot][:, c, :], c == 0, c == 7,
                   r=[wslb[slot], hb], w=[pbb])
            return pb, pbb

        def stg():
            i = rr["st"] % 2
            rr["st"] += 1
            return stage[i], stageb[i]

        GA = Reg(GEN.start, GEN.end)
        z_t = GA.take([128, 8, 128])
        e2_t = GA.take([128, 8, 128])
        s2_t = GA.take([128, 8, 128])
        pf_t = GA.take([128, 8, 128])
        zb, e2b, s2b, pfb = Buf(), Buf(), Buf(), Buf()
        kchb = [Buf(), Buf(), Buf()]
        vch = [P3.take([128, TK * 512], BF16) for _ in range(3)]
        vchb = [Buf(), Buf(), Buf()]
        idxb = Buf()
        qbb = Buf()
        w2_t = P3.take([128, 8, 128])
        w2b = Buf()
        wz = P3.take([128, 128, 16], BF16)
        wzb = Buf()
        tot_t = P3.take([128, 16])
        totb = Buf()
        od_t = P3.take([16, 512])
        odb = Buf()
        odT = P3.take([128, 4, 16])
        odTb = Buf()
        ones_f = P3.take([128, 128])
        onesb = Buf()
        PD_BANK, PD_BANKb = PB[7], PBb[7]

        def gather(out, off, src, ob):
            return S.emit("pool", lambda e: e.indirect_dma_start(
                out=out, out_offset=None, in_=src,
                in_offset=bass.IndirectOffsetOnAxis(ap=off, axis=0)), reads=[idxb], writes=[ob], dma=True)

        def pd_gen():
            for pr in range(2):
                for ch in range(NCH):
                    ts("dve", idxa[:, pr, ch:ch + 1], ptab[:, pr:pr + 1], float(NCH), float(ch), ALU.mult, ALU.add, w=[idxb])
            yield "K"
            for pr in range(2):
                mm(PD_BANK, sel_f[:, pr * 128:(pr + 1) * 128], qtok, True, True, r=[qtokb], w=[PD_BANKb])
                cpy("act", qb_t, PD_BANK, r=[PD_BANKb], w=[qbb])
                def k_reduce(ch_):
                    kk = ch_ % 3
                    S.emit("dve", (lambda o, i_: (lambda e: e.tensor_reduce(o, i_, AX.X, ALU.add)))(
                        z_t[:, :, ch_ * TK:(ch_ + 1) * TK].rearrange("p h t -> p t h"),
                        kch[kk].rearrange("p (t h d) -> p t h d", t=TK, h=8)), reads=[kchb[kk]], writes=[zb])

                gather(kch[0], idxa[:, pr, 0:1], ck_in, kchb[0])
                gather(kch[1], idxa[:, pr, 1:2], ck_in, kchb[1])
                for ch in range(NCH):
                    k = ch % 3
                    if ch >= 1:
                        k_reduce(ch - 1)
                    k3 = kch[k].rearrange("p (t n) -> p t n", t=TK)
                    tt("pool", k3, k3, qb_t.unsqueeze(1).broadcast_to([128, TK, 512]), ALU.mult,
                       r=[kchb[k], qbb], w=[kchb[k]])
                    if ch + 2 < NCH:
                        gather(kch[(ch + 2) % 3], idxa[:, pr, ch + 2:ch + 3], ck_in, kchb[(ch + 2) % 3])
                    yield "K"
                k_reduce(NCH - 1)
                if pr == 0:
                    mset("dve", ones_f, 1.0, w=[onesb])
                    mset("dve", wz, 0.0, w=[wzb])
                stt(z_t, z_t, 0.125, sbb.unsqueeze(2).broadcast_to([128, 8, 128]), ALU.mult, ALU.add, r=[zb], w=[zb])
                act(e2_t, z_t, AF.Exp, r=[zb], w=[e2b])
                act(s2_t, e2_t, AF.Ln, r=[e2b], w=[s2b], bias=1.0)
                for h in range(8):
                    S.emit("dve", (lambda o, d0, d1: (lambda e: e.tensor_tensor_scan(o, d0, d1, 0.0, ALU.mult, ALU.add)))(
                        pf_t[:, h, :], ones_f, s2_t[:, h, :]), reads=[s2b, onesb], writes=[pfb])
                cpy("dve", tot_t[:, 0:8], pf_t[:, :, 127], r=[pfb], w=[totb])
                yield
                mm(PD_BANK[:, 0:8], ustr_f, tot_t[:, 0:8], True, True, r=[totb], w=[PD_BANKb])
                tt("dve", tot_t[:, 8:16], tot_t[:, 0:8], PD_BANK[:, 0:8], ALU.add, r=[totb, PD_BANKb], w=[totb])
                tt("dve", w2_t, pf_t, s2_t, ALU.subtract, r=[pfb, s2b], w=[w2b])
                tt("dve", w2_t, w2_t, z_t, ALU.add, r=[w2b, zb], w=[w2b])
                tt("dve", w2_t, w2_t, tot_t[:, 8:16].unsqueeze(2).broadcast_to([128, 8, 128]), ALU.subtract,
                   r=[w2b, totb], w=[w2b])
                act(e2_t, w2_t, AF.Exp, r=[w2b], w=[e2b])
                cpy("dve", wz[0:64, :, 0:8], e2_t[0:64].rearrange("p h t -> p t h"), r=[e2b], w=[wzb])
                cpy("dve", wz[64:128, :, 8:16], e2_t[64:128].rearrange("p h t -> p t h"), r=[e2b], w=[wzb])
                yield
                gather(vch[0], idxa[:, pr, 0:1], cv_in, vchb[0])
                gather(vch[1], idxa[:, pr, 1:2], cv_in, vchb[1])
                for ch in range(NCH):
                    k = ch % 3
                    for t in range(TK):
                        tok = ch * TK + t
                        mm(PD_BANK[0:16, :], wz[:, tok, :], vch[k][:, t * 512:(t + 1) * 512], tok == 0, tok == 127,
                           r=[wzb, vchb[k]], w=[PD_BANKb])
                    if ch + 2 < NCH:
                        gather(vch[(ch + 2) % 3], idxa[:, pr, ch + 2:ch + 3], cv_in, vchb[(ch + 2) % 3])
                    yield "V"
                cpy("act", od_t, PD_BANK[0:16, :], r=[PD_BANKb], w=[odb])
                for hq in range(4):
                    S.emit("pe", (lambda o, i_: (lambda e: e.transpose(o, i_, ident_f[0:16, 0:16])))(
                        PD_BANK[:, hq * 16:(hq + 1) * 16], od_t[:, hq * 128:(hq + 1) * 128]), reads=[odb], writes=[PD_BANKb])
                cpy("act", odT.rearrange("p a b -> p (a b)"), PD_BANK[:, 0:64], r=[PD_BANKb], w=[odTb])
                for h in range(8):
                    hq, hp = h // 2, h % 2
                    src = odT[hp * 64:(hp + 1) * 64, hq, h:h + 9:8]
                    cpy("dve", oT[hp * 64:(hp + 1) * 64, hq, 2048 + 2 * pr:2048 + 2 * pr + 2], src,
                        r=[odTb], w=[oTb[4]])
                yield

        pdg = pd_gen() if run_pd else iter(())

        pd_state = {"ph": "K", "n": 0, "items": 0}

        def pd_step(force=False, only_k=False):
            pd_state["n"] += 1
            if only_k and (pd_state["ph"] != "K" or pd_state["items"] >= NCH):
                return True
            if not force and not only_k and pd_state["ph"] == "K" and pd_state["n"] % 3 == 0:
                return True
            prev = S.enabled
            S.enabled = True
            try:
                pd_state["ph"] = next(pdg)
                pd_state["items"] += 1
            except StopIteration:
                return False
            finally:
                S.enabled = prev
            return True

        load_w(2, 0)
        load_w(0, 1)
        load_w(1, 2)
        import os
        DBG = os.environ.get("KDBG", "")
        if DBG == "loadonly":
            S.enabled = False
        for nsub in range(4):
            for (c0, n, ci) in CH:
                pb, pbb = feat_mm(2, nsub, c0, n, hTb[ci])
                S.emit("act", (lambda o, i_: (lambda e: e.mul(o, i_, 0.125)))(qT[:, nsub, c0:c0 + n], pb[:, 0:n]),
                       reads=[pbb], writes=[qTb[ci]])
        qtokb = Buf()
        pb, pbb = tok_mm(2, 2048, NM + ND, hTb[4])
        cpy("act", qtok, pb[0:NM + ND, :], r=[pbb], w=[qtokb])

        for nsub in range(4):
            for (c0, n, ci) in CH + [(2048, NM, 4)]:
                pb, pbb = feat_mm(0, nsub, c0, n, hTb[ci])
                cpy("act", kT[:, nsub, c0:c0 + n], pb[:, 0:n], r=[pbb], w=[kTb[ci]])
        if DBG == "featonly":
            S.enabled = False
        for i in range(NT + 1):
            if DBG == "noext" and i == NT:
                continue
            if DBG == "extonly" and i < NT:
                continue
            rows = 128 if i < NT else NM + ND
            c0 = i * 128 if i < NT else 2048
            hb = hTb[i // 4] if i < NT else hTb[4]
            pd_step(only_k=True)
            pb, pbb = tok_mm(0, c0, rows, hb)
            sg, sgb = stg()
            cpy("act", sg[0:rows, :], pb[0:rows, :], r=[pbb], w=[sgb])
            if i < NT:
                dma("sp", kp_out[NM + i * 128:NM + (i + 1) * 128, :], sg, r=[sgb])
            else:
                dma("sp", kp_out[0:NM, :], sg[0:NM, :], r=[sgb])
                dma("sp", ks_out, sg[NM:NM + ND, :], r=[sgb])
            pb, pbb = tok_mm(1, c0, rows, hb)
            sg, sgb = stg()
            cpy("act", sg[0:rows, :], pb[0:rows, :], r=[pbb], w=[sgb])
            if i < NT:
                cpy("dve", v_sb[:, i, :], pb, r=[pbb], w=[vb[i]])
                dma("sp", vp_out[NM + i * 128:NM + (i + 1) * 128, :], sg, r=[sgb])
            else:
                cpy("dve", vmeta, pb[0:NM, :], r=[pbb], w=[vb[16]])
                dma("sp", vp_out[0:NM, :], sg[0:NM, :], r=[sgb])
                dma("sp", vs_out, sg[NM:NM + ND, :], r=[sgb])
        S.enabled = "P2" in phases or "P2b" in phases
        load_w(0, 3)
        load_w(1, 4)
        load_w(2, 5)
        udecb = Buf()
        cpb = Buf()
        for nsub in range(4):
            ui = nsub % 2
            u = ubuf[ui]
            ub = ubufb[ui]
            mset("pool", u[:, 0:2], 0.0, w=[ub])
            w0 = cw[:, nsub * 3 + 0:nsub * 3 + 1]
            w1 = cw[:, nsub * 3 + 1:nsub * 3 + 2]
            w2 = cw[:, nsub * 3 + 2:nsub * 3 + 3]
            for (c0, n, ci) in [(2048, NM + ND, 4)] + CH:
                k = rr["n"] % 2
                rr["n"] += 1
                pd_step(only_k=True)
                pd_step(only_k=True)
                pcg, pcgb = feat_mm(1, nsub, c0, n, hTb[ci])
                pxc, pxcb = feat_mm(2, nsub, c0, n, hTb[ci])
                pbg, pbgb = feat_mm(0, nsub, c0, n, hTb[ci])
                cpy("act", cgt[k][:, 0:n], pcg[:, 0:n], r=[pcgb], w=[cgtb[k]])
                if ci == 4:
                    tt("dve", u[:, 2:2 + NM], pxc[:, 0:NM], cgt[k][:, 0:NM], ALU.mult, r=[pxcb, cgtb[k]], w=[ub])
                    tt("dve", udec[:, nsub, :], pxc[:, NM:NM + ND], cgt[k][:, NM:NM + ND], ALU.mult,
                       r=[pxcb, cgtb[k]], w=[udecb])
                    cv = cvt[k]
                    ts("dve", cv[:, 0:ND], scT[:, nsub, :, 0], w0, None, ALU.mult, r=[], w=[cvtb[k]])
                    stt(cv[:, 0:ND], scT[:, nsub, :, 1], w1, cv[:, 0:ND], ALU.mult, ALU.add, r=[cvtb[k]], w=[cvtb[k]])
                    stt(cv[:, 0:ND], udec[:, nsub, :], w2, cv[:, 0:ND], ALU.mult, ALU.add, r=[cvtb[k], udecb], w=[cvtb[k]])
                    tt("dve", ycT[:, nsub, 2048:2052], pbg[:, NM:NM + ND], cv[:, 0:ND], ALU.mult,
                       r=[pbgb, cvtb[k]], w=[ycTb[4]])
                else:
                    uo = 2 + NM + c0
                    tt("dve", u[:, uo:uo + n], pxc[:, 0:n], cgt[k][:, 0:n], ALU.mult, r=[pxcb, cgtb[k]], w=[ub])
                    cv = cvt[k]
                    ts("pool", cv[:, 0:n], u[:, uo - 2:uo - 2 + n], w0, None, ALU.mult, r=[ub], w=[cvtb[k]])
                    stt(cv[:, 0:n], u[:, uo - 1:uo - 1 + n], w1, cv[:, 0:n], ALU.mult, ALU.add, r=[ub, cvtb[k]], w=[cvtb[k]])
                    stt(cv[:, 0:n], u[:, uo:uo + n], w2, cv[:, 0:n], ALU.mult, ALU.add, r=[ub, cvtb[k]], w=[cvtb[k]])
                    tt("dve", ycT[:, nsub, c0:c0 + n], pbg[:, 0:n], cv[:, 0:n], ALU.mult, r=[pbgb, cvtb[k]], w=[ycTb[ci]])
            if "P2" in phases or "P2c" in phases:
              dma("sp", cp_out[:, nsub * 128:(nsub + 1) * 128].rearrange("r p -> p r"),
                u[:, 2 + NM + T - 2:2 + NM + T], r=[ub], w=[cpb], allow_slow_non_contiguous=True)
        S.enabled = "P2" in phases or "P2c" in phases
        dma("sp", cs_out[:, 0, :], sc_in[:, 1, :], w=[cpb])
        for n_ in range(4):
            dma("sp", cs_out[:, 1, n_ * 128:(n_ + 1) * 128].rearrange("s p -> p s"), udec[:, n_, :], r=[udecb], w=[cpb],
                allow_slow_non_contiguous=True)

        S.enabled = True
        S.barrier()
        e_t = [P3.take([128, 512]) for _ in range(2)]
        e_b = [Buf(), Buf()]
        sp_t = [P3.take([128, 512], BF16) for _ in range(3)]
        sp_b = [Buf() for _ in range(3)]
        tmp_t = [P3.take([128, 512]) for _ in range(2)]
        tmp_b = [Buf(), Buf()]
        w_t = [P3.take([128, 512], BF16) for _ in range(3)]
        w_b = [Buf() for _ in range(3)]
        R_t = [P3.take([128, 512]) for _ in range(3)]
        R_b = [Buf() for _ in range(3)]

        pairs = []
        for h in range(8):
            for c in range(4):
                blocks = list(range(4 * c + 3, -1, -1)) + [-1]
                for bi, i in enumerate(blocks):
                    pairs.append(dict(h=h, c=c, i=i, first=(bi == 0), last=(i == -1), g=h * 4 + c))
        NP = len(pairs)
        state = {"R": None, "ri": 0}

        def geom(p):
            h, c, i = p["h"], p["c"], p["i"]
            hq, hp = h // 2, h % 2
            if i >= 0:
                ns, kcol, kb = 128, i * 128, kTb[i // 4]
                vap, vbb = v_sb[:, i, h * 64:(h + 1) * 64], vb[i]
            else:
                ns, kcol, kb = NM, 2048, kTb[4]
                vap, vbb = vmeta[:, h * 64:(h + 1) * 64], vb[16]
            j = i - 4 * c
            t0 = j * 128 if j > 0 else 0
            return hq, hp * 64, ns, kcol, kb, vap, vbb, t0, (j >= 0)

        DUMb = Buf(True)

        def warm(n=1):
            for _ in range(n):
                mm(PB[4], zeros_b[:, 0:128], zeros_b[:, 0:512], True, True, w=[DUMb])

        def S1(k):
            p = pairs[k]
            h, c = p["h"], p["c"]
            hq, p0, ns, kcol, kb, vap, vbb, t0, diag = geom(p)
            ZB, ZBb = PB[k % 3], PBb[k % 3]
            Z = ZB[0:ns, t0:512]
            mm(Z, kT[p0:p0 + 64, hq, kcol:kcol + ns], qT[p0:p0 + 64, hq, c * 512 + t0:(c + 1) * 512],
               True, True, r=[kb, qTb[c]], w=[ZBb])
            warm(NWARM[0])
            e = e_t[k % 2][0:ns, t0:512]
            act(e, Z, AF.Exp, r=[ZBb], w=[e_b[k % 2]], bias=sbb[0:ns, h:h + 1])
            spv = sp_t[k % 3][0:ns, t0:512]
            act(spv, e, AF.Ln, r=[e_b[k % 2]], w=[sp_b[k % 3]], bias=1.0)
            if diag:
                tt("dve", sp_t[k % 3][:, t0:t0 + 128], sp_t[k % 3][:, t0:t0 + 128], dmask_b, ALU.mult,
                   r=[sp_b[k % 3]], w=[sp_b[k % 3]])

        def S2(k):
            p = pairs[k]
            h, c = p["h"], p["c"]
            hq, p0, ns, kcol, kb, vap, vbb, t0, diag = geom(p)
            ZB, ZBb = PB[k % 3], PBb[k % 3]
            g = p["g"]
            CB, CBb = PB[3], PBb[3]
            Z = ZB[0:ns, t0:512]
            spv = sp_t[k % 3][0:ns, t0:512]
            mm(Z, ntri_b[0:ns, 0:ns], spv, False, True, r=[sp_b[k % 3]], w=[ZBb], skip=True)
            if not p["last"]:
                mm(CB[:, t0:512], nones_b[0:ns, :], spv, p["first"], True, r=[sp_b[k % 3]], w=[CBb],
                   skip=not p["first"])
            warm(NWARM[1])
            wv = w_t[k % 3][0:ns, t0:512]
            Rcur = state["R"]
            if p["first"]:
                act(wv, Z, AF.Exp, r=[ZBb], w=[w_b[k % 3]], bias=sbb[0:ns, h:h + 1])
            else:
                tm = tmp_t[k % 2][0:ns, t0:512]
                tt("dve", tm, Z, Rcur[0][0:ns, t0:512], ALU.add, r=[ZBb, Rcur[1]], w=[tmp_b[k % 2]])
                act(wv, tm, AF.Exp, r=[tmp_b[k % 2]], w=[w_b[k % 3]], bias=sbb[0:ns, h:h + 1])
            if diag:
                tt("dve", w_t[k % 3][:, t0:t0 + 128], w_t[k % 3][:, t0:t0 + 128], dmask_b, ALU.mult,
                   r=[w_b[k % 3]], w=[w_b[k % 3]])
            if not p["last"]:
                ri = state["ri"]
                state["ri"] = (ri + 1) % 3
                Rn = (R_t[ri], R_b[ri])
                if t0 > 0:
                    mset("dve", Rn[0][:, 0:t0], 0.0, w=[Rn[1]])
                cpy("act" if k % 2 == 0 else "dve", Rn[0][:, t0:512], CB[:, t0:512], r=[CBb], w=[Rn[1]])
                state["R"] = Rn
            else:
                state["R"] = None

        def S3(k):
            p = pairs[k]
            h, c, g = p["h"], p["c"], p["g"]
            hq, p0, ns, kcol, kb, vap, vbb, t0, diag = geom(p)
            OB, OBb = PB[5 + g % 2], PBb[5 + g % 2]
            if p["first"]:
                mm(OB[p0:p0 + 64, :], zeros_b[:, 0:64], zeros_b[:, 0:512], True, False, w=[OBb])
            wv = w_t[k % 3][0:ns, t0:512]
            mm(OB[p0:p0 + 64, t0:512], vap, wv, False, p["last"], r=[vbb, w_b[k % 3]], w=[OBb])
            warm(NWARM[2])
            if p["last"]:
                cpy("act", oT[p0:p0 + 64, hq, c * 512:(c + 1) * 512], OB[p0:p0 + 64, :], r=[OBb], w=[oTb[c]])

        S.enabled = run_p3
        for s in range(NP + 2):
            if s < NP:
                S1(s)
            if 0 <= s - 1 < NP:
                S2(s - 1)
            if 0 <= s - 2 < NP:
                S3(s - 2)
            pd_step()
        while pd_step(True):
            pass

        S.enabled = True
        S.barrier()
        S.enabled = "P4" in phases
        P4 = Reg(big_start, big_end)
        wga = P4.take([128, 8, 1024], BF16)
        wgb = P4.take([128, 8, 1024], BF16)
        wao = P4.take([128, 4, 1024], BF16)
        wco = P4.take([128, 4, 1024], BF16)
        wo = P4.take([128, 8, 1024], BF16)
        w4b = [Buf() for _ in range(5)]
        dma("pool", wao, w_ao.rearrange("(c p) n -> p c n", p=128), w=[w4b[2]])
        dma("pool", wco, w_co.rearrange("(c p) n -> p c n", p=128), w=[w4b[3]])
        for hf in range(2):
            dma("pool", wga[:, :, hf * 512:(hf + 1) * 512], w_in_v[:, :, 3072 + hf * 512:3072 + (hf + 1) * 512], w=[w4b[0]])
            dma("pool", wgb[:, :, hf * 512:(hf + 1) * 512], w_in_v[:, :, 4096 + hf * 512:4096 + (hf + 1) * 512], w=[w4b[1]])
        w_o_v = w_o.rearrange("(c p) n -> p c n", p=128)
        for hf in range(2):
            dma("pool", wo[:, :, hf * 512:(hf + 1) * 512], w_o_v[:, :, hf * 512:(hf + 1) * 512], w=[w4b[4]])
        mT = [P4.take([128, 8, 512], BF16) for _ in range(2)]
        mTb = [Buf(), Buf()]
        sga = [P4.take([128, 512]) for _ in range(2)]
        sgab = [Buf(), Buf()]
        sgb_ = [P4.take([128, 512]) for _ in range(2)]
        sgbb = [Buf(), Buf()]
        t1 = [P4.take([128, 512]) for _ in range(2)]
        t1b = [Buf(), Buf()]
        x1t = [P4.take([128, D]) for _ in range(2)]
        x1tb = [Buf(), Buf()]
        x1sb = Buf()
        it = [0]
        for (hc0, oc0, n, ci) in [(c * 512, c * 512, 512, c) for c in range(4)] + [(2064, 2048, ND, 4)]:
            mi = ci % 2
            for dmc in range(8):
                k = it[0] % 2
                it[0] += 1
                dsl = slice(dmc * 128, (dmc + 1) * 128)
                A, Ab = bank()
                for c in range(4):
                    mm(A[:, 0:n], wao[:, c, dsl], oT[:, c, oc0:oc0 + n], c == 0, c == 3, r=[w4b[2], oTb[ci]], w=[Ab])
                Bk, Bb = bank()
                for c in range(4):
                    mm(Bk[:, 0:n], wco[:, c, dsl], ycT[:, c, oc0:oc0 + n], c == 0, c == 3, r=[w4b[3], ycTb[ci]], w=[Bb])
                G, Gb = bank()
                for c in range(8):
                    mm(G[:, 0:n], wga[:, c, dsl], hT[:, c, hc0:hc0 + n], c == 0, c == 7, r=[w4b[0], hTb[ci]], w=[Gb])
                H, Hb = bank()
                for c in range(8):
                    mm(H[:, 0:n], wgb[:, c, dsl], hT[:, c, hc0:hc0 + n], c == 0, c == 7, r=[w4b[1], hTb[ci]], w=[Hb])
                act(sga[k][:, 0:n], G[:, 0:n], AF.Sigmoid, r=[Gb], w=[sgab[k]])
                act(sgb_[k][:, 0:n], H[:, 0:n], AF.Sigmoid, r=[Hb], w=[sgbb[k]])
                tt("dve", t1[k][:, 0:n], A[:, 0:n], sga[k][:, 0:n], ALU.mult, r=[Ab, sgab[k]], w=[t1b[k]])
                tt("dve", sgb_[k][:, 0:n], Bk[:, 0:n], sgb_[k][:, 0:n], ALU.mult, r=[Bb, sgbb[k]], w=[sgbb[k]])
                tt("pool", mT[mi][:, dmc, 0:n], t1[k][:, 0:n], sgb_[k][:, 0:n], ALU.add, r=[t1b[k], sgbb[k]], w=[mTb[mi]])
            ntile = 4 if ci < 4 else 1
            for tl in range(ntile):
                rows = 128 if ci < 4 else ND
                k = it[0] % 2
                it[0] += 1
                if ci < 4:
                    row0 = ci * 512 + tl * 128
                    dma("sp", x1t[k], x_in[row0:row0 + 128, :], w=[x1tb[k]])
                else:
                    row0 = T
                    dma("sp", x1t[k][0:ND, :], xe_in[NM:NM + ND, :], w=[x1tb[k]])
                for hf in range(2):
                    Pk, Pb_ = bank()
                    for c in range(8):
                        mm(Pk[0:rows, :], mT[mi][:, c, tl * 128:tl * 128 + rows], wo[:, c, hf * 512:(hf + 1) * 512],
                           c == 0, c == 7, r=[mTb[mi], w4b[4]], w=[Pb_])
                    tt("dve", x1t[k][0:rows, hf * 512:(hf + 1) * 512], Pk[0:rows, :],
                       x1t[k][0:rows, hf * 512:(hf + 1) * 512], ALU.add, r=[Pb_, x1tb[k]], w=[x1tb[k]])
                dma("sp", x1s[row0:row0 + rows, :], x1t[k][0:rows, :], r=[x1tb[k]], w=[x1sb])

        S.enabled = True
        S.barrier()
        S.enabled = "P5" in phases
        P5 = Reg(big_start, TOTAL)
        wd = P5.take([128, NF, 1024], BF16)
        wdb = Buf()
        w_d_v = w_d.rearrange("(c p) n -> p c n", p=128)
        for hf in range(2):
            for fh in range(2):
                dma("pool", wd[:, fh * 11:(fh + 1) * 11, hf * 512:(hf + 1) * 512],
                    w_d_v[:, fh * 11:(fh + 1) * 11, hf * 512:(hf + 1) * 512], w=[wdb])
        HC = 1024 + ND
        aT = P5.take([128, NF, HC], BF16)
        aTb = [Buf() for _ in range(3)]
        h2 = P5.take([128, 8, HC], BF16)
        h2b = [Buf() for _ in range(3)]
        x1q = P5.take([128, 9, D])
        x1qb = [Buf() for _ in range(9)]
        wgs = [P5.take([128, 8, 256], BF16) for _ in range(2)]
        wgsb = [Buf(), Buf()]
        wus = [P5.take([128, 8, 256], BF16) for _ in range(2)]
        wusb = [Buf(), Buf()]
        sg_t = [P5.take([128, 512]) for _ in range(2)]
        sg_b = [Buf(), Buf()]
        yst = [P5.take([128, D]) for _ in range(2)]
        ystb = [Buf(), Buf()]
        w_g_v = w_g.rearrange("(c p) n -> p c n", p=128)
        w_u_v = w_u.rearrange("(c p) n -> p c n", p=128)
        wl = [0]
        for hh in range(2):
            ntl = 9 if hh == 1 else 8
            for tl in range(ntl):
                rows = 128 if tl < 8 else ND
                row0 = hh * 1024 + tl * 128 if tl < 8 else T
                dma("sp", x1q[0:rows, tl, :], x1s[row0:row0 + rows, :], r=[x1sb], w=[x1qb[tl]])
                norm_transpose(x1q[0:rows, tl, :], x1qb[tl], rows, gffn,
                               h2[:, :, tl * 128:tl * 128 + rows], h2b[tl // 4])
            cols = [(0, 512, 0), (512, 512, 1)] + ([(1024, ND, 2)] if hh == 1 else [])
            for fg in range(11):
                s = wl[0] % 2
                wl[0] += 1
                dma("pool", wgs[s], w_g_v[:, :, fg * 256:(fg + 1) * 256], w=[wgsb[s]])
                dma("pool", wus[s], w_u_v[:, :, fg * 256:(fg + 1) * 256], w=[wusb[s]])
                for fs in range(2):
                    f = fg * 2 + fs
                    for (c0, n, cb) in cols:
                        k = it[0] % 2
                        it[0] += 1
                        G, Gb = bank()
                        for c in range(8):
                            mm(G[:, 0:n], wgs[s][:, c, fs * 128:(fs + 1) * 128], h2[:, c, c0:c0 + n],
                               c == 0, c == 7, r=[wgsb[s], h2b[cb]], w=[Gb])
                        U, Ub = bank()
                        for c in range(8):
                            mm(U[:, 0:n], wus[s][:, c, fs * 128:(fs + 1) * 128], h2[:, c, c0:c0 + n],
                               c == 0, c == 7, r=[wusb[s], h2b[cb]], w=[Ub])
                        act(sg_t[k][:, 0:n], G[:, 0:n], AF.Silu, r=[Gb], w=[sg_b[k]])
                        tt("dve", aT[:, f, c0:c0 + n], U[:, 0:n], sg_t[k][:, 0:n], ALU.mult, r=[Ub, sg_b[k]], w=[aTb[cb]])
            for tl in range(ntl):
                rows = 128 if tl < 8 else ND
                row0 = hh * 1024 + tl * 128 if tl < 8 else T
                xq = x1q[0:rows, tl, :]
                xqb = x1qb[tl]
                for hf in range(2):
                    Pk, Pb_ = bank()
                    for f in range(NF):
                        mm(Pk[0:rows, :], aT[:, f, tl * 128:tl * 128 + rows], wd[:, f, hf * 512:(hf + 1) * 512],
                           f == 0, f == NF - 1, r=[aTb[tl // 4], wdb], w=[Pb_])
                    tt("dve", xq[:, hf * 512:(hf + 1) * 512], Pk[0:rows, :], xq[:, hf * 512:(hf + 1) * 512], ALU.add,
                       r=[Pb_, xqb], w=[xqb])
                i = rr["n"] % 2
                rr["n"] += 1
                st = stat[i]
                act(junk[0:rows, :], xq, AF.Square, r=[xqb], w=[junkb, statb[i]], accum_out=st[0:rows, 0:1])
                act(st[0:rows, 1:2], st[0:rows, 0:1], AF.Sqrt, r=[statb[i]], w=[statb[i]], scale=1.0 / D, bias=EPS)
                S.emit("dve", (lambda o, i_: (lambda e: e.reciprocal(o, i_)))(st[0:rows, 2:3], st[0:rows, 1:2]),
                       reads=[statb[i]], writes=[statb[i]])
                yk = it[0] % 2
                it[0] += 1
                stt(yst[yk][0:rows, :], xq, st[0:rows, 2:3], gfin[0:rows, :], ALU.mult, ALU.mult,
                    r=[xqb, statb[i]], w=[ystb[yk]])
                if tl < 8:
                    dma("sp", y_out[row0:row0 + 128, :], yst[yk], r=[ystb[yk]])
                else:
                    dma("sp", ys_out, yst[yk][0:ND, :], r=[ystb[yk]])

        S.enabled = True
        replay = S.finalize(sems, dsems)
        with nc.Block() as block:
            @block.tensor
            def _(e):
                replay("pe", e)

            @block.scalar
            def _(e):
                replay("act", e)

            @block.vector
            def _(e):
                replay("dve", e)

            @block.gpsimd
            def _(e):
                replay("pool", e)

            @block.sync
            def _(e):
                replay("sp", e)
    return nc


_CACHE = {}


def _consts():
    ident = np.eye(128, dtype=np.float32)
    j = np.arange(128)[:, None]
    s = np.arange(128)[None, :]
    ntri = np.where(j >= s, -1.0, 0.0).astype(np.float32)
    dmask = np.where(j < s, 1.0, 0.0).astype(np.float32)
    sel = np.zeros((NM + ND, 256), np.float32)
    for pr in range(2):
        for p in range(128):
            sel[NM + 2 * pr + p // 64, pr * 128 + p] = 1.0
    ustr = np.where((j > s) & ((j // 64) == (s // 64)), 1.0, 0.0).astype(np.float32)
    return ident, ntri, dmask, sel, ustr


def _prepare(x_prompt, x_sample, cache_k, cache_v, state_conv, page_table, meta_tokens,
             norm_mix, w_in, sb_bias, conv_w, w_att_out, w_conv_out, w_o, norm_ffn,
             w_gate, w_up, w_down, norm_final, cores=range(NCORES)):
    f = lambda a: np.ascontiguousarray(np.asarray(a, dtype=np.float32))
    x_prompt = f(x_prompt)
    x_sample = f(x_sample)
    ck = f(cache_k).reshape(NPHYS * 64, 1024)
    cv = f(cache_v).reshape(NPHYS * 64, 1024)
    state_conv = f(state_conv)
    page_table = np.asarray(page_table, dtype=np.int32)
    meta = f(meta_tokens)
    ident, ntri, dmask, sel, ustr = _consts()
    col = lambda v: np.ascontiguousarray(f(v).reshape(8, 128).T)
    shared = {
        "cache_k": ck, "cache_v": cv,
        "w_in": f(w_in)[0], "w_att_out": f(w_att_out)[0], "w_conv_out": f(w_conv_out)[0], "w_o": f(w_o)[0],
        "w_gate": f(w_gate)[0], "w_up": f(w_up)[0], "w_down": f(w_down)[0],
        "nmix": col(norm_mix[0]), "nffn": col(norm_ffn[0]), "nfin": f(norm_final).reshape(1, D),
        "sbb": f(sb_bias).reshape(1, 8),
        "cw": np.ascontiguousarray(f(conv_w)[0].reshape(3, 4, 128).transpose(2, 1, 0).reshape(128, 12)),
        "ident": ident, "ntri": ntri, "dmask": dmask, "sel": sel, "ustr": ustr,
    }
    in_maps = []
    for c in cores:
        pt = page_table[4 * c:4 * c + 4]
        ptc = np.ascontiguousarray(pt.reshape(2, 128).T).astype(np.int32)
        m = dict(shared)
        m["x"] = x_prompt[c]
        m["xe"] = np.ascontiguousarray(np.concatenate([meta, x_sample[4 * c:4 * c + 4, 0, :]], axis=0))
        m["state_conv"] = np.ascontiguousarray(state_conv[0, 4 * c:4 * c + 4])
        m["pt"] = ptc
        in_maps.append(m)
    return in_maps


def kernel(**inputs):
    in_maps = _prepare(**inputs)
    if "nc" not in _CACHE:
        _CACHE["nc"] = build_program()
    nc = _CACHE["nc"]
    res = run_bass_kernel_spmd(nc, in_maps, core_ids=list(range(NCORES)))
    R = res.results
    y = np.stack([R[c]["y"] for c in range(NCORES)], axis=0)
    ys = np.concatenate([R[c]["ys"] for c in range(NCORES)], axis=0).reshape(32, 1, D)
    kp = np.stack([R[c]["kp"] for c in range(NCORES)], axis=0).reshape(1, 8, NM + T, 8, 64)
    vp = np.stack([R[c]["vp"] for c in range(NCORES)], axis=0).reshape(1, 8, NM + T, 8, 64)
    cp = np.stack([R[c]["cp"] for c in range(NCORES)], axis=0).reshape(1, 8, 2, 512)
    ks = np.concatenate([R[c]["ks"] for c in range(NCORES)], axis=0).reshape(1, 32, 1, 8, 64)
    vs = np.concatenate([R[c]["vs"] for c in range(NCORES)], axis=0).reshape(1, 32, 1, 8, 64)
    cs = np.concatenate([R[c]["cs"] for c in range(NCORES)], axis=0).reshape(1, 32, 2, 512)
    return (y.astype(np.float32), ys.astype(np.float32), kp.astype(np.float32), vp.astype(np.float32),
            cp.astype(np.float32), ks.astype(np.float32), vs.astype(np.float32), cs.astype(np.float32))
```

```python
import contextlib
import numpy as np
import concourse.bass as bass
import concourse.mybir as mybir
from concourse.bass_utils import run_bass_kernel_spmd

F32 = mybir.dt.float32
BF16 = mybir.dt.bfloat16
I32 = mybir.dt.int32
AF = mybir.ActivationFunctionType
ALU = mybir.AluOpType
AX = mybir.AxisListType

D = 1024
T = 2048
NM = 16
ND = 4
NT = T // 128
DFF = 2816
NF = DFF // 128
NPHYS = 2560
EPS = 1e-6
NCORES = 8
ARENA_F32 = 52480

DMA_K = {"sp": 12, "pool": 12, "act": 4}
ENGS = ["pe", "act", "dve", "pool", "sp"]


class Buf:
    __slots__ = ("writer", "readers", "excl")

    def __init__(self, excl=False):
        self.writer = None
        self.readers = []
        self.excl = excl


class Ins:
    __slots__ = ("eng", "fn", "deps", "needs_inc", "semval", "is_dma", "dsem", "dval", "didx")

    def __init__(self, eng, fn, is_dma):
        self.eng = eng
        self.fn = fn
        self.is_dma = is_dma
        self.deps = []
        self.needs_inc = False
        self.semval = 0
        self.dsem = None
        self.dval = 0
        self.didx = 0


class Sched:
    def __init__(self):
        self.ins = []
        self.last = {}
        self.recent_dma = {q: [] for q in DMA_K}
        self.pending = {}
        self.enabled = True

    def emit(self, eng, fn, reads=(), writes=(), dma=False):
        ins = Ins(eng, fn, dma)
        if not self.enabled:
            return ins
        deps = []
        for b in reads:
            if b.writer is not None:
                deps.append((b.writer, True))
            if b.excl:
                for r in b.readers:
                    if r.eng != eng:
                        deps.append((r, False))
        for b in writes:
            if b.writer is not None:
                deps.append((b.writer, False))
            for r in b.readers:
                deps.append((r, False))
        if eng in self.pending:
            deps.extend(self.pending.pop(eng))
        ins.deps = deps
        for b in reads:
            if not dma:
                b.readers = [r for r in b.readers if r.is_dma or r.eng != eng]
            b.readers.append(ins)
        for b in writes:
            b.writer = ins
            b.readers = []
        self.ins.append(ins)
        if dma:
            lst = self.recent_dma[eng]
            lst.append(ins)
            if len(lst) > DMA_K[eng]:
                lst.pop(0)
        else:
            self.last[eng] = ins
        return ins

    def barrier(self):
        deps = [(i, True) for i in self.last.values()]
        for q in DMA_K:
            deps.extend((i, True) for i in self.recent_dma[q])
        for e in ENGS:
            self.pending[e] = list(deps) + self.pending.get(e, [])

    @staticmethod
    def _need(ins, d, raw):
        if d is ins:
            return False
        if d.is_dma or ins.is_dma:
            return True
        if d.eng != ins.eng:
            return True
        if d.eng == "pe":
            return False
        return raw

    def finalize(self, sems, dsems):
        for ins in self.ins:
            for (d, raw) in ins.deps:
                if not d.is_dma and self._need(ins, d, raw):
                    d.needs_inc = True
        cnt = {}
        dcnt = {}
        for ins in self.ins:
            if ins.is_dma:
                i = dcnt.get(ins.eng, 0)
                K = DMA_K[ins.eng]
                ins.didx = i
                ins.dsem = dsems[ins.eng][i % K]
                ins.dval = 16 * (i // K + 1)
                dcnt[ins.eng] = i + 1
            elif ins.needs_inc:
                cnt[ins.eng] = cnt.get(ins.eng, 0) + 1
                ins.semval = cnt[ins.eng]
        per = {e: [] for e in ENGS}
        for ins in self.ins:
            per[ins.eng].append(ins)

        def replay(ename, handle):
            waited = {}
            for ins in per[ename]:
                waits = {}
                for (d, raw) in ins.deps:
                    if not self._need(ins, d, raw):
                        continue
                    if d.is_dma:
                        key = ("d", d.eng, d.didx % DMA_K[d.eng])
                        sem = d.dsem
                        val = d.dval
                    else:
                        key = ("e", d.eng)
                        sem = sems[d.eng]
                        val = d.semval
                    if key not in waits or waits[key][1] < val:
                        waits[key] = (sem, val)
                if ins.is_dma and ins.dval > 16:
                    key = ("d", ins.eng, ins.didx % DMA_K[ins.eng])
                    val = ins.dval - 16
                    if key not in waits or waits[key][1] < val:
                        waits[key] = (ins.dsem, val)
                for key, (sem, val) in waits.items():
                    if waited.get(key, 0) < val:
                        handle.wait_ge(sem, val)
                        waited[key] = val
                r = ins.fn(handle)
                if ins.is_dma:
                    r.then_inc(ins.dsem, 16)
                elif ins.needs_inc:
                    r.then_inc(sems[ename], 1)
            if ename in dcnt:
                K = DMA_K[ename]
                n = dcnt[ename]
                for k in range(min(K, n)):
                    uses = (n - 1 - k) // K + 1
                    handle.wait_ge(dsems[ename][k], 16 * uses)

        return replay


ALL_PHASES = ("P1", "P2", "P3", "PD", "P4", "P5")


def build_program(phases=ALL_PHASES, cache_rows=NPHYS * 64):
    nc = bass.Bass("TRN2", target_bir_lowering=False)
    S = Sched()

    def din(name, shape, dtype=F32):
        return nc.dram_tensor(name, list(shape), dtype, kind="ExternalInput").ap()

    def dout(name, shape):
        return nc.dram_tensor(name, list(shape), F32, kind="ExternalOutput").ap()

    x_in = din("x", [T, D])
    xe_in = din("xe", [NM + ND, D])
    ck_in = din("cache_k", [cache_rows, 1024])
    cv_in = din("cache_v", [cache_rows, 1024])
    sc_in = din("state_conv", [ND, 2, 512])
    pt_in = din("pt", [128, 2], I32)
    w_in = din("w_in", [D, 5120])
    w_ao = din("w_att_out", [512, D])
    w_co = din("w_conv_out", [512, D])
    w_o = din("w_o", [D, D])
    w_g = din("w_gate", [D, DFF])
    w_u = din("w_up", [D, DFF])
    w_d = din("w_down", [DFF, D])
    nmix_in = din("nmix", [128, 8])
    nffn_in = din("nffn", [128, 8])
    nfin_in = din("nfin", [1, D])
    sbb_in = din("sbb", [1, 8])
    cw_in = din("cw", [128, 12])
    ident_in = din("ident", [128, 128])
    ntri_in = din("ntri", [128, 128])
    dmask_in = din("dmask", [128, 128])
    sel_in = din("sel", [NM + ND, 256])
    ustr_in = din("ustr", [128, 128])

    y_out = dout("y", [T, D])
    ys_out = dout("ys", [ND, D])
    kp_out = dout("kp", [NM + T, 512])
    vp_out = dout("vp", [NM + T, 512])
    cp_out = dout("cp", [2, 512])
    ks_out = dout("ks", [ND, 512])
    vs_out = dout("vs", [ND, 512])
    cs_out = dout("cs", [ND, 2, 512])
    x1s = nc.dram_tensor("x1s", [T + ND, D], F32, kind="Internal").ap()

    es = contextlib.ExitStack()
    with es:
        sems = {e: es.enter_context(nc.semaphore("s_" + e)) for e in ["pe", "act", "dve", "pool"]}
        dsems = {q: [es.enter_context(nc.semaphore("d_%s%d" % (q, i))) for i in range(k)]
                 for q, k in DMA_K.items()}
        arena = es.enter_context(nc.sbuf_tensor("arena", [128, ARENA_F32], F32))
        PB = [es.enter_context(nc.psum_tensor("pb%d" % i, [128, 512], F32))[:, :] for i in range(8)]
        PBb = [Buf(True) for _ in range(8)]
        PT = PB[7].bitcast(BF16).rearrange("p (a b) -> p a b", a=8)
        PTb = PBb[7]

        class Reg:
            def __init__(self, start, end):
                self.start = start
                self.end = end
                self.cur = start

            def take(self, shape, dtype=F32):
                esz = 4 if dtype in (F32, I32) else 2
                n = 1
                for s in shape[1:]:
                    n *= s
                nb = (n * esz + 63) // 64 * 64
                assert self.cur + nb <= self.end, ("arena overflow", shape, self.cur, nb, self.end)
                o4 = self.cur // 4
                v = arena[0:shape[0], o4:o4 + nb // 4]
                if dtype != F32:
                    v = v.bitcast(dtype)
                v = v[:, 0:n]
                if len(shape) == 3:
                    v = v.rearrange("p (a b) -> p a b", a=shape[1])
                elif len(shape) == 4:
                    v = v.rearrange("p (a b c) -> p a b c", a=shape[1], b=shape[2])
                self.cur += nb
                return v

            def sub(self, nbytes):
                r = Reg(self.cur, self.cur + nbytes)
                self.cur += nbytes
                assert self.cur <= self.end
                return r

        TOTAL = ARENA_F32 * 4
        top = Reg(0, TOTAL)
        CONST = top.sub(14 * 1024)
        GEN = top.sub(19 * 1024)
        SZ_HT = 8 * 2068 * 2
        SZ_YC = 4 * 2052 * 2
        tail_bytes = SZ_HT + 2 * SZ_YC + 192
        BIG = top.sub(TOTAL - top.cur - tail_bytes)
        TAIL = top.sub(tail_bytes)
        big_start, big_end = BIG.start, BIG.end

        def dma(q, out, in_, r=(), w=(), **kw):
            import os
            if os.environ.get("KDBG", "") == "noout" and getattr(out.tensor, "name", "") in ("kp", "vp", "ks", "vs"):
                return None
            return S.emit(q, lambda e: e.dma_start(out=out, in_=in_, **kw), reads=r, writes=w, dma=True)

        def mm(out, lhsT, rhs, start, stop, r=(), w=(), skip=False):
            return S.emit("pe", lambda e: e.matmul(out, lhsT, rhs, start=start, stop=stop, skip_group_check=skip),
                          reads=r, writes=w)

        def tr(out, in_, ident, r=(), w=()):
            return S.emit("pe", lambda e: e.transpose(out, in_, ident), reads=r, writes=w)

        def act(out, in_, func, r=(), w=(), **kw):
            return S.emit("act", lambda e: e.activation(out, in_, func, **kw), reads=r, writes=w)

        def tt(eng, out, in0, in1, op, r=(), w=()):
            return S.emit(eng, lambda e: e.tensor_tensor(out, in0, in1, op), reads=r, writes=w)

        def ts(eng, out, in0, s1, s2, op0, op1=None, r=(), w=()):
            if op1 is None:
                return S.emit(eng, lambda e: e.tensor_scalar(out, in0, s1, None, op0), reads=r, writes=w)
            return S.emit(eng, lambda e: e.tensor_scalar(out, in0, s1, s2, op0, op1), reads=r, writes=w)

        def stt(out, in0, scalar, in1, op0, op1, r=(), w=()):
            return S.emit("dve", lambda e: e.scalar_tensor_tensor(out, in0, scalar, in1, op0, op1), reads=r, writes=w)

        def cpy(eng, out, in_, r=(), w=()):
            if eng == "act":
                return S.emit("act", lambda e: e.copy(out, in_), reads=r, writes=w)
            return S.emit(eng, lambda e: e.tensor_copy(out, in_), reads=r, writes=w)

        def mset(eng, ap, val, w=()):
            return S.emit(eng, lambda e: e.memset(ap, val), writes=w)

        KB = Buf()
        ident_f = CONST.take([128, 128])
        ident_b = CONST.take([128, 128], BF16)
        ntri_b = CONST.take([128, 128], BF16)
        nones_b = CONST.take([128, 128], BF16)
        dmask_b = CONST.take([128, 128], BF16)
        zeros_b = CONST.take([128, 512], BF16)
        gmix = CONST.take([128, 8])
        gffn = CONST.take([128, 8])
        gfin = CONST.take([128, D])
        sbb = CONST.take([128, 8])
        cw = CONST.take([128, 12])
        sel_f = CONST.take([NM + ND, 256])
        ustr_f = CONST.take([128, 128])
        ptab = CONST.take([128, 2], I32)
        qtok = CONST.take([NM + ND, 512])
        scT = CONST.take([128, 4, ND, 2])
        udec = CONST.take([128, 4, ND])
        cbufs = []

        def cdma(q, out, in_, **kw):
            b = Buf()
            cbufs.append(b)
            dma(q, out, in_, w=[b], **kw)

        cdma("sp", ident_f, ident_in)
        cdma("pool", ident_b, ident_in)
        cdma("pool", ntri_b, ntri_in)
        cdma("pool", dmask_b, dmask_in)
        cdma("sp", gmix, nmix_in)
        cdma("sp", gffn, nffn_in)
        cdma("sp", gfin, nfin_in.broadcast_to([128, D]))
        cdma("sp", sbb, sbb_in.broadcast_to([128, 8]))
        cdma("sp", cw, cw_in)
        cdma("sp", sel_f, sel_in)
        cdma("sp", ustr_f, ustr_in)
        cdma("sp", ptab, pt_in)
        for n_ in range(4):
            cdma("sp", scT[:, n_].rearrange("p s r -> p (s r)"),
                 sc_in[:, :, n_ * 128:(n_ + 1) * 128].rearrange("s r p -> p (s r)"), allow_slow_non_contiguous=True)
        b = Buf()
        cbufs.append(b)
        mset("dve", nones_b, -1.0, w=[b])
        mset("dve", zeros_b, 0.0, w=[b])
        for e in ["pe", "act", "dve", "pool"]:
            S.pending[e] = [(bb.writer, True) for bb in cbufs]

        xt = [GEN.take([128, D]) for _ in range(2)]
        xtb = [Buf(), Buf()]
        xn = [GEN.take([128, D], BF16) for _ in range(2)]
        xnb = [Buf(), Buf()]
        junk = GEN.take([128, D], BF16)
        junkb = Buf()
        stat = [GEN.take([128, 4]) for _ in range(2)]
        statb = [Buf(), Buf()]
        stage = [GEN.take([128, 512]) for _ in range(2)]
        stageb = [Buf(), Buf()]
        rr = {"n": 0, "st": 0, "pb": 0}

        def bank():
            i = rr["pb"] % 6
            rr["pb"] += 1
            return PB[i], PBb[i]

        hT = TAIL.take([128, 8, 2068], BF16)
        ycT = TAIL.take([128, 4, 2052], BF16)
        oT = TAIL.take([128, 4, 2052], BF16)
        hTb = [Buf() for _ in range(5)]
        ycTb = [Buf() for _ in range(5)]
        oTb = [Buf() for _ in range(5)]

        def norm_transpose(src, srcb, rows, g, dst, dstb):
            i = rr["n"] % 2
            rr["n"] += 1
            st = stat[i]
            act(junk[0:rows, :], src, AF.Square, r=[srcb], w=[junkb, statb[i]], accum_out=st[0:rows, 0:1])
            act(st[0:rows, 1:2], st[0:rows, 0:1], AF.Sqrt, r=[statb[i]], w=[statb[i]], scale=1.0 / D, bias=EPS)
            S.emit("dve", lambda e: e.reciprocal(st[0:rows, 2:3], st[0:rows, 1:2]), reads=[statb[i]], writes=[statb[i]])
            ts("dve", xn[i][0:rows, :], src, st[0:rows, 2:3], None, ALU.mult, r=[srcb, statb[i]], w=[xnb[i]])
            for c in range(8):
                tr(PT[:, c, 0:rows], xn[i][0:rows, c * 128:(c + 1) * 128], ident_b[0:rows, 0:rows],
                   r=[xnb[i]], w=[PTb])
            tt("dve", dst, PT[:, :, 0:rows], g.unsqueeze(2).broadcast_to([128, 8, rows]), ALU.mult,
               r=[PTb], w=[dstb])
            return i

        S.enabled = "P1" in phases
        for i in range(NT + 1):
            k = i % 2
            if i < NT:
                rows = 128
                dma("sp", xt[k], x_in[i * 128:(i + 1) * 128, :], w=[xtb[k]])
                dst = hT[:, :, i * 128:(i + 1) * 128]
                db = hTb[i // 4]
            else:
                rows = NM + ND
                dma("sp", xt[k][0:rows, :], xe_in, w=[xtb[k]])
                dst = hT[:, :, 2048:2068]
                db = hTb[4]
            norm_transpose(xt[k][0:rows, :], xtb[k], rows, gmix, dst, db)

        S.enabled = "P2" in phases or "P2a" in phases
        P2 = Reg(big_start, big_end)
        qT = P2.take([128, 4, 2048], BF16)
        kT = P2.take([128, 4, 2064], BF16)
        v_sb = P2.take([128, 16, 512], BF16)
        vmeta = P2.take([NM, 512], BF16)
        qTb = [Buf() for _ in range(4)]
        kTb = [Buf() for _ in range(5)]
        vb = [Buf() for _ in range(17)]
        p3_start = P2.cur
        wsl = [P2.take([128, 8, 512], BF16) for _ in range(3)]
        wslb = [Buf() for _ in range(3)]
        ubuf = [P2.take([128, 2 + NM + T]) for _ in range(2)]
        ubufb = [Buf(), Buf()]
        cgt = [P2.take([128, 512]) for _ in range(2)]
        cgtb = [Buf(), Buf()]
        cvt = [P2.take([128, 512]) for _ in range(2)]
        cvtb = [Buf(), Buf()]
        w_in_v = w_in.rearrange("(c p) n -> p c n", p=128)

        def load_w(slot, g):
            dma("pool", wsl[slot], w_in_v[:, :, g * 512:(g + 1) * 512], w=[wslb[slot]])

        CH = [(c * 512, 512, c) for c in range(4)]

        def feat_mm(slot, nsub, col0, n, hb):
            pb, pbb = bank()
            for c in range(8):
                mm(pb[:, 0:n], wsl[slot][:, c, nsub * 128:(nsub + 1) * 128], hT[:, c, col0:col0 + n],
                   c == 0, c == 7, r=[wslb[slot], hb], w=[pbb])
            return pb, pbb

        def tok_mm(slot, col0, rows, hb):
            pb, pbb = bank()
            for c in range(8):
                mm(pb[0:rows, :], hT[:, c, col0:col0 + rows], wsl[slot][:, c, :], c == 0, c == 7,
                   r=[wslb[slot], hb], w=[pbb])
            return pb, pbb

        def stg():
            i = rr["st"] % 2
            rr["st"] += 1
            return stage[i], stageb[i]

        load_w(0, 1)
        load_w(1, 2)
        load_w(2, 0)
        import os
        DBG = os.environ.get("KDBG", "")
        if DBG == "loadonly":
            S.enabled = False
        for nsub in range(4):
            for (c0, n, ci) in CH + [(2048, NM, 4)]:
                pb, pbb = feat_mm(0, nsub, c0, n, hTb[ci])
                cpy("act", kT[:, nsub, c0:c0 + n], pb[:, 0:n], r=[pbb], w=[kTb[ci]])
        if DBG == "featonly":
            S.enabled = False
        for i in range(NT + 1):
            if DBG == "noext" and i == NT:
                continue
            if DBG == "extonly" and i < NT:
                continue
            rows = 128 if i < NT else NM + ND
            c0 = i * 128 if i < NT else 2048
            hb = hTb[i // 4] if i < NT else hTb[4]
            pb, pbb = tok_mm(0, c0, rows, hb)
            sg, sgb = stg()
            cpy("act", sg[0:rows, :], pb[0:rows, :], r=[pbb], w=[sgb])
            if i < NT:
                dma("sp", kp_out[NM + i * 128:NM + (i + 1) * 128, :], sg, r=[sgb])
            else:
                dma("sp", kp_out[0:NM, :], sg[0:NM, :], r=[sgb])
                dma("sp", ks_out, sg[NM:NM + ND, :], r=[sgb])
            pb, pbb = tok_mm(1, c0, rows, hb)
            sg, sgb = stg()
            cpy("act", sg[0:rows, :], pb[0:rows, :], r=[pbb], w=[sgb])
            if i < NT:
                cpy("dve", v_sb[:, i, :], pb, r=[pbb], w=[vb[i]])
                dma("sp", vp_out[NM + i * 128:NM + (i + 1) * 128, :], sg, r=[sgb])
            else:
                cpy("dve", vmeta, pb[0:NM, :], r=[pbb], w=[vb[16]])
                dma("sp", vp_out[0:NM, :], sg[0:NM, :], r=[sgb])
                dma("sp", vs_out, sg[NM:NM + ND, :], r=[sgb])
        if DBG in ("noq", "noext", "extonly", "noout"):
            S.enabled = False
        for nsub in range(4):
            for (c0, n, ci) in CH:
                pb, pbb = feat_mm(2, nsub, c0, n, hTb[ci])
                S.emit("act", (lambda o, i_: (lambda e: e.mul(o, i_, 0.125)))(qT[:, nsub, c0:c0 + n], pb[:, 0:n]),
                       reads=[pbb], writes=[qTb[ci]])
        qtokb = Buf()
        pb, pbb = tok_mm(2, 2048, NM + ND, hTb[4])
        cpy("act", qtok, pb[0:NM + ND, :], r=[pbb], w=[qtokb])

        S.enabled = "P2" in phases or "P2b" in phases
        load_w(0, 3)
        load_w(1, 4)
        load_w(2, 5)
        udecb = Buf()
        cpb = Buf()
        for nsub in range(4):
            ui = nsub % 2
            u = ubuf[ui]
            ub = ubufb[ui]
            mset("pool", u[:, 0:2], 0.0, w=[ub])
            w0 = cw[:, nsub * 3 + 0:nsub * 3 + 1]
            w1 = cw[:, nsub * 3 + 1:nsub * 3 + 2]
            w2 = cw[:, nsub * 3 + 2:nsub * 3 + 3]
            for (c0, n, ci) in [(2048, NM + ND, 4)] + CH:
                k = rr["n"] % 2
                rr["n"] += 1
                pcg, pcgb = feat_mm(1, nsub, c0, n, hTb[ci])
                pxc, pxcb = feat_mm(2, nsub, c0, n, hTb[ci])
                pbg, pbgb = feat_mm(0, nsub, c0, n, hTb[ci])
                cpy("act", cgt[k][:, 0:n], pcg[:, 0:n], r=[pcgb], w=[cgtb[k]])
                if ci == 4:
                    tt("dve", u[:, 2:2 + NM], pxc[:, 0:NM], cgt[k][:, 0:NM], ALU.mult, r=[pxcb, cgtb[k]], w=[ub])
                    tt("dve", udec[:, nsub, :], pxc[:, NM:NM + ND], cgt[k][:, NM:NM + ND], ALU.mult,
                       r=[pxcb, cgtb[k]], w=[udecb])
                    cv = cvt[k]
                    ts("dve", cv[:, 0:ND], scT[:, nsub, :, 0], w0, None, ALU.mult, r=[], w=[cvtb[k]])
                    stt(cv[:, 0:ND], scT[:, nsub, :, 1], w1, cv[:, 0:ND], ALU.mult, ALU.add, r=[cvtb[k]], w=[cvtb[k]])
                    stt(cv[:, 0:ND], udec[:, nsub, :], w2, cv[:, 0:ND], ALU.mult, ALU.add, r=[cvtb[k], udecb], w=[cvtb[k]])
                    tt("dve", ycT[:, nsub, 2048:2052], pbg[:, NM:NM + ND], cv[:, 0:ND], ALU.mult,
                       r=[pbgb, cvtb[k]], w=[ycTb[4]])
                else:
                    uo = 2 + NM + c0
                    tt("dve", u[:, uo:uo + n], pxc[:, 0:n], cgt[k][:, 0:n], ALU.mult, r=[pxcb, cgtb[k]], w=[ub])
                    cv = cvt[k]
                    ts("pool", cv[:, 0:n], u[:, uo - 2:uo - 2 + n], w0, None, ALU.mult, r=[ub], w=[cvtb[k]])
                    stt(cv[:, 0:n], u[:, uo - 1:uo - 1 + n], w1, cv[:, 0:n], ALU.mult, ALU.add, r=[ub, cvtb[k]], w=[cvtb[k]])
                    stt(cv[:, 0:n], u[:, uo:uo + n], w2, cv[:, 0:n], ALU.mult, ALU.add, r=[ub, cvtb[k]], w=[cvtb[k]])
                    tt("dve", ycT[:, nsub, c0:c0 + n], pbg[:, 0:n], cv[:, 0:n], ALU.mult, r=[pbgb, cvtb[k]], w=[ycTb[ci]])
            if "P2" in phases or "P2c" in phases:
              dma("sp", cp_out[:, nsub * 128:(nsub + 1) * 128].rearrange("r p -> p r"),
                u[:, 2 + NM + T - 2:2 + NM + T], r=[ub], w=[cpb], allow_slow_non_contiguous=True)
        S.enabled = "P2" in phases or "P2c" in phases
        dma("sp", cs_out[:, 0, :], sc_in[:, 1, :], w=[cpb])
        for n_ in range(4):
            dma("sp", cs_out[:, 1, n_ * 128:(n_ + 1) * 128].rearrange("s p -> p s"), udec[:, n_, :], r=[udecb], w=[cpb],
                allow_slow_non_contiguous=True)

        S.enabled = True
        S.barrier()
        run_p3 = "P3" in phases
        run_pd = "PD" in phases
        P3 = Reg(p3_start, big_end)
        e_t = [P3.take([128, 512]) for _ in range(2)]
        e_b = [Buf(), Buf()]
        sp_t = [P3.take([128, 512], BF16) for _ in range(3)]
        sp_b = [Buf() for _ in range(3)]
        tmp_t = [P3.take([128, 512]) for _ in range(2)]
        tmp_b = [Buf(), Buf()]
        w_t = [P3.take([128, 512], BF16) for _ in range(3)]
        w_b = [Buf() for _ in range(3)]
        R_t = [P3.take([128, 512]) for _ in range(3)]
        R_b = [Buf() for _ in range(3)]

        TK = 2
        NCH = 128 // TK
        GA = Reg(GEN.start, GEN.end)
        z_t = GA.take([128, 8, 128])
        e2_t = GA.take([128, 8, 128])
        s2_t = GA.take([128, 8, 128])
        pf_t = GA.take([128, 8, 128])
        zb, e2b, s2b, pfb = Buf(), Buf(), Buf(), Buf()
        kch = [P3.take([128, TK * 512]) for _ in range(3)]
        kchb = [Buf(), Buf(), Buf()]
        vch = [P3.take([128, TK * 512], BF16) for _ in range(3)]
        vchb = [Buf(), Buf(), Buf()]
        idxa = P3.take([128, 2, NCH], I32)
        idxb = Buf()
        qb_t = P3.take([128, 512])
        qbb = Buf()
        w2_t = P3.take([128, 8, 128])
        w2b = Buf()
        wz = P3.take([128, 128, 16], BF16)
        wzb = Buf()
        tot_t = P3.take([128, 16])
        totb = Buf()
        od_t = P3.take([16, 512])
        odb = Buf()
        odT = P3.take([128, 4, 16])
        odTb = Buf()
        ones_f = P3.take([128, 128])
        onesb = Buf()
        PD_BANK, PD_BANKb = PB[7], PBb[7]

        def gather(out, off, src, ob):
            return S.emit("pool", lambda e: e.indirect_dma_start(
                out=out, out_offset=None, in_=src,
                in_offset=bass.IndirectOffsetOnAxis(ap=off, axis=0)), reads=[idxb], writes=[ob], dma=True)

        def pd_gen():
            for pr in range(2):
                for ch in range(NCH):
                    ts("dve", idxa[:, pr, ch:ch + 1], ptab[:, pr:pr + 1], float(NCH), float(ch), ALU.mult, ALU.add, w=[idxb])
            mset("dve", ones_f, 1.0, w=[onesb])
            mset("pool", wz, 0.0, w=[wzb])
            yield
            for pr in range(2):
                mm(PD_BANK, sel_f[:, pr * 128:(pr + 1) * 128], qtok, True, True, r=[qtokb], w=[PD_BANKb])
                cpy("act", qb_t, PD_BANK, r=[PD_BANKb], w=[qbb])
                def k_reduce(ch_):
                    kk = ch_ % 3
                    S.emit("dve", (lambda o, i_: (lambda e: e.tensor_reduce(o, i_, AX.X, ALU.add)))(
                        z_t[:, :, ch_ * TK:(ch_ + 1) * TK].rearrange("p h t -> p t h"),
                        kch[kk].rearrange("p (t h d) -> p t h d", t=TK, h=8)), reads=[kchb[kk]], writes=[zb])

                gather(kch[0], idxa[:, pr, 0:1], ck_in, kchb[0])
                gather(kch[1], idxa[:, pr, 1:2], ck_in, kchb[1])
                for ch in range(NCH):
                    k = ch % 3
                    if ch >= 1:
                        k_reduce(ch - 1)
                    k3 = kch[k].rearrange("p (t n) -> p t n", t=TK)
                    tt("pool", k3, k3, qb_t.unsqueeze(1).broadcast_to([128, TK, 512]), ALU.mult,
                       r=[kchb[k], qbb], w=[kchb[k]])
                    if ch + 2 < NCH:
                        gather(kch[(ch + 2) % 3], idxa[:, pr, ch + 2:ch + 3], ck_in, kchb[(ch + 2) % 3])
                    yield "K"
                k_reduce(NCH - 1)
                stt(z_t, z_t, 0.125, sbb.unsqueeze(2).broadcast_to([128, 8, 128]), ALU.mult, ALU.add, r=[zb], w=[zb])
                act(e2_t, z_t, AF.Exp, r=[zb], w=[e2b])
                act(s2_t, e2_t, AF.Ln, r=[e2b], w=[s2b], bias=1.0)
                for h in range(8):
                    S.emit("dve", (lambda o, d0, d1: (lambda e: e.tensor_tensor_scan(o, d0, d1, 0.0, ALU.mult, ALU.add)))(
                        pf_t[:, h, :], ones_f, s2_t[:, h, :]), reads=[s2b, onesb], writes=[pfb])
                cpy("dve", tot_t[:, 0:8], pf_t[:, :, 127], r=[pfb], w=[totb])
                yield
                mm(PD_BANK[:, 0:8], ustr_f, tot_t[:, 0:8], True, True, r=[totb], w=[PD_BANKb])
                tt("dve", tot_t[:, 8:16], tot_t[:, 0:8], PD_BANK[:, 0:8], ALU.add, r=[totb, PD_BANKb], w=[totb])
                tt("dve", w2_t, pf_t, s2_t, ALU.subtract, r=[pfb, s2b], w=[w2b])
                tt("dve", w2_t, w2_t, z_t, ALU.add, r=[w2b, zb], w=[w2b])
                tt("dve", w2_t, w2_t, tot_t[:, 8:16].unsqueeze(2).broadcast_to([128, 8, 128]), ALU.subtract,
                   r=[w2b, totb], w=[w2b])
                act(e2_t, w2_t, AF.Exp, r=[w2b], w=[e2b])
                cpy("dve", wz[0:64, :, 0:8], e2_t[0:64].rearrange("p h t -> p t h"), r=[e2b], w=[wzb])
                cpy("dve", wz[64:128, :, 8:16], e2_t[64:128].rearrange("p h t -> p t h"), r=[e2b], w=[wzb])
                yield
                gather(vch[0], idxa[:, pr, 0:1], cv_in, vchb[0])
                gather(vch[1], idxa[:, pr, 1:2], cv_in, vchb[1])
                for ch in range(NCH):
                    k = ch % 3
                    for t in range(TK):
                        tok = ch * TK + t
                        mm(PD_BANK[0:16, :], wz[:, tok, :], vch[k][:, t * 512:(t + 1) * 512], tok == 0, tok == 127,
                           r=[wzb, vchb[k]], w=[PD_BANKb])
                    if ch + 2 < NCH:
                        gather(vch[(ch + 2) % 3], idxa[:, pr, ch + 2:ch + 3], cv_in, vchb[(ch + 2) % 3])
                    yield "V"
                cpy("act", od_t, PD_BANK[0:16, :], r=[PD_BANKb], w=[odb])
                for hq in range(4):
                    S.emit("pe", (lambda o, i_: (lambda e: e.transpose(o, i_, ident_f[0:16, 0:16])))(
                        PD_BANK[:, hq * 16:(hq + 1) * 16], od_t[:, hq * 128:(hq + 1) * 128]), reads=[odb], writes=[PD_BANKb])
                cpy("act", odT.rearrange("p a b -> p (a b)"), PD_BANK[:, 0:64], r=[PD_BANKb], w=[odTb])
                for h in range(8):
                    hq, hp = h // 2, h % 2
                    src = odT[hp * 64:(hp + 1) * 64, hq, h:h + 9:8]
                    cpy("dve", oT[hp * 64:(hp + 1) * 64, hq, 2048 + 2 * pr:2048 + 2 * pr + 2], src,
                        r=[odTb], w=[oTb[4]])
                yield

        pdg = pd_gen() if run_pd else iter(())

        pd_state = {"ph": "K", "n": 0}

        def pd_step(force=False):
            pd_state["n"] += 1
            if not force and pd_state["ph"] == "K" and pd_state["n"] % 3 == 0:
                return True
            S.enabled = True
            try:
                pd_state["ph"] = next(pdg)
            except StopIteration:
                return False
            finally:
                S.enabled = run_p3
            return True

        pairs = []
        for h in range(8):
            for c in range(4):
                blocks = list(range(4 * c + 3, -1, -1)) + [-1]
                for bi, i in enumerate(blocks):
                    pairs.append(dict(h=h, c=c, i=i, first=(bi == 0), last=(i == -1), g=h * 4 + c))
        NP = len(pairs)
        state = {"R": None, "ri": 0}

        def geom(p):
            h, c, i = p["h"], p["c"], p["i"]
            hq, hp = h // 2, h % 2
            if i >= 0:
                ns, kcol, kb = 128, i * 128, kTb[i // 4]
                vap, vbb = v_sb[:, i, h * 64:(h + 1) * 64], vb[i]
            else:
                ns, kcol, kb = NM, 2048, kTb[4]
                vap, vbb = vmeta[:, h * 64:(h + 1) * 64], vb[16]
            j = i - 4 * c
            t0 = j * 128 if j > 0 else 0
            return hq, hp * 64, ns, kcol, kb, vap, vbb, t0, (j >= 0)

        def S1(k):
            p = pairs[k]
            h, c = p["h"], p["c"]
            hq, p0, ns, kcol, kb, vap, vbb, t0, diag = geom(p)
            ZB, ZBb = PB[k % 3], PBb[k % 3]
            Z = ZB[0:ns, t0:512]
            mm(Z, kT[p0:p0 + 64, hq, kcol:kcol + ns], qT[p0:p0 + 64, hq, c * 512 + t0:(c + 1) * 512],
               True, True, r=[kb, qTb[c]], w=[ZBb])
            if diag:
                mm(ZB[:, t0:t0 + 128], ident_b, dmask_b, False, True, w=[ZBb], skip=True)
            e = e_t[k % 2][0:ns, t0:512]
            act(e, Z, AF.Exp, r=[ZBb], w=[e_b[k % 2]], bias=sbb[0:ns, h:h + 1])
            spv = sp_t[k % 3][0:ns, t0:512]
            act(spv, e, AF.Ln, r=[e_b[k % 2]], w=[sp_b[k % 3]], bias=1.0)

        def S2(k):
            p = pairs[k]
            h, c = p["h"], p["c"]
            hq, p0, ns, kcol, kb, vap, vbb, t0, diag = geom(p)
            ZB, ZBb = PB[k % 3], PBb[k % 3]
            g = p["g"]
            CB, CBb = PB[3 + g % 2], PBb[3 + g % 2]
            Z = ZB[0:ns, t0:512]
            spv = sp_t[k % 3][0:ns, t0:512]
            mm(Z, ntri_b[0:ns, 0:ns], spv, False, True, r=[sp_b[k % 3]], w=[ZBb], skip=True)
            if not p["last"]:
                mm(CB[:, t0:512], nones_b[0:ns, :], spv, p["first"], True, r=[sp_b[k % 3]], w=[CBb],
                   skip=not p["first"])
            wv = w_t[k % 3][0:ns, t0:512]
            Rcur = state["R"]
            if p["first"]:
                act(wv, Z, AF.Exp, r=[ZBb], w=[w_b[k % 3]], bias=sbb[0:ns, h:h + 1])
            else:
                tm = tmp_t[k % 2][0:ns, t0:512]
                tt("dve", tm, Z, Rcur[0][0:ns, t0:512], ALU.add, r=[ZBb, Rcur[1]], w=[tmp_b[k % 2]])
                act(wv, tm, AF.Exp, r=[tmp_b[k % 2]], w=[w_b[k % 3]], bias=sbb[0:ns, h:h + 1])
            if not p["last"]:
                ri = state["ri"]
                state["ri"] = (ri + 1) % 3
                Rn = (R_t[ri], R_b[ri])
                if t0 > 0:
                    mset("dve", Rn[0][:, 0:t0], 0.0, w=[Rn[1]])
                cpy("act" if k % 2 == 0 else "dve", Rn[0][:, t0:512], CB[:, t0:512], r=[CBb], w=[Rn[1]])
                state["R"] = Rn
            else:
                state["R"] = None

        def S3(k):
            p = pairs[k]
            h, c, g = p["h"], p["c"], p["g"]
            hq, p0, ns, kcol, kb, vap, vbb, t0, diag = geom(p)
            OB, OBb = PB[5 + g % 2], PBb[5 + g % 2]
            if p["first"]:
                mm(OB[p0:p0 + 64, :], zeros_b[:, 0:64], zeros_b[:, 0:512], True, False, w=[OBb])
            wv = w_t[k % 3][0:ns, t0:512]
            mm(OB[p0:p0 + 64, t0:512], vap, wv, False, p["last"], r=[vbb, w_b[k % 3]], w=[OBb])
            if p["last"]:
                cpy("act", oT[p0:p0 + 64, hq, c * 512:(c + 1) * 512], OB[p0:p0 + 64, :], r=[OBb], w=[oTb[c]])

        S.enabled = run_p3
        for s in range(NP + 2):
            if s < NP:
                S1(s)
            if 0 <= s - 1 < NP:
                S2(s - 1)
            if 0 <= s - 2 < NP:
                S3(s - 2)
            pd_step()
        while pd_step(True):
            pass

        S.enabled = True
        S.barrier()
        S.enabled = "P4" in phases
        P4 = Reg(big_start, big_end)
        wga = P4.take([128, 8, 1024], BF16)
        wgb = P4.take([128, 8, 1024], BF16)
        wao = P4.take([128, 4, 1024], BF16)
        wco = P4.take([128, 4, 1024], BF16)
        wo = P4.take([128, 8, 1024], BF16)
        w4b = [Buf() for _ in range(5)]
        dma("pool", wao, w_ao.rearrange("(c p) n -> p c n", p=128), w=[w4b[2]])
        dma("pool", wco, w_co.rearrange("(c p) n -> p c n", p=128), w=[w4b[3]])
        for hf in range(2):
            dma("pool", wga[:, :, hf * 512:(hf + 1) * 512], w_in_v[:, :, 3072 + hf * 512:3072 + (hf + 1) * 512], w=[w4b[0]])
            dma("pool", wgb[:, :, hf * 512:(hf + 1) * 512], w_in_v[:, :, 4096 + hf * 512:4096 + (hf + 1) * 512], w=[w4b[1]])
        w_o_v = w_o.rearrange("(c p) n -> p c n", p=128)
        for hf in range(2):
            dma("pool", wo[:, :, hf * 512:(hf + 1) * 512], w_o_v[:, :, hf * 512:(hf + 1) * 512], w=[w4b[4]])
        mT = [P4.take([128, 8, 512], BF16) for _ in range(2)]
        mTb = [Buf(), Buf()]
        sga = [P4.take([128, 512]) for _ in range(2)]
        sgab = [Buf(), Buf()]
        sgb_ = [P4.take([128, 512]) for _ in range(2)]
        sgbb = [Buf(), Buf()]
        t1 = [P4.take([128, 512]) for _ in range(2)]
        t1b = [Buf(), Buf()]
        x1t = [P4.take([128, D]) for _ in range(2)]
        x1tb = [Buf(), Buf()]
        x1sb = Buf()
        it = [0]
        for (hc0, oc0, n, ci) in [(c * 512, c * 512, 512, c) for c in range(4)] + [(2064, 2048, ND, 4)]:
            mi = ci % 2
            for dmc in range(8):
                k = it[0] % 2
                it[0] += 1
                dsl = slice(dmc * 128, (dmc + 1) * 128)
                A, Ab = bank()
                for c in range(4):
                    mm(A[:, 0:n], wao[:, c, dsl], oT[:, c, oc0:oc0 + n], c == 0, c == 3, r=[w4b[2], oTb[ci]], w=[Ab])
                Bk, Bb = bank()
                for c in range(4):
                    mm(Bk[:, 0:n], wco[:, c, dsl], ycT[:, c, oc0:oc0 + n], c == 0, c == 3, r=[w4b[3], ycTb[ci]], w=[Bb])
                G, Gb = bank()
                for c in range(8):
                    mm(G[:, 0:n], wga[:, c, dsl], hT[:, c, hc0:hc0 + n], c == 0, c == 7, r=[w4b[0], hTb[ci]], w=[Gb])
                H, Hb = bank()
                for c in range(8):
                    mm(H[:, 0:n], wgb[:, c, dsl], hT[:, c, hc0:hc0 + n], c == 0, c == 7, r=[w4b[1], hTb[ci]], w=[Hb])
                act(sga[k][:, 0:n], G[:, 0:n], AF.Sigmoid, r=[Gb], w=[sgab[k]])
                act(sgb_[k][:, 0:n], H[:, 0:n], AF.Sigmoid, r=[Hb], w=[sgbb[k]])
                tt("dve", t1[k][:, 0:n], A[:, 0:n], sga[k][:, 0:n], ALU.mult, r=[Ab, sgab[k]], w=[t1b[k]])
                tt("dve", sgb_[k][:, 0:n], Bk[:, 0:n], sgb_[k][:, 0:n], ALU.mult, r=[Bb, sgbb[k]], w=[sgbb[k]])
                tt("pool", mT[mi][:, dmc, 0:n], t1[k][:, 0:n], sgb_[k][:, 0:n], ALU.add, r=[t1b[k], sgbb[k]], w=[mTb[mi]])
            ntile = 4 if ci < 4 else 1
            for tl in range(ntile):
                rows = 128 if ci < 4 else ND
                k = it[0] % 2
                it[0] += 1
                if ci < 4:
                    row0 = ci * 512 + tl * 128
                    dma("sp", x1t[k], x_in[row0:row0 + 128, :], w=[x1tb[k]])
                else:
                    row0 = T
                    dma("sp", x1t[k][0:ND, :], xe_in[NM:NM + ND, :], w=[x1tb[k]])
                for hf in range(2):
                    Pk, Pb_ = bank()
                    for c in range(8):
                        mm(Pk[0:rows, :], mT[mi][:, c, tl * 128:tl * 128 + rows], wo[:, c, hf * 512:(hf + 1) * 512],
                           c == 0, c == 7, r=[mTb[mi], w4b[4]], w=[Pb_])
                    tt("dve", x1t[k][0:rows, hf * 512:(hf + 1) * 512], Pk[0:rows, :],
                       x1t[k][0:rows, hf * 512:(hf + 1) * 512], ALU.add, r=[Pb_, x1tb[k]], w=[x1tb[k]])
                dma("sp", x1s[row0:row0 + rows, :], x1t[k][0:rows, :], r=[x1tb[k]], w=[x1sb])

        S.enabled = True
        S.barrier()
        S.enabled = "P5" in phases
        P5 = Reg(big_start, TOTAL)
        wd = P5.take([128, NF, 1024], BF16)
        wdb = Buf()
        w_d_v = w_d.rearrange("(c p) n -> p c n", p=128)
        for hf in range(2):
            for fh in range(2):
                dma("pool", wd[:, fh * 11:(fh + 1) * 11, hf * 512:(hf + 1) * 512],
                    w_d_v[:, fh * 11:(fh + 1) * 11, hf * 512:(hf + 1) * 512], w=[wdb])
        HC = 1024 + ND
        aT = P5.take([128, NF, HC], BF16)
        aTb = [Buf() for _ in range(3)]
        h2 = P5.take([128, 8, HC], BF16)
        h2b = [Buf() for _ in range(3)]
        x1q = P5.take([128, 9, D])
        x1qb = [Buf() for _ in range(9)]
        wgs = [P5.take([128, 8, 256], BF16) for _ in range(2)]
        wgsb = [Buf(), Buf()]
        wus = [P5.take([128, 8, 256], BF16) for _ in range(2)]
        wusb = [Buf(), Buf()]
        sg_t = [P5.take([128, 512]) for _ in range(2)]
        sg_b = [Buf(), Buf()]
        yst = [P5.take([128, D]) for _ in range(2)]
        ystb = [Buf(), Buf()]
        w_g_v = w_g.rearrange("(c p) n -> p c n", p=128)
        w_u_v = w_u.rearrange("(c p) n -> p c n", p=128)
        wl = [0]
        for hh in range(2):
            ntl = 9 if hh == 1 else 8
            for tl in range(ntl):
                rows = 128 if tl < 8 else ND
                row0 = hh * 1024 + tl * 128 if tl < 8 else T
                dma("sp", x1q[0:rows, tl, :], x1s[row0:row0 + rows, :], r=[x1sb], w=[x1qb[tl]])
                norm_transpose(x1q[0:rows, tl, :], x1qb[tl], rows, gffn,
                               h2[:, :, tl * 128:tl * 128 + rows], h2b[tl // 4])
            cols = [(0, 512, 0), (512, 512, 1)] + ([(1024, ND, 2)] if hh == 1 else [])
            for fg in range(11):
                s = wl[0] % 2
                wl[0] += 1
                dma("pool", wgs[s], w_g_v[:, :, fg * 256:(fg + 1) * 256], w=[wgsb[s]])
                dma("pool", wus[s], w_u_v[:, :, fg * 256:(fg + 1) * 256], w=[wusb[s]])
                for fs in range(2):
                    f = fg * 2 + fs
                    for (c0, n, cb) in cols:
                        k = it[0] % 2
                        it[0] += 1
                        G, Gb = bank()
                        for c in range(8):
                            mm(G[:, 0:n], wgs[s][:, c, fs * 128:(fs + 1) * 128], h2[:, c, c0:c0 + n],
                               c == 0, c == 7, r=[wgsb[s], h2b[cb]], w=[Gb])
                        U, Ub = bank()
                        for c in range(8):
                            mm(U[:, 0:n], wus[s][:, c, fs * 128:(fs + 1) * 128], h2[:, c, c0:c0 + n],
                               c == 0, c == 7, r=[wusb[s], h2b[cb]], w=[Ub])
                        act(sg_t[k][:, 0:n], G[:, 0:n], AF.Silu, r=[Gb], w=[sg_b[k]])
                        tt("dve", aT[:, f, c0:c0 + n], U[:, 0:n], sg_t[k][:, 0:n], ALU.mult, r=[Ub, sg_b[k]], w=[aTb[cb]])
            for tl in range(ntl):
                rows = 128 if tl < 8 else ND
                row0 = hh * 1024 + tl * 128 if tl < 8 else T
                xq = x1q[0:rows, tl, :]
                xqb = x1qb[tl]
                for hf in range(2):
                    Pk, Pb_ = bank()
                    for f in range(NF):
                        mm(Pk[0:rows, :], aT[:, f, tl * 128:tl * 128 + rows], wd[:, f, hf * 512:(hf + 1) * 512],
                           f == 0, f == NF - 1, r=[aTb[tl // 4], wdb], w=[Pb_])
                    tt("dve", xq[:, hf * 512:(hf + 1) * 512], Pk[0:rows, :], xq[:, hf * 512:(hf + 1) * 512], ALU.add,
                       r=[Pb_, xqb], w=[xqb])
                i = rr["n"] % 2
                rr["n"] += 1
                st = stat[i]
                act(junk[0:rows, :], xq, AF.Square, r=[xqb], w=[junkb, statb[i]], accum_out=st[0:rows, 0:1])
                act(st[0:rows, 1:2], st[0:rows, 0:1], AF.Sqrt, r=[statb[i]], w=[statb[i]], scale=1.0 / D, bias=EPS)
                S.emit("dve", (lambda o, i_: (lambda e: e.reciprocal(o, i_)))(st[0:rows, 2:3], st[0:rows, 1:2]),
                       reads=[statb[i]], writes=[statb[i]])
                yk = it[0] % 2
                it[0] += 1
                stt(yst[yk][0:rows, :], xq, st[0:rows, 2:3], gfin[0:rows, :], ALU.mult, ALU.mult,
                    r=[xqb, statb[i]], w=[ystb[yk]])
                if tl < 8:
                    dma("sp", y_out[row0:row0 + 128, :], yst[yk], r=[ystb[yk]])
                else:
                    dma("sp", ys_out, yst[yk][0:ND, :], r=[ystb[yk]])

        S.enabled = True
        replay = S.finalize(sems, dsems)
        with nc.Block() as block:
            @block.tensor
            def _(e):
                replay("pe", e)

            @block.scalar
            def _(e):
                replay("act", e)

            @block.vector
            def _(e):
                replay("dve", e)

            @block.gpsimd
            def _(e):
                replay("pool", e)

            @block.sync
            def _(e):
                replay("sp", e)
    return nc


_CACHE = {}


def _consts():
    ident = np.eye(128, dtype=np.float32)
    j = np.arange(128)[:, None]
    s = np.arange(128)[None, :]
    ntri = np.where(j >= s, -1.0, 0.0).astype(np.float32)
    dmask = np.where(j < s, 0.0, -30000.0).astype(np.float32)
    sel = np.zeros((NM + ND, 256), np.float32)
    for pr in range(2):
        for p in range(128):
            sel[NM + 2 * pr + p // 64, pr * 128 + p] = 1.0
    ustr = np.where((j > s) & ((j // 64) == (s // 64)), 1.0, 0.0).astype(np.float32)
    return ident, ntri, dmask, sel, ustr


def _prepare(x_prompt, x_sample, cache_k, cache_v, state_conv, page_table, meta_tokens,
             norm_mix, w_in, sb_bias, conv_w, w_att_out, w_conv_out, w_o, norm_ffn,
             w_gate, w_up, w_down, norm_final, cores=range(NCORES)):
    f = lambda a: np.ascontiguousarray(np.asarray(a, dtype=np.float32))
    x_prompt = f(x_prompt)
    x_sample = f(x_sample)
    ck = f(cache_k).reshape(NPHYS * 64, 1024)
    cv = f(cache_v).reshape(NPHYS * 64, 1024)
    state_conv = f(state_conv)
    page_table = np.asarray(page_table, dtype=np.int32)
    meta = f(meta_tokens)
    ident, ntri, dmask, sel, ustr = _consts()
    col = lambda v: np.ascontiguousarray(f(v).reshape(8, 128).T)
    shared = {
        "cache_k": ck, "cache_v": cv,
        "w_in": f(w_in)[0], "w_att_out": f(w_att_out)[0], "w_conv_out": f(w_conv_out)[0], "w_o": f(w_o)[0],
        "w_gate": f(w_gate)[0], "w_up": f(w_up)[0], "w_down": f(w_down)[0],
        "nmix": col(norm_mix[0]), "nffn": col(norm_ffn[0]), "nfin": f(norm_final).reshape(1, D),
        "sbb": f(sb_bias).reshape(1, 8),
        "cw": np.ascontiguousarray(f(conv_w)[0].reshape(3, 4, 128).transpose(2, 1, 0).reshape(128, 12)),
        "ident": ident, "ntri": ntri, "dmask": dmask, "sel": sel, "ustr": ustr,
    }
    in_maps = []
    for c in cores:
        pt = page_table[4 * c:4 * c + 4]
        ptc = np.ascontiguousarray(pt.reshape(2, 128).T).astype(np.int32)
        m = dict(shared)
        m["x"] = x_prompt[c]
        m["xe"] = np.ascontiguousarray(np.concatenate([meta, x_sample[4 * c:4 * c + 4, 0, :]], axis=0))
        m["state_conv"] = np.ascontiguousarray(state_conv[0, 4 * c:4 * c + 4])
        m["pt"] = ptc
        in_maps.append(m)
    return in_maps


def kernel(**inputs):
    in_maps = _prepare(**inputs)
    if "nc" not in _CACHE:
        _CACHE["nc"] = build_program()
    nc = _CACHE["nc"]
    res = run_bass_kernel_spmd(nc, in_maps, core_ids=list(range(NCORES)))
    R = res.results
    y = np.stack([R[c]["y"] for c in range(NCORES)], axis=0)
    ys = np.concatenate([R[c]["ys"] for c in range(NCORES)], axis=0).reshape(32, 1, D)
    kp = np.stack([R[c]["kp"] for c in range(NCORES)], axis=0).reshape(1, 8, NM + T, 8, 64)
    vp = np.stack([R[c]["vp"] for c in range(NCORES)], axis=0).reshape(1, 8, NM + T, 8, 64)
    cp = np.stack([R[c]["cp"] for c in range(NCORES)], axis=0).reshape(1, 8, 2, 512)
    ks = np.concatenate([R[c]["ks"] for c in range(NCORES)], axis=0).reshape(1, 32, 1, 8, 64)
    vs = np.concatenate([R[c]["vs"] for c in range(NCORES)], axis=0).reshape(1, 32, 1, 8, 64)
    cs = np.concatenate([R[c]["cs"] for c in range(NCORES)], axis=0).reshape(1, 32, 2, 512)
    return (y.astype(np.float32), ys.astype(np.float32), kp.astype(np.float32), vp.astype(np.float32),
            cp.astype(np.float32), ks.astype(np.float32), vs.astype(np.float32), cs.astype(np.float32))
```

```python
import contextlib
import numpy as np
import concourse.bass as bass
import concourse.mybir as mybir
from concourse.bass_utils import run_bass_kernel_spmd

F32 = mybir.dt.float32
BF16 = mybir.dt.bfloat16
I32 = mybir.dt.int32
AF = mybir.ActivationFunctionType
ALU = mybir.AluOpType
AX = mybir.AxisListType

D = 1024
T = 2048
NM = 16
ND = 4
NT = T // 128
DFF = 2816
NF = DFF // 128
NPHYS = 2560
EPS = 1e-6
NCORES = 8
ARENA_F32 = 52480

DMA_K = {"sp": 12, "pool": 12, "act": 4}
ENGS = ["pe", "act", "dve", "pool", "sp"]


class Buf:
    __slots__ = ("writer", "readers", "excl")

    def __init__(self, excl=False):
        self.writer = None
        self.readers = []
        self.excl = excl


class Ins:
    __slots__ = ("eng", "fn", "deps", "needs_inc", "semval", "is_dma", "dsem", "dval", "didx")

    def __init__(self, eng, fn, is_dma):
        self.eng = eng
        self.fn = fn
        self.is_dma = is_dma
        self.deps = []
        self.needs_inc = False
        self.semval = 0
        self.dsem = None
        self.dval = 0
        self.didx = 0


class Sched:
    def __init__(self):
        self.ins = []
        self.last = {}
        self.recent_dma = {q: [] for q in DMA_K}
        self.pending = {}
        self.enabled = True

    def emit(self, eng, fn, reads=(), writes=(), dma=False):
        ins = Ins(eng, fn, dma)
        if not self.enabled:
            return ins
        deps = []
        for b in reads:
            if b.writer is not None:
                deps.append((b.writer, True))
            if b.excl:
                for r in b.readers:
                    if r.eng != eng:
                        deps.append((r, False))
        for b in writes:
            if b.writer is not None:
                deps.append((b.writer, False))
            for r in b.readers:
                deps.append((r, False))
        if eng in self.pending:
            deps.extend(self.pending.pop(eng))
        ins.deps = deps
        for b in reads:
            if not dma:
                b.readers = [r for r in b.readers if r.is_dma or r.eng != eng]
            b.readers.append(ins)
        for b in writes:
            b.writer = ins
            b.readers = []
        self.ins.append(ins)
        if dma:
            lst = self.recent_dma[eng]
            lst.append(ins)
            if len(lst) > DMA_K[eng]:
                lst.pop(0)
        else:
            self.last[eng] = ins
        return ins

    def barrier(self):
        deps = [(i, True) for i in self.last.values()]
        for q in DMA_K:
            deps.extend((i, True) for i in self.recent_dma[q])
        for e in ENGS:
            self.pending[e] = list(deps) + self.pending.get(e, [])

    @staticmethod
    def _need(ins, d, raw):
        if d is ins:
            return False
        if d.is_dma or ins.is_dma:
            return True
        if d.eng != ins.eng:
            return True
        if d.eng == "pe":
            return False
        return raw

    def finalize(self, sems, dsems):
        for ins in self.ins:
            for (d, raw) in ins.deps:
                if not d.is_dma and self._need(ins, d, raw):
                    d.needs_inc = True
        cnt = {}
        dcnt = {}
        for ins in self.ins:
            if ins.is_dma:
                i = dcnt.get(ins.eng, 0)
                K = DMA_K[ins.eng]
                ins.didx = i
                ins.dsem = dsems[ins.eng][i % K]
                ins.dval = 16 * (i // K + 1)
                dcnt[ins.eng] = i + 1
            elif ins.needs_inc:
                cnt[ins.eng] = cnt.get(ins.eng, 0) + 1
                ins.semval = cnt[ins.eng]
        per = {e: [] for e in ENGS}
        for ins in self.ins:
            per[ins.eng].append(ins)

        def replay(ename, handle):
            waited = {}
            for ins in per[ename]:
                waits = {}
                for (d, raw) in ins.deps:
                    if not self._need(ins, d, raw):
                        continue
                    if d.is_dma:
                        key = ("d", d.eng, d.didx % DMA_K[d.eng])
                        sem = d.dsem
                        val = d.dval
                    else:
                        key = ("e", d.eng)
                        sem = sems[d.eng]
                        val = d.semval
                    if key not in waits or waits[key][1] < val:
                        waits[key] = (sem, val)
                if ins.is_dma and ins.dval > 16:
                    key = ("d", ins.eng, ins.didx % DMA_K[ins.eng])
                    val = ins.dval - 16
                    if key not in waits or waits[key][1] < val:
                        waits[key] = (ins.dsem, val)
                for key, (sem, val) in waits.items():
                    if waited.get(key, 0) < val:
                        handle.wait_ge(sem, val)
                        waited[key] = val
                r = ins.fn(handle)
                if ins.is_dma:
                    r.then_inc(ins.dsem, 16)
                elif ins.needs_inc:
                    r.then_inc(sems[ename], 1)
            if ename in dcnt:
                K = DMA_K[ename]
                n = dcnt[ename]
                for k in range(min(K, n)):
                    uses = (n - 1 - k) // K + 1
                    handle.wait_ge(dsems[ename][k], 16 * uses)

        return replay


ALL_PHASES = ("P1", "P2", "P3", "PD", "P4", "P5")


def build_program(phases=ALL_PHASES, cache_rows=NPHYS * 64):
    nc = bass.Bass("TRN2", target_bir_lowering=False)
    S = Sched()

    def din(name, shape, dtype=F32):
        return nc.dram_tensor(name, list(shape), dtype, kind="ExternalInput").ap()

    def dout(name, shape):
        return nc.dram_tensor(name, list(shape), F32, kind="ExternalOutput").ap()

    x_in = din("x", [T, D])
    xe_in = din("xe", [NM + ND, D])
    ck_in = din("cache_k", [cache_rows, 1024])
    cv_in = din("cache_v", [cache_rows, 1024])
    sc_in = din("state_conv", [ND, 2, 512])
    pt_in = din("pt", [128, 2], I32)
    w_in = din("w_in", [D, 5120])
    w_ao = din("w_att_out", [512, D])
    w_co = din("w_conv_out", [512, D])
    w_o = din("w_o", [D, D])
    w_g = din("w_gate", [D, DFF])
    w_u = din("w_up", [D, DFF])
    w_d = din("w_down", [DFF, D])
    nmix_in = din("nmix", [128, 8])
    nffn_in = din("nffn", [128, 8])
    nfin_in = din("nfin", [1, D])
    sbb_in = din("sbb", [1, 8])
    cw_in = din("cw", [128, 12])
    ident_in = din("ident", [128, 128])
    ntri_in = din("ntri", [128, 128])
    dmask_in = din("dmask", [128, 128])
    sel_in = din("sel", [NM + ND, 256])
    ustr_in = din("ustr", [128, 128])

    y_out = dout("y", [T, D])
    ys_out = dout("ys", [ND, D])
    kp_out = dout("kp", [NM + T, 512])
    vp_out = dout("vp", [NM + T, 512])
    cp_out = dout("cp", [2, 512])
    ks_out = dout("ks", [ND, 512])
    vs_out = dout("vs", [ND, 512])
    cs_out = dout("cs", [ND, 2, 512])
    x1s = nc.dram_tensor("x1s", [T + ND, D], F32, kind="Internal").ap()

    es = contextlib.ExitStack()
    with es:
        sems = {e: es.enter_context(nc.semaphore("s_" + e)) for e in ["pe", "act", "dve", "pool"]}
        dsems = {q: [es.enter_context(nc.semaphore("d_%s%d" % (q, i))) for i in range(k)]
                 for q, k in DMA_K.items()}
        arena = es.enter_context(nc.sbuf_tensor("arena", [128, ARENA_F32], F32))
        PB = [es.enter_context(nc.psum_tensor("pb%d" % i, [128, 512], F32))[:, :] for i in range(8)]
        PBb = [Buf(True) for _ in range(8)]
        PT = PB[7].bitcast(BF16).rearrange("p (a b) -> p a b", a=8)
        PTb = PBb[7]

        class Reg:
            def __init__(self, start, end):
                self.start = start
                self.end = end
                self.cur = start

            def take(self, shape, dtype=F32):
                esz = 4 if dtype in (F32, I32) else 2
                n = 1
                for s in shape[1:]:
                    n *= s
                nb = (n * esz + 63) // 64 * 64
                assert self.cur + nb <= self.end, ("arena overflow", shape, self.cur, nb, self.end)
                o4 = self.cur // 4
                v = arena[0:shape[0], o4:o4 + nb // 4]
                if dtype != F32:
                    v = v.bitcast(dtype)
                v = v[:, 0:n]
                if len(shape) == 3:
                    v = v.rearrange("p (a b) -> p a b", a=shape[1])
                elif len(shape) == 4:
                    v = v.rearrange("p (a b c) -> p a b c", a=shape[1], b=shape[2])
                self.cur += nb
                return v

            def sub(self, nbytes):
                r = Reg(self.cur, self.cur + nbytes)
                self.cur += nbytes
                assert self.cur <= self.end
                return r

        TOTAL = ARENA_F32 * 4
        top = Reg(0, TOTAL)
        CONST = top.sub(14 * 1024)
        GEN = top.sub(19 * 1024)
        SZ_HT = 8 * 2068 * 2
        SZ_YC = 4 * 2052 * 2
        tail_bytes = SZ_HT + 2 * SZ_YC + 192
        BIG = top.sub(TOTAL - top.cur - tail_bytes)
        TAIL = top.sub(tail_bytes)
        big_start, big_end = BIG.start, BIG.end

        def dma(q, out, in_, r=(), w=(), **kw):
            import os
            if os.environ.get("KDBG", "") == "noout" and getattr(out.tensor, "name", "") in ("kp", "vp", "ks", "vs"):
                return None
            return S.emit(q, lambda e: e.dma_start(out=out, in_=in_, **kw), reads=r, writes=w, dma=True)

        def mm(out, lhsT, rhs, start, stop, r=(), w=(), skip=False):
            return S.emit("pe", lambda e: e.matmul(out, lhsT, rhs, start=start, stop=stop, skip_group_check=skip),
                          reads=r, writes=w)

        def tr(out, in_, ident, r=(), w=()):
            return S.emit("pe", lambda e: e.transpose(out, in_, ident), reads=r, writes=w)

        def act(out, in_, func, r=(), w=(), **kw):
            return S.emit("act", lambda e: e.activation(out, in_, func, **kw), reads=r, writes=w)

        def tt(eng, out, in0, in1, op, r=(), w=()):
            return S.emit(eng, lambda e: e.tensor_tensor(out, in0, in1, op), reads=r, writes=w)

        def ts(eng, out, in0, s1, s2, op0, op1=None, r=(), w=()):
            if op1 is None:
                return S.emit(eng, lambda e: e.tensor_scalar(out, in0, s1, None, op0), reads=r, writes=w)
            return S.emit(eng, lambda e: e.tensor_scalar(out, in0, s1, s2, op0, op1), reads=r, writes=w)

        def stt(out, in0, scalar, in1, op0, op1, r=(), w=()):
            return S.emit("dve", lambda e: e.scalar_tensor_tensor(out, in0, scalar, in1, op0, op1), reads=r, writes=w)

        def cpy(eng, out, in_, r=(), w=()):
            if eng == "act":
                return S.emit("act", lambda e: e.copy(out, in_), reads=r, writes=w)
            return S.emit(eng, lambda e: e.tensor_copy(out, in_), reads=r, writes=w)

        def mset(eng, ap, val, w=()):
            return S.emit(eng, lambda e: e.memset(ap, val), writes=w)

        KB = Buf()
        ident_f = CONST.take([128, 128])
        ident_b = CONST.take([128, 128], BF16)
        ntri_b = CONST.take([128, 128], BF16)
        nones_b = CONST.take([128, 128], BF16)
        dmask_b = CONST.take([128, 128], BF16)
        zeros_b = CONST.take([128, 512], BF16)
        gmix = CONST.take([128, 8])
        gffn = CONST.take([128, 8])
        gfin = CONST.take([128, D])
        sbb = CONST.take([128, 8])
        cw = CONST.take([128, 12])
        sel_f = CONST.take([NM + ND, 256])
        ustr_f = CONST.take([128, 128])
        ptab = CONST.take([128, 2], I32)
        qtok = CONST.take([NM + ND, 512])
        scT = CONST.take([128, 4, ND, 2])
        udec = CONST.take([128, 4, ND])
        cbufs = []

        def cdma(q, out, in_, **kw):
            b = Buf()
            cbufs.append(b)
            dma(q, out, in_, w=[b], **kw)

        cdma("sp", ident_f, ident_in)
        cdma("pool", ident_b, ident_in)
        cdma("pool", ntri_b, ntri_in)
        cdma("pool", dmask_b, dmask_in)
        cdma("sp", gmix, nmix_in)
        cdma("sp", gffn, nffn_in)
        cdma("sp", gfin, nfin_in.broadcast_to([128, D]))
        cdma("sp", sbb, sbb_in.broadcast_to([128, 8]))
        cdma("sp", cw, cw_in)
        cdma("sp", sel_f, sel_in)
        cdma("sp", ustr_f, ustr_in)
        cdma("sp", ptab, pt_in)
        for n_ in range(4):
            cdma("sp", scT[:, n_].rearrange("p s r -> p (s r)"),
                 sc_in[:, :, n_ * 128:(n_ + 1) * 128].rearrange("s r p -> p (s r)"), allow_slow_non_contiguous=True)
        b = Buf()
        cbufs.append(b)
        mset("dve", nones_b, -1.0, w=[b])
        mset("dve", zeros_b, 0.0, w=[b])
        for e in ["pe", "act", "dve", "pool"]:
            S.pending[e] = [(bb.writer, True) for bb in cbufs]

        xt = [GEN.take([128, D]) for _ in range(2)]
        xtb = [Buf(), Buf()]
        xn = [GEN.take([128, D], BF16) for _ in range(2)]
        xnb = [Buf(), Buf()]
        junk = GEN.take([128, D], BF16)
        junkb = Buf()
        stat = [GEN.take([128, 4]) for _ in range(2)]
        statb = [Buf(), Buf()]
        stage = [GEN.take([128, 512]) for _ in range(2)]
        stageb = [Buf(), Buf()]
        rr = {"n": 0, "st": 0, "pb": 0}

        def bank():
            i = rr["pb"] % 6
            rr["pb"] += 1
            return PB[i], PBb[i]

        hT = TAIL.take([128, 8, 2068], BF16)
        ycT = TAIL.take([128, 4, 2052], BF16)
        oT = TAIL.take([128, 4, 2052], BF16)
        hTb = [Buf() for _ in range(5)]
        ycTb = [Buf() for _ in range(5)]
        oTb = [Buf() for _ in range(5)]

        def norm_transpose(src, srcb, rows, g, dst, dstb):
            i = rr["n"] % 2
            rr["n"] += 1
            st = stat[i]
            act(junk[0:rows, :], src, AF.Square, r=[srcb], w=[junkb, statb[i]], accum_out=st[0:rows, 0:1])
            act(st[0:rows, 1:2], st[0:rows, 0:1], AF.Sqrt, r=[statb[i]], w=[statb[i]], scale=1.0 / D, bias=EPS)
            S.emit("dve", lambda e: e.reciprocal(st[0:rows, 2:3], st[0:rows, 1:2]), reads=[statb[i]], writes=[statb[i]])
            ts("dve", xn[i][0:rows, :], src, st[0:rows, 2:3], None, ALU.mult, r=[srcb, statb[i]], w=[xnb[i]])
            for c in range(8):
                tr(PT[:, c, 0:rows], xn[i][0:rows, c * 128:(c + 1) * 128], ident_b[0:rows, 0:rows],
                   r=[xnb[i]], w=[PTb])
            tt("dve", dst, PT[:, :, 0:rows], g.unsqueeze(2).broadcast_to([128, 8, rows]), ALU.mult,
               r=[PTb], w=[dstb])
            return i

        S.enabled = "P1" in phases
        for i in range(NT + 1):
            k = i % 2
            if i < NT:
                rows = 128
                dma("sp", xt[k], x_in[i * 128:(i + 1) * 128, :], w=[xtb[k]])
                dst = hT[:, :, i * 128:(i + 1) * 128]
                db = hTb[i // 4]
            else:
                rows = NM + ND
                dma("sp", xt[k][0:rows, :], xe_in, w=[xtb[k]])
                dst = hT[:, :, 2048:2068]
                db = hTb[4]
            norm_transpose(xt[k][0:rows, :], xtb[k], rows, gmix, dst, db)

        S.enabled = "P2" in phases or "P2a" in phases
        P2 = Reg(big_start, big_end)
        qT = P2.take([128, 4, 2048], BF16)
        kT = P2.take([128, 4, 2064], BF16)
        v_sb = P2.take([128, 16, 512], BF16)
        vmeta = P2.take([NM, 512], BF16)
        qTb = [Buf() for _ in range(4)]
        kTb = [Buf() for _ in range(5)]
        vb = [Buf() for _ in range(17)]
        p3_start = P2.cur
        wsl = [P2.take([128, 8, 512], BF16) for _ in range(3)]
        wslb = [Buf() for _ in range(3)]
        ubuf = [P2.take([128, 2 + NM + T]) for _ in range(2)]
        ubufb = [Buf(), Buf()]
        cgt = [P2.take([128, 512]) for _ in range(2)]
        cgtb = [Buf(), Buf()]
        cvt = [P2.take([128, 512]) for _ in range(2)]
        cvtb = [Buf(), Buf()]
        w_in_v = w_in.rearrange("(c p) n -> p c n", p=128)

        def load_w(slot, g):
            dma("pool", wsl[slot], w_in_v[:, :, g * 512:(g + 1) * 512], w=[wslb[slot]])

        CH = [(c * 512, 512, c) for c in range(4)]

        def feat_mm(slot, nsub, col0, n, hb):
            pb, pbb = bank()
            for c in range(8):
                mm(pb[:, 0:n], wsl[slot][:, c, nsub * 128:(nsub + 1) * 128], hT[:, c, col0:col0 + n],
                   c == 0, c == 7, r=[wslb[slot], hb], w=[pbb])
            return pb, pbb

        def tok_mm(slot, col0, rows, hb):
            pb, pbb = bank()
            for c in range(8):
                mm(pb[0:rows, :], hT[:, c, col0:col0 + rows], wsl[slot][:, c, :], c == 0, c == 7,
                   r=[wslb[slot], hb], w=[pbb])
            return pb, pbb

        def stg():
            i = rr["st"] % 2
            rr["st"] += 1
            return stage[i], stageb[i]

        load_w(0, 1)
        load_w(1, 2)
        load_w(2, 0)
        import os
        DBG = os.environ.get("KDBG", "")
        if DBG == "loadonly":
            S.enabled = False
        for nsub in range(4):
            for (c0, n, ci) in CH + [(2048, NM, 4)]:
                pb, pbb = feat_mm(0, nsub, c0, n, hTb[ci])
                cpy("act", kT[:, nsub, c0:c0 + n], pb[:, 0:n], r=[pbb], w=[kTb[ci]])
        if DBG == "featonly":
            S.enabled = False
        for i in range(NT + 1):
            if DBG == "noext" and i == NT:
                continue
            if DBG == "extonly" and i < NT:
                continue
            rows = 128 if i < NT else NM + ND
            c0 = i * 128 if i < NT else 2048
            hb = hTb[i // 4] if i < NT else hTb[4]
            pb, pbb = tok_mm(0, c0, rows, hb)
            sg, sgb = stg()
            cpy("act", sg[0:rows, :], pb[0:rows, :], r=[pbb], w=[sgb])
            if i < NT:
                dma("sp", kp_out[NM + i * 128:NM + (i + 1) * 128, :], sg, r=[sgb])
            else:
                dma("sp", kp_out[0:NM, :], sg[0:NM, :], r=[sgb])
                dma("sp", ks_out, sg[NM:NM + ND, :], r=[sgb])
            pb, pbb = tok_mm(1, c0, rows, hb)
            sg, sgb = stg()
            cpy("act", sg[0:rows, :], pb[0:rows, :], r=[pbb], w=[sgb])
            if i < NT:
                cpy("dve", v_sb[:, i, :], pb, r=[pbb], w=[vb[i]])
                dma("sp", vp_out[NM + i * 128:NM + (i + 1) * 128, :], sg, r=[sgb])
            else:
                cpy("dve", vmeta, pb[0:NM, :], r=[pbb], w=[vb[16]])
                dma("sp", vp_out[0:NM, :], sg[0:NM, :], r=[sgb])
                dma("sp", vs_out, sg[NM:NM + ND, :], r=[sgb])
        if DBG in ("noq", "noext", "extonly", "noout"):
            S.enabled = False
        for nsub in range(4):
            for (c0, n, ci) in CH:
                pb, pbb = feat_mm(2, nsub, c0, n, hTb[ci])
                S.emit("act", (lambda o, i_: (lambda e: e.mul(o, i_, 0.125)))(qT[:, nsub, c0:c0 + n], pb[:, 0:n]),
                       reads=[pbb], writes=[qTb[ci]])
        qtokb = Buf()
        pb, pbb = tok_mm(2, 2048, NM + ND, hTb[4])
        cpy("act", qtok, pb[0:NM + ND, :], r=[pbb], w=[qtokb])

        S.enabled = "P2" in phases or "P2b" in phases
        load_w(0, 3)
        load_w(1, 4)
        load_w(2, 5)
        udecb = Buf()
        cpb = Buf()
        for nsub in range(4):
            ui = nsub % 2
            u = ubuf[ui]
            ub = ubufb[ui]
            mset("pool", u[:, 0:2], 0.0, w=[ub])
            w0 = cw[:, nsub * 3 + 0:nsub * 3 + 1]
            w1 = cw[:, nsub * 3 + 1:nsub * 3 + 2]
            w2 = cw[:, nsub * 3 + 2:nsub * 3 + 3]
            for (c0, n, ci) in [(2048, NM + ND, 4)] + CH:
                k = rr["n"] % 2
                rr["n"] += 1
                pcg, pcgb = feat_mm(1, nsub, c0, n, hTb[ci])
                pxc, pxcb = feat_mm(2, nsub, c0, n, hTb[ci])
                pbg, pbgb = feat_mm(0, nsub, c0, n, hTb[ci])
                cpy("act", cgt[k][:, 0:n], pcg[:, 0:n], r=[pcgb], w=[cgtb[k]])
                if ci == 4:
                    tt("dve", u[:, 2:2 + NM], pxc[:, 0:NM], cgt[k][:, 0:NM], ALU.mult, r=[pxcb, cgtb[k]], w=[ub])
                    tt("dve", udec[:, nsub, :], pxc[:, NM:NM + ND], cgt[k][:, NM:NM + ND], ALU.mult,
                       r=[pxcb, cgtb[k]], w=[udecb])
                    cv = cvt[k]
                    ts("dve", cv[:, 0:ND], scT[:, nsub, :, 0], w0, None, ALU.mult, r=[], w=[cvtb[k]])
                    stt(cv[:, 0:ND], scT[:, nsub, :, 1], w1, cv[:, 0:ND], ALU.mult, ALU.add, r=[cvtb[k]], w=[cvtb[k]])
                    stt(cv[:, 0:ND], udec[:, nsub, :], w2, cv[:, 0:ND], ALU.mult, ALU.add, r=[cvtb[k], udecb], w=[cvtb[k]])
                    tt("dve", ycT[:, nsub, 2048:2052], pbg[:, NM:NM + ND], cv[:, 0:ND], ALU.mult,
                       r=[pbgb, cvtb[k]], w=[ycTb[4]])
                else:
                    uo = 2 + NM + c0
                    tt("dve", u[:, uo:uo + n], pxc[:, 0:n], cgt[k][:, 0:n], ALU.mult, r=[pxcb, cgtb[k]], w=[ub])
                    cv = cvt[k]
                    ts("pool", cv[:, 0:n], u[:, uo - 2:uo - 2 + n], w0, None, ALU.mult, r=[ub], w=[cvtb[k]])
                    stt(cv[:, 0:n], u[:, uo - 1:uo - 1 + n], w1, cv[:, 0:n], ALU.mult, ALU.add, r=[ub, cvtb[k]], w=[cvtb[k]])
                    stt(cv[:, 0:n], u[:, uo:uo + n], w2, cv[:, 0:n], ALU.mult, ALU.add, r=[ub, cvtb[k]], w=[cvtb[k]])
                    tt("dve", ycT[:, nsub, c0:c0 + n], pbg[:, 0:n], cv[:, 0:n], ALU.mult, r=[pbgb, cvtb[k]], w=[ycTb[ci]])
            if "P2" in phases or "P2c" in phases:
              dma("sp", cp_out[:, nsub * 128:(nsub + 1) * 128].rearrange("r p -> p r"),
                u[:, 2 + NM + T - 2:2 + NM + T], r=[ub], w=[cpb], allow_slow_non_contiguous=True)
        S.enabled = "P2" in phases or "P2c" in phases
        dma("sp", cs_out[:, 0, :], sc_in[:, 1, :], w=[cpb])
        for n_ in range(4):
            dma("sp", cs_out[:, 1, n_ * 128:(n_ + 1) * 128].rearrange("s p -> p s"), udec[:, n_, :], r=[udecb], w=[cpb],
                allow_slow_non_contiguous=True)

        S.enabled = True
        S.barrier()
        run_p3 = "P3" in phases
        run_pd = "PD" in phases
        P3 = Reg(p3_start, big_end)
        e_t = [P3.take([128, 512]) for _ in range(2)]
        e_b = [Buf(), Buf()]
        sp_t = [P3.take([128, 512], BF16) for _ in range(3)]
        sp_b = [Buf() for _ in range(3)]
        tmp_t = [P3.take([128, 512]) for _ in range(2)]
        tmp_b = [Buf(), Buf()]
        w_t = [P3.take([128, 512], BF16) for _ in range(3)]
        w_b = [Buf() for _ in range(3)]
        R_t = [P3.take([128, 512]) for _ in range(3)]
        R_b = [Buf() for _ in range(3)]

        TK = 2
        NCH = 128 // TK
        GA = Reg(GEN.start, GEN.end)
        z_t = GA.take([128, 8, 128])
        e2_t = GA.take([128, 8, 128])
        s2_t = GA.take([128, 8, 128])
        pf_t = GA.take([128, 8, 128])
        zb, e2b, s2b, pfb = Buf(), Buf(), Buf(), Buf()
        kch = [P3.take([128, TK * 512]) for _ in range(3)]
        kchb = [Buf(), Buf(), Buf()]
        vch = [P3.take([128, TK * 512], BF16) for _ in range(3)]
        vchb = [Buf(), Buf(), Buf()]
        idxa = P3.take([128, 2, NCH], I32)
        idxb = Buf()
        qb_t = P3.take([128, 512])
        qbb = Buf()
        w2_t = P3.take([128, 8, 128])
        w2b = Buf()
        wz = P3.take([128, 128, 16], BF16)
        wzb = Buf()
        tot_t = P3.take([128, 16])
        totb = Buf()
        od_t = P3.take([16, 512])
        odb = Buf()
        odT = P3.take([128, 4, 16])
        odTb = Buf()
        ones_f = P3.take([128, 128])
        onesb = Buf()
        PD_BANK, PD_BANKb = PB[7], PBb[7]

        def gather(out, off, src, ob):
            return S.emit("pool", lambda e: e.indirect_dma_start(
                out=out, out_offset=None, in_=src,
                in_offset=bass.IndirectOffsetOnAxis(ap=off, axis=0)), reads=[idxb], writes=[ob], dma=True)

        def pd_gen():
            for pr in range(2):
                for ch in range(NCH):
                    ts("dve", idxa[:, pr, ch:ch + 1], ptab[:, pr:pr + 1], float(NCH), float(ch), ALU.mult, ALU.add, w=[idxb])
            mset("dve", ones_f, 1.0, w=[onesb])
            mset("pool", wz, 0.0, w=[wzb])
            yield
            for pr in range(2):
                mm(PD_BANK, sel_f[:, pr * 128:(pr + 1) * 128], qtok, True, True, r=[qtokb], w=[PD_BANKb])
                cpy("act", qb_t, PD_BANK, r=[PD_BANKb], w=[qbb])
                def k_reduce(ch_):
                    kk = ch_ % 3
                    S.emit("dve", (lambda o, i_: (lambda e: e.tensor_reduce(o, i_, AX.X, ALU.add)))(
                        z_t[:, :, ch_ * TK:(ch_ + 1) * TK].rearrange("p h t -> p t h"),
                        kch[kk].rearrange("p (t h d) -> p t h d", t=TK, h=8)), reads=[kchb[kk]], writes=[zb])

                gather(kch[0], idxa[:, pr, 0:1], ck_in, kchb[0])
                gather(kch[1], idxa[:, pr, 1:2], ck_in, kchb[1])
                for ch in range(NCH):
                    k = ch % 3
                    if ch >= 1:
                        k_reduce(ch - 1)
                    k3 = kch[k].rearrange("p (t n) -> p t n", t=TK)
                    tt("pool", k3, k3, qb_t.unsqueeze(1).broadcast_to([128, TK, 512]), ALU.mult,
                       r=[kchb[k], qbb], w=[kchb[k]])
                    if ch + 2 < NCH:
                        gather(kch[(ch + 2) % 3], idxa[:, pr, ch + 2:ch + 3], ck_in, kchb[(ch + 2) % 3])
                    yield "K"
                k_reduce(NCH - 1)
                stt(z_t, z_t, 0.125, sbb.unsqueeze(2).broadcast_to([128, 8, 128]), ALU.mult, ALU.add, r=[zb], w=[zb])
                act(e2_t, z_t, AF.Exp, r=[zb], w=[e2b])
                act(s2_t, e2_t, AF.Ln, r=[e2b], w=[s2b], bias=1.0)
                for h in range(8):
                    S.emit("dve", (lambda o, d0, d1: (lambda e: e.tensor_tensor_scan(o, d0, d1, 0.0, ALU.mult, ALU.add)))(
                        pf_t[:, h, :], ones_f, s2_t[:, h, :]), reads=[s2b, onesb], writes=[pfb])
                cpy("dve", tot_t[:, 0:8], pf_t[:, :, 127], r=[pfb], w=[totb])
                yield
                mm(PD_BANK[:, 0:8], ustr_f, tot_t[:, 0:8], True, True, r=[totb], w=[PD_BANKb])
                tt("dve", tot_t[:, 8:16], tot_t[:, 0:8], PD_BANK[:, 0:8], ALU.add, r=[totb, PD_BANKb], w=[totb])
                tt("dve", w2_t, pf_t, s2_t, ALU.subtract, r=[pfb, s2b], w=[w2b])
                tt("dve", w2_t, w2_t, z_t, ALU.add, r=[w2b, zb], w=[w2b])
                tt("dve", w2_t, w2_t, tot_t[:, 8:16].unsqueeze(2).broadcast_to([128, 8, 128]), ALU.subtract,
                   r=[w2b, totb], w=[w2b])
                act(e2_t, w2_t, AF.Exp, r=[w2b], w=[e2b])
                cpy("dve", wz[0:64, :, 0:8], e2_t[0:64].rearrange("p h t -> p t h"), r=[e2b], w=[wzb])
                cpy("dve", wz[64:128, :, 8:16], e2_t[64:128].rearrange("p h t -> p t h"), r=[e2b], w=[wzb])
                yield
                gather(vch[0], idxa[:, pr, 0:1], cv_in, vchb[0])
                gather(vch[1], idxa[:, pr, 1:2], cv_in, vchb[1])
                for ch in range(NCH):
                    k = ch % 3
                    for t in range(TK):
                        tok = ch * TK + t
                        mm(PD_BANK[0:16, :], wz[:, tok, :], vch[k][:, t * 512:(t + 1) * 512], tok == 0, tok == 127,
                           r=[wzb, vchb[k]], w=[PD_BANKb])
                    if ch + 2 < NCH:
                        gather(vch[(ch + 2) % 3], idxa[:, pr, ch + 2:ch + 3], cv_in, vchb[(ch + 2) % 3])
                    yield "V"
                cpy("act", od_t, PD_BANK[0:16, :], r=[PD_BANKb], w=[odb])
                for hq in range(4):
                    S.emit("pe", (lambda o, i_: (lambda e: e.transpose(o, i_, ident_f[0:16, 0:16])))(
                        PD_BANK[:, hq * 16:(hq + 1) * 16], od_t[:, hq * 128:(hq + 1) * 128]), reads=[odb], writes=[PD_BANKb])
                cpy("act", odT.rearrange("p a b -> p (a b)"), PD_BANK[:, 0:64], r=[PD_BANKb], w=[odTb])
                for h in range(8):
                    hq, hp = h // 2, h % 2
                    src = odT[hp * 64:(hp + 1) * 64, hq, h:h + 9:8]
                    cpy("dve", oT[hp * 64:(hp + 1) * 64, hq, 2048 + 2 * pr:2048 + 2 * pr + 2], src,
                        r=[odTb], w=[oTb[4]])
                yield

        pdg = pd_gen() if run_pd else iter(())

        pd_state = {"ph": "K", "n": 0}

        def pd_step(force=False):
            pd_state["n"] += 1
            if not force and pd_state["ph"] == "K" and pd_state["n"] % 3 == 0:
                return True
            S.enabled = True
            try:
                pd_state["ph"] = next(pdg)
            except StopIteration:
                return False
            finally:
                S.enabled = run_p3
            return True

        pairs = []
        for h in range(8):
            for c in range(4):
                blocks = list(range(4 * c + 3, -1, -1)) + [-1]
                for bi, i in enumerate(blocks):
                    pairs.append(dict(h=h, c=c, i=i, first=(bi == 0), last=(i == -1), g=h * 4 + c))
        NP = len(pairs)
        state = {"R": None, "ri": 0}

        def geom(p):
            h, c, i = p["h"], p["c"], p["i"]
            hq, hp = h // 2, h % 2
            if i >= 0:
                ns, kcol, kb = 128, i * 128, kTb[i // 4]
                vap, vbb = v_sb[:, i, h * 64:(h + 1) * 64], vb[i]
            else:
                ns, kcol, kb = NM, 2048, kTb[4]
                vap, vbb = vmeta[:, h * 64:(h + 1) * 64], vb[16]
            j = i - 4 * c
            t0 = j * 128 if j > 0 else 0
            return hq, hp * 64, ns, kcol, kb, vap, vbb, t0, (j >= 0)

        DUMb = Buf(True)

        def warm(n=1):
            for _ in range(n):
                mm(PB[4], zeros_b[:, 0:128], zeros_b[:, 0:512], True, True, w=[DUMb])

        def S1(k):
            p = pairs[k]
            h, c = p["h"], p["c"]
            hq, p0, ns, kcol, kb, vap, vbb, t0, diag = geom(p)
            ZB, ZBb = PB[k % 3], PBb[k % 3]
            Z = ZB[0:ns, t0:512]
            mm(Z, kT[p0:p0 + 64, hq, kcol:kcol + ns], qT[p0:p0 + 64, hq, c * 512 + t0:(c + 1) * 512],
               True, True, r=[kb, qTb[c]], w=[ZBb])
            warm(1)
            if diag:
                mm(ZB[:, t0:t0 + 128], ident_b, dmask_b, False, True, w=[ZBb], skip=True)
            e = e_t[k % 2][0:ns, t0:512]
            act(e, Z, AF.Exp, r=[ZBb], w=[e_b[k % 2]], bias=sbb[0:ns, h:h + 1])
            spv = sp_t[k % 3][0:ns, t0:512]
            act(spv, e, AF.Ln, r=[e_b[k % 2]], w=[sp_b[k % 3]], bias=1.0)

        def S2(k):
            p = pairs[k]
            h, c = p["h"], p["c"]
            hq, p0, ns, kcol, kb, vap, vbb, t0, diag = geom(p)
            ZB, ZBb = PB[k % 3], PBb[k % 3]
            g = p["g"]
            CB, CBb = PB[3], PBb[3]
            Z = ZB[0:ns, t0:512]
            spv = sp_t[k % 3][0:ns, t0:512]
            mm(Z, ntri_b[0:ns, 0:ns], spv, False, True, r=[sp_b[k % 3]], w=[ZBb], skip=True)
            if not p["last"]:
                mm(CB[:, t0:512], nones_b[0:ns, :], spv, p["first"], True, r=[sp_b[k % 3]], w=[CBb],
                   skip=not p["first"])
            warm(1)
            wv = w_t[k % 3][0:ns, t0:512]
            Rcur = state["R"]
            if p["first"]:
                act(wv, Z, AF.Exp, r=[ZBb], w=[w_b[k % 3]], bias=sbb[0:ns, h:h + 1])
            else:
                tm = tmp_t[k % 2][0:ns, t0:512]
                tt("dve", tm, Z, Rcur[0][0:ns, t0:512], ALU.add, r=[ZBb, Rcur[1]], w=[tmp_b[k % 2]])
                act(wv, tm, AF.Exp, r=[tmp_b[k % 2]], w=[w_b[k % 3]], bias=sbb[0:ns, h:h + 1])
            if not p["last"]:
                ri = state["ri"]
                state["ri"] = (ri + 1) % 3
                Rn = (R_t[ri], R_b[ri])
                if t0 > 0:
                    mset("dve", Rn[0][:, 0:t0], 0.0, w=[Rn[1]])
                cpy("dve", Rn[0][:, t0:512], CB[:, t0:512], r=[CBb], w=[Rn[1]])
                state["R"] = Rn
            else:
                state["R"] = None

        def S3(k):
            p = pairs[k]
            h, c, g = p["h"], p["c"], p["g"]
            hq, p0, ns, kcol, kb, vap, vbb, t0, diag = geom(p)
            OB, OBb = PB[5 + g % 2], PBb[5 + g % 2]
            if p["first"]:
                mm(OB[p0:p0 + 64, :], zeros_b[:, 0:64], zeros_b[:, 0:512], True, False, w=[OBb])
            wv = w_t[k % 3][0:ns, t0:512]
            mm(OB[p0:p0 + 64, t0:512], vap, wv, False, p["last"], r=[vbb, w_b[k % 3]], w=[OBb])
            warm(1)
            if p["last"]:
                cpy("dve", oT[p0:p0 + 64, hq, c * 512:(c + 1) * 512], OB[p0:p0 + 64, :], r=[OBb], w=[oTb[c]])

        S.enabled = run_p3
        for s in range(NP + 2):
            if s < NP:
                S1(s)
            if 0 <= s - 1 < NP:
                S2(s - 1)
            if 0 <= s - 2 < NP:
                S3(s - 2)
            pd_step()
        while pd_step(True):
            pass

        S.enabled = True
        S.barrier()
        S.enabled = "P4" in phases
        P4 = Reg(big_start, big_end)
        wga = P4.take([128, 8, 1024], BF16)
        wgb = P4.take([128, 8, 1024], BF16)
        wao = P4.take([128, 4, 1024], BF16)
        wco = P4.take([128, 4, 1024], BF16)
        wo = P4.take([128, 8, 1024], BF16)
        w4b = [Buf() for _ in range(5)]
        dma("pool", wao, w_ao.rearrange("(c p) n -> p c n", p=128), w=[w4b[2]])
        dma("pool", wco, w_co.rearrange("(c p) n -> p c n", p=128), w=[w4b[3]])
        for hf in range(2):
            dma("pool", wga[:, :, hf * 512:(hf + 1) * 512], w_in_v[:, :, 3072 + hf * 512:3072 + (hf + 1) * 512], w=[w4b[0]])
            dma("pool", wgb[:, :, hf * 512:(hf + 1) * 512], w_in_v[:, :, 4096 + hf * 512:4096 + (hf + 1) * 512], w=[w4b[1]])
        w_o_v = w_o.rearrange("(c p) n -> p c n", p=128)
        for hf in range(2):
            dma("pool", wo[:, :, hf * 512:(hf + 1) * 512], w_o_v[:, :, hf * 512:(hf + 1) * 512], w=[w4b[4]])
        mT = [P4.take([128, 8, 512], BF16) for _ in range(2)]
        mTb = [Buf(), Buf()]
        sga = [P4.take([128, 512]) for _ in range(2)]
        sgab = [Buf(), Buf()]
        sgb_ = [P4.take([128, 512]) for _ in range(2)]
        sgbb = [Buf(), Buf()]
        t1 = [P4.take([128, 512]) for _ in range(2)]
        t1b = [Buf(), Buf()]
        x1t = [P4.take([128, D]) for _ in range(2)]
        x1tb = [Buf(), Buf()]
        x1sb = Buf()
        it = [0]
        for (hc0, oc0, n, ci) in [(c * 512, c * 512, 512, c) for c in range(4)] + [(2064, 2048, ND, 4)]:
            mi = ci % 2
            for dmc in range(8):
                k = it[0] % 2
                it[0] += 1
                dsl = slice(dmc * 128, (dmc + 1) * 128)
                A, Ab = bank()
                for c in range(4):
                    mm(A[:, 0:n], wao[:, c, dsl], oT[:, c, oc0:oc0 + n], c == 0, c == 3, r=[w4b[2], oTb[ci]], w=[Ab])
                Bk, Bb = bank()
                for c in range(4):
                    mm(Bk[:, 0:n], wco[:, c, dsl], ycT[:, c, oc0:oc0 + n], c == 0, c == 3, r=[w4b[3], ycTb[ci]], w=[Bb])
                G, Gb = bank()
                for c in range(8):
                    mm(G[:, 0:n], wga[:, c, dsl], hT[:, c, hc0:hc0 + n], c == 0, c == 7, r=[w4b[0], hTb[ci]], w=[Gb])
                H, Hb = bank()
                for c in range(8):
                    mm(H[:, 0:n], wgb[:, c, dsl], hT[:, c, hc0:hc0 + n], c == 0, c == 7, r=[w4b[1], hTb[ci]], w=[Hb])
                act(sga[k][:, 0:n], G[:, 0:n], AF.Sigmoid, r=[Gb], w=[sgab[k]])
                act(sgb_[k][:, 0:n], H[:, 0:n], AF.Sigmoid, r=[Hb], w=[sgbb[k]])
                tt("dve", t1[k][:, 0:n], A[:, 0:n], sga[k][:, 0:n], ALU.mult, r=[Ab, sgab[k]], w=[t1b[k]])
                tt("dve", sgb_[k][:, 0:n], Bk[:, 0:n], sgb_[k][:, 0:n], ALU.mult, r=[Bb, sgbb[k]], w=[sgbb[k]])
                tt("pool", mT[mi][:, dmc, 0:n], t1[k][:, 0:n], sgb_[k][:, 0:n], ALU.add, r=[t1b[k], sgbb[k]], w=[mTb[mi]])
            ntile = 4 if ci < 4 else 1
            for tl in range(ntile):
                rows = 128 if ci < 4 else ND
                k = it[0] % 2
                it[0] += 1
                if ci < 4:
                    row0 = ci * 512 + tl * 128
                    dma("sp", x1t[k], x_in[row0:row0 + 128, :], w=[x1tb[k]])
                else:
                    row0 = T
                    dma("sp", x1t[k][0:ND, :], xe_in[NM:NM + ND, :], w=[x1tb[k]])
                for hf in range(2):
                    Pk, Pb_ = bank()
                    for c in range(8):
                        mm(Pk[0:rows, :], mT[mi][:, c, tl * 128:tl * 128 + rows], wo[:, c, hf * 512:(hf + 1) * 512],
                           c == 0, c == 7, r=[mTb[mi], w4b[4]], w=[Pb_])
                    tt("dve", x1t[k][0:rows, hf * 512:(hf + 1) * 512], Pk[0:rows, :],
                       x1t[k][0:rows, hf * 512:(hf + 1) * 512], ALU.add, r=[Pb_, x1tb[k]], w=[x1tb[k]])
                dma("sp", x1s[row0:row0 + rows, :], x1t[k][0:rows, :], r=[x1tb[k]], w=[x1sb])

        S.enabled = True
        S.barrier()
        S.enabled = "P5" in phases
        P5 = Reg(big_start, TOTAL)
        wd = P5.take([128, NF, 1024], BF16)
        wdb = Buf()
        w_d_v = w_d.rearrange("(c p) n -> p c n", p=128)
        for hf in range(2):
            for fh in range(2):
                dma("pool", wd[:, fh * 11:(fh + 1) * 11, hf * 512:(hf + 1) * 512],
                    w_d_v[:, fh * 11:(fh + 1) * 11, hf * 512:(hf + 1) * 512], w=[wdb])
        HC = 1024 + ND
        aT = P5.take([128, NF, HC], BF16)
        aTb = [Buf() for _ in range(3)]
        h2 = P5.take([128, 8, HC], BF16)
        h2b = [Buf() for _ in range(3)]
        x1q = P5.take([128, 9, D])
        x1qb = [Buf() for _ in range(9)]
        wgs = [P5.take([128, 8, 256], BF16) for _ in range(2)]
        wgsb = [Buf(), Buf()]
        wus = [P5.take([128, 8, 256], BF16) for _ in range(2)]
        wusb = [Buf(), Buf()]
        sg_t = [P5.take([128, 512]) for _ in range(2)]
        sg_b = [Buf(), Buf()]
        yst = [P5.take([128, D]) for _ in range(2)]
        ystb = [Buf(), Buf()]
        w_g_v = w_g.rearrange("(c p) n -> p c n", p=128)
        w_u_v = w_u.rearrange("(c p) n -> p c n", p=128)
        wl = [0]
        for hh in range(2):
            ntl = 9 if hh == 1 else 8
            for tl in range(ntl):
                rows = 128 if tl < 8 else ND
                row0 = hh * 1024 + tl * 128 if tl < 8 else T
                dma("sp", x1q[0:rows, tl, :], x1s[row0:row0 + rows, :], r=[x1sb], w=[x1qb[tl]])
                norm_transpose(x1q[0:rows, tl, :], x1qb[tl], rows, gffn,
                               h2[:, :, tl * 128:tl * 128 + rows], h2b[tl // 4])
            cols = [(0, 512, 0), (512, 512, 1)] + ([(1024, ND, 2)] if hh == 1 else [])
            for fg in range(11):
                s = wl[0] % 2
                wl[0] += 1
                dma("pool", wgs[s], w_g_v[:, :, fg * 256:(fg + 1) * 256], w=[wgsb[s]])
                dma("pool", wus[s], w_u_v[:, :, fg * 256:(fg + 1) * 256], w=[wusb[s]])
                for fs in range(2):
                    f = fg * 2 + fs
                    for (c0, n, cb) in cols:
                        k = it[0] % 2
                        it[0] += 1
                        G, Gb = bank()
                        for c in range(8):
                            mm(G[:, 0:n], wgs[s][:, c, fs * 128:(fs + 1) * 128], h2[:, c, c0:c0 + n],
                               c == 0, c == 7, r=[wgsb[s], h2b[cb]], w=[Gb])
                        U, Ub = bank()
                        for c in range(8):
                            mm(U[:, 0:n], wus[s][:, c, fs * 128:(fs + 1) * 128], h2[:, c, c0:c0 + n],
                               c == 0, c == 7, r=[wusb[s], h2b[cb]], w=[Ub])
                        act(sg_t[k][:, 0:n], G[:, 0:n], AF.Silu, r=[Gb], w=[sg_b[k]])
                        tt("dve", aT[:, f, c0:c0 + n], U[:, 0:n], sg_t[k][:, 0:n], ALU.mult, r=[Ub, sg_b[k]], w=[aTb[cb]])
            for tl in range(ntl):
                rows = 128 if tl < 8 else ND
                row0 = hh * 1024 + tl * 128 if tl < 8 else T
                xq = x1q[0:rows, tl, :]
                xqb = x1qb[tl]
                for hf in range(2):
                    Pk, Pb_ = bank()
                    for f in range(NF):
                        mm(Pk[0:rows, :], aT[:, f, tl * 128:tl * 128 + rows], wd[:, f, hf * 512:(hf + 1) * 512],
                           f == 0, f == NF - 1, r=[aTb[tl // 4], wdb], w=[Pb_])
                    tt("dve", xq[:, hf * 512:(hf + 1) * 512], Pk[0:rows, :], xq[:, hf * 512:(hf + 1) * 512], ALU.add,
                       r=[Pb_, xqb], w=[xqb])
                i = rr["n"] % 2
                rr["n"] += 1
                st = stat[i]
                act(junk[0:rows, :], xq, AF.Square, r=[xqb], w=[junkb, statb[i]], accum_out=st[0:rows, 0:1])
                act(st[0:rows, 1:2], st[0:rows, 0:1], AF.Sqrt, r=[statb[i]], w=[statb[i]], scale=1.0 / D, bias=EPS)
                S.emit("dve", (lambda o, i_: (lambda e: e.reciprocal(o, i_)))(st[0:rows, 2:3], st[0:rows, 1:2]),
                       reads=[statb[i]], writes=[statb[i]])
                yk = it[0] % 2
                it[0] += 1
                stt(yst[yk][0:rows, :], xq, st[0:rows, 2:3], gfin[0:rows, :], ALU.mult, ALU.mult,
                    r=[xqb, statb[i]], w=[ystb[yk]])
                if tl < 8:
                    dma("sp", y_out[row0:row0 + 128, :], yst[yk], r=[ystb[yk]])
                else:
                    dma("sp", ys_out, yst[yk][0:ND, :], r=[ystb[yk]])

        S.enabled = True
        replay = S.finalize(sems, dsems)
        with nc.Block() as block:
            @block.tensor
            def _(e):
                replay("pe", e)

            @block.scalar
            def _(e):
                replay("act", e)

            @block.vector
            def _(e):
                replay("dve", e)

            @block.gpsimd
            def _(e):
                replay("pool", e)

            @block.sync
            def _(e):
                replay("sp", e)
    return nc


_CACHE = {}


def _consts():
    ident = np.eye(128, dtype=np.float32)
    j = np.arange(128)[:, None]
    s = np.arange(128)[None, :]
    ntri = np.where(j >= s, -1.0, 0.0).astype(np.float32)
    dmask = np.where(j < s, 0.0, -30000.0).astype(np.float32)
    sel = np.zeros((NM + ND, 256), np.float32)
    for pr in range(2):
        for p in range(128):
            sel[NM + 2 * pr + p // 64, pr * 128 + p] = 1.0
    ustr = np.where((j > s) & ((j // 64) == (s // 64)), 1.0, 0.0).astype(np.float32)
    return ident, ntri, dmask, sel, ustr


def _prepare(x_prompt, x_sample, cache_k, cache_v, state_conv, page_table, meta_tokens,
             norm_mix, w_in, sb_bias, conv_w, w_att_out, w_conv_out, w_o, norm_ffn,
             w_gate, w_up, w_down, norm_final, cores=range(NCORES)):
    f = lambda a: np.ascontiguousarray(np.asarray(a, dtype=np.float32))
    x_prompt = f(x_prompt)
    x_sample = f(x_sample)
    ck = f(cache_k).reshape(NPHYS * 64, 1024)
    cv = f(cache_v).reshape(NPHYS * 64, 1024)
    state_conv = f(state_conv)
    page_table = np.asarray(page_table, dtype=np.int32)
    meta = f(meta_tokens)
    ident, ntri, dmask, sel, ustr = _consts()
    col = lambda v: np.ascontiguousarray(f(v).reshape(8, 128).T)
    shared = {
        "cache_k": ck, "cache_v": cv,
        "w_in": f(w_in)[0], "w_att_out": f(w_att_out)[0], "w_conv_out": f(w_conv_out)[0], "w_o": f(w_o)[0],
        "w_gate": f(w_gate)[0], "w_up": f(w_up)[0], "w_down": f(w_down)[0],
        "nmix": col(norm_mix[0]), "nffn": col(norm_ffn[0]), "nfin": f(norm_final).reshape(1, D),
        "sbb": f(sb_bias).reshape(1, 8),
        "cw": np.ascontiguousarray(f(conv_w)[0].reshape(3, 4, 128).transpose(2, 1, 0).reshape(128, 12)),
        "ident": ident, "ntri": ntri, "dmask": dmask, "sel": sel, "ustr": ustr,
    }
    in_maps = []
    for c in cores:
        pt = page_table[4 * c:4 * c + 4]
        ptc = np.ascontiguousarray(pt.reshape(2, 128).T).astype(np.int32)
        m = dict(shared)
        m["x"] = x_prompt[c]
        m["xe"] = np.ascontiguousarray(np.concatenate([meta, x_sample[4 * c:4 * c + 4, 0, :]], axis=0))
        m["state_conv"] = np.ascontiguousarray(state_conv[0, 4 * c:4 * c + 4])
        m["pt"] = ptc
        in_maps.append(m)
    return in_maps


def kernel(**inputs):
    in_maps = _prepare(**inputs)
    if "nc" not in _CACHE:
        _CACHE["nc"] = build_program()
    nc = _CACHE["nc"]
    res = run_bass_kernel_spmd(nc, in_maps, core_ids=list(range(NCORES)))
    R = res.results
    y = np.stack([R[c]["y"] for c in range(NCORES)], axis=0)
    ys = np.concatenate([R[c]["ys"] for c in range(NCORES)], axis=0).reshape(32, 1, D)
    kp = np.stack([R[c]["kp"] for c in range(NCORES)], axis=0).reshape(1, 8, NM + T, 8, 64)
    vp = np.stack([R[c]["vp"] for c in range(NCORES)], axis=0).reshape(1, 8, NM + T, 8, 64)
    cp = np.stack([R[c]["cp"] for c in range(NCORES)], axis=0).reshape(1, 8, 2, 512)
    ks = np.concatenate([R[c]["ks"] for c in range(NCORES)], axis=0).reshape(1, 32, 1, 8, 64)
    vs = np.concatenate([R[c]["vs"] for c in range(NCORES)], axis=0).reshape(1, 32, 1, 8, 64)
    cs = np.concatenate([R[c]["cs"] for c in range(NCORES)], axis=0).reshape(1, 32, 2, 512)
    return (y.astype(np.float32), ys.astype(np.float32), kp.astype(np.float32), vp.astype(np.float32),
            cp.astype(np.float32), ks.astype(np.float32), vs.astype(np.float32), cs.astype(np.float32))
```

```python
import contextlib
import numpy as np
import concourse.bass as bass
import concourse.mybir as mybir
from concourse.bass_utils import run_bass_kernel_spmd

F32 = mybir.dt.float32
BF16 = mybir.dt.bfloat16
I32 = mybir.dt.int32
AF = mybir.ActivationFunctionType
ALU = mybir.AluOpType
AX = mybir.AxisListType

D = 1024
T = 2048
NM = 16
ND = 4
NT = T // 128
DFF = 2816
NF = DFF // 128
NPHYS = 2560
EPS = 1e-6
NCORES = 8
ARENA_F32 = 52480

DMA_K = {"sp": 12, "pool": 12, "act": 4}
ENGS = ["pe", "act", "dve", "pool", "sp"]


class Buf:
    __slots__ = ("writer", "readers", "excl")

    def __init__(self, excl=False):
        self.writer = None
        self.readers = []
        self.excl = excl


class Ins:
    __slots__ = ("eng", "fn", "deps", "needs_inc", "semval", "is_dma", "dsem", "dval", "didx")

    def __init__(self, eng, fn, is_dma):
        self.eng = eng
        self.fn = fn
        self.is_dma = is_dma
        self.deps = []
        self.needs_inc = False
        self.semval = 0
        self.dsem = None
        self.dval = 0
        self.didx = 0


class Sched:
    def __init__(self):
        self.ins = []
        self.last = {}
        self.recent_dma = {q: [] for q in DMA_K}
        self.pending = {}
        self.enabled = True

    def emit(self, eng, fn, reads=(), writes=(), dma=False):
        ins = Ins(eng, fn, dma)
        if not self.enabled:
            return ins
        deps = []
        for b in reads:
            if b.writer is not None:
                deps.append((b.writer, True))
            if b.excl:
                for r in b.readers:
                    if r.eng != eng:
                        deps.append((r, False))
        for b in writes:
            if b.writer is not None:
                deps.append((b.writer, False))
            for r in b.readers:
                deps.append((r, False))
        if eng in self.pending:
            deps.extend(self.pending.pop(eng))
        ins.deps = deps
        for b in reads:
            if not dma:
                b.readers = [r for r in b.readers if r.is_dma or r.eng != eng]
            b.readers.append(ins)
        for b in writes:
            b.writer = ins
            b.readers = []
        self.ins.append(ins)
        if dma:
            lst = self.recent_dma[eng]
            lst.append(ins)
            if len(lst) > DMA_K[eng]:
                lst.pop(0)
        else:
            self.last[eng] = ins
        return ins

    def barrier(self):
        deps = [(i, True) for i in self.last.values()]
        for q in DMA_K:
            deps.extend((i, True) for i in self.recent_dma[q])
        for e in ENGS:
            self.pending[e] = list(deps) + self.pending.get(e, [])

    @staticmethod
    def _need(ins, d, raw):
        if d is ins:
            return False
        if d.is_dma or ins.is_dma:
            return True
        if d.eng != ins.eng:
            return True
        if d.eng == "pe":
            return False
        return raw

    def finalize(self, sems, dsems):
        for ins in self.ins:
            for (d, raw) in ins.deps:
                if not d.is_dma and self._need(ins, d, raw):
                    d.needs_inc = True
        cnt = {}
        dcnt = {}
        for ins in self.ins:
            if ins.is_dma:
                i = dcnt.get(ins.eng, 0)
                K = DMA_K[ins.eng]
                ins.didx = i
                ins.dsem = dsems[ins.eng][i % K]
                ins.dval = 16 * (i // K + 1)
                dcnt[ins.eng] = i + 1
            elif ins.needs_inc:
                cnt[ins.eng] = cnt.get(ins.eng, 0) + 1
                ins.semval = cnt[ins.eng]
        per = {e: [] for e in ENGS}
        for ins in self.ins:
            per[ins.eng].append(ins)

        def replay(ename, handle):
            waited = {}
            for ins in per[ename]:
                waits = {}
                for (d, raw) in ins.deps:
                    if not self._need(ins, d, raw):
                        continue
                    if d.is_dma:
                        key = ("d", d.eng, d.didx % DMA_K[d.eng])
                        sem = d.dsem
                        val = d.dval
                    else:
                        key = ("e", d.eng)
                        sem = sems[d.eng]
                        val = d.semval
                    if key not in waits or waits[key][1] < val:
                        waits[key] = (sem, val)
                if ins.is_dma and ins.dval > 16:
                    key = ("d", ins.eng, ins.didx % DMA_K[ins.eng])
                    val = ins.dval - 16
                    if key not in waits or waits[key][1] < val:
                        waits[key] = (ins.dsem, val)
                for key, (sem, val) in waits.items():
                    if waited.get(key, 0) < val:
                        handle.wait_ge(sem, val)
                        waited[key] = val
                r = ins.fn(handle)
                if ins.is_dma:
                    r.then_inc(ins.dsem, 16)
                elif ins.needs_inc:
                    r.then_inc(sems[ename], 1)
            if ename in dcnt:
                K = DMA_K[ename]
                n = dcnt[ename]
                for k in range(min(K, n)):
                    uses = (n - 1 - k) // K + 1
                    handle.wait_ge(dsems[ename][k], 16 * uses)

        return replay


ALL_PHASES = ("P1", "P2", "P3", "PD", "P4", "P5")


def build_program(phases=ALL_PHASES, cache_rows=NPHYS * 64):
    nc = bass.Bass("TRN2", target_bir_lowering=False)
    S = Sched()

    def din(name, shape, dtype=F32):
        return nc.dram_tensor(name, list(shape), dtype, kind="ExternalInput").ap()

    def dout(name, shape):
        return nc.dram_tensor(name, list(shape), F32, kind="ExternalOutput").ap()

    x_in = din("x", [T, D])
    xe_in = din("xe", [NM + ND, D])
    ck_in = din("cache_k", [cache_rows, 1024])
    cv_in = din("cache_v", [cache_rows, 1024])
    sc_in = din("state_conv", [ND, 2, 512])
    pt_in = din("pt", [128, 2], I32)
    w_in = din("w_in", [D, 5120])
    w_ao = din("w_att_out", [512, D])
    w_co = din("w_conv_out", [512, D])
    w_o = din("w_o", [D, D])
    w_g = din("w_gate", [D, DFF])
    w_u = din("w_up", [D, DFF])
    w_d = din("w_down", [DFF, D])
    nmix_in = din("nmix", [128, 8])
    nffn_in = din("nffn", [128, 8])
    nfin_in = din("nfin", [1, D])
    sbb_in = din("sbb", [1, 8])
    cw_in = din("cw", [128, 12])
    ident_in = din("ident", [128, 128])
    ntri_in = din("ntri", [128, 128])
    dmask_in = din("dmask", [128, 128])
    sel_in = din("sel", [NM + ND, 256])
    ustr_in = din("ustr", [128, 128])

    y_out = dout("y", [T, D])
    ys_out = dout("ys", [ND, D])
    kp_out = dout("kp", [NM + T, 512])
    vp_out = dout("vp", [NM + T, 512])
    cp_out = dout("cp", [2, 512])
    ks_out = dout("ks", [ND, 512])
    vs_out = dout("vs", [ND, 512])
    cs_out = dout("cs", [ND, 2, 512])
    x1s = nc.dram_tensor("x1s", [T + ND, D], F32, kind="Internal").ap()

    es = contextlib.ExitStack()
    with es:
        sems = {e: es.enter_context(nc.semaphore("s_" + e)) for e in ["pe", "act", "dve", "pool"]}
        dsems = {q: [es.enter_context(nc.semaphore("d_%s%d" % (q, i))) for i in range(k)]
                 for q, k in DMA_K.items()}
        arena = es.enter_context(nc.sbuf_tensor("arena", [128, ARENA_F32], F32))
        PB = [es.enter_context(nc.psum_tensor("pb%d" % i, [128, 512], F32))[:, :] for i in range(8)]
        PBb = [Buf(True) for _ in range(8)]
        PT = PB[7].bitcast(BF16).rearrange("p (a b) -> p a b", a=8)
        PTb = PBb[7]

        class Reg:
            def __init__(self, start, end):
                self.start = start
                self.end = end
                self.cur = start

            def take(self, shape, dtype=F32):
                esz = 4 if dtype in (F32, I32) else 2
                n = 1
                for s in shape[1:]:
                    n *= s
                nb = (n * esz + 63) // 64 * 64
                assert self.cur + nb <= self.end, ("arena overflow", shape, self.cur, nb, self.end)
                o4 = self.cur // 4
                v = arena[0:shape[0], o4:o4 + nb // 4]
                if dtype != F32:
                    v = v.bitcast(dtype)
                v = v[:, 0:n]
                if len(shape) == 3:
                    v = v.rearrange("p (a b) -> p a b", a=shape[1])
                elif len(shape) == 4:
                    v = v.rearrange("p (a b c) -> p a b c", a=shape[1], b=shape[2])
                self.cur += nb
                return v

            def sub(self, nbytes):
                r = Reg(self.cur, self.cur + nbytes)
                self.cur += nbytes
                assert self.cur <= self.end
                return r

        TOTAL = ARENA_F32 * 4
        top = Reg(0, TOTAL)
        CONST = top.sub(14 * 1024)
        GEN = top.sub(19 * 1024)
        SZ_HT = 8 * 2068 * 2
        SZ_YC = 4 * 2052 * 2
        tail_bytes = SZ_HT + 2 * SZ_YC + 192
        BIG = top.sub(TOTAL - top.cur - tail_bytes)
        TAIL = top.sub(tail_bytes)
        big_start, big_end = BIG.start, BIG.end

        def dma(q, out, in_, r=(), w=(), **kw):
            import os
            if os.environ.get("KDBG", "") == "noout" and getattr(out.tensor, "name", "") in ("kp", "vp", "ks", "vs"):
                return None
            return S.emit(q, lambda e: e.dma_start(out=out, in_=in_, **kw), reads=r, writes=w, dma=True)

        def mm(out, lhsT, rhs, start, stop, r=(), w=(), skip=False):
            return S.emit("pe", lambda e: e.matmul(out, lhsT, rhs, start=start, stop=stop, skip_group_check=skip),
                          reads=r, writes=w)

        def tr(out, in_, ident, r=(), w=()):
            return S.emit("pe", lambda e: e.transpose(out, in_, ident), reads=r, writes=w)

        def act(out, in_, func, r=(), w=(), **kw):
            return S.emit("act", lambda e: e.activation(out, in_, func, **kw), reads=r, writes=w)

        def tt(eng, out, in0, in1, op, r=(), w=()):
            return S.emit(eng, lambda e: e.tensor_tensor(out, in0, in1, op), reads=r, writes=w)

        def ts(eng, out, in0, s1, s2, op0, op1=None, r=(), w=()):
            if op1 is None:
                return S.emit(eng, lambda e: e.tensor_scalar(out, in0, s1, None, op0), reads=r, writes=w)
            return S.emit(eng, lambda e: e.tensor_scalar(out, in0, s1, s2, op0, op1), reads=r, writes=w)

        def stt(out, in0, scalar, in1, op0, op1, r=(), w=()):
            return S.emit("dve", lambda e: e.scalar_tensor_tensor(out, in0, scalar, in1, op0, op1), reads=r, writes=w)

        def cpy(eng, out, in_, r=(), w=()):
            if eng == "act":
                return S.emit("act", lambda e: e.copy(out, in_), reads=r, writes=w)
            return S.emit(eng, lambda e: e.tensor_copy(out, in_), reads=r, writes=w)

        def mset(eng, ap, val, w=()):
            return S.emit(eng, lambda e: e.memset(ap, val), writes=w)

        KB = Buf()
        ident_f = CONST.take([128, 128])
        ident_b = CONST.take([128, 128], BF16)
        ntri_b = CONST.take([128, 128], BF16)
        nones_b = CONST.take([128, 128], BF16)
        dmask_b = CONST.take([128, 128], BF16)
        zeros_b = CONST.take([128, 512], BF16)
        gmix = CONST.take([128, 8])
        gffn = CONST.take([128, 8])
        gfin = CONST.take([128, D])
        sbb = CONST.take([128, 8])
        cw = CONST.take([128, 12])
        sel_f = CONST.take([NM + ND, 256])
        ustr_f = CONST.take([128, 128])
        ptab = CONST.take([128, 2], I32)
        qtok = CONST.take([NM + ND, 512])
        scT = CONST.take([128, 4, ND, 2])
        udec = CONST.take([128, 4, ND])
        cbufs = []

        def cdma(q, out, in_, **kw):
            b = Buf()
            cbufs.append(b)
            dma(q, out, in_, w=[b], **kw)

        cdma("sp", ident_f, ident_in)
        cdma("pool", ident_b, ident_in)
        cdma("pool", ntri_b, ntri_in)
        cdma("pool", dmask_b, dmask_in)
        cdma("sp", gmix, nmix_in)
        cdma("sp", gffn, nffn_in)
        cdma("sp", gfin, nfin_in.broadcast_to([128, D]))
        cdma("sp", sbb, sbb_in.broadcast_to([128, 8]))
        cdma("sp", cw, cw_in)
        cdma("sp", sel_f, sel_in)
        cdma("sp", ustr_f, ustr_in)
        cdma("sp", ptab, pt_in)
        for n_ in range(4):
            cdma("sp", scT[:, n_].rearrange("p s r -> p (s r)"),
                 sc_in[:, :, n_ * 128:(n_ + 1) * 128].rearrange("s r p -> p (s r)"), allow_slow_non_contiguous=True)
        b = Buf()
        cbufs.append(b)
        mset("dve", nones_b, -1.0, w=[b])
        mset("dve", zeros_b, 0.0, w=[b])
        for e in ["pe", "act", "dve", "pool"]:
            S.pending[e] = [(bb.writer, True) for bb in cbufs]

        xt = [GEN.take([128, D]) for _ in range(2)]
        xtb = [Buf(), Buf()]
        xn = [GEN.take([128, D], BF16) for _ in range(2)]
        xnb = [Buf(), Buf()]
        junk = GEN.take([128, D], BF16)
        junkb = Buf()
        stat = [GEN.take([128, 4]) for _ in range(2)]
        statb = [Buf(), Buf()]
        stage = [GEN.take([128, 512]) for _ in range(2)]
        stageb = [Buf(), Buf()]
        rr = {"n": 0, "st": 0, "pb": 0}

        def bank():
            i = rr["pb"] % 6
            rr["pb"] += 1
            return PB[i], PBb[i]

        hT = TAIL.take([128, 8, 2068], BF16)
        ycT = TAIL.take([128, 4, 2052], BF16)
        oT = TAIL.take([128, 4, 2052], BF16)
        hTb = [Buf() for _ in range(5)]
        ycTb = [Buf() for _ in range(5)]
        oTb = [Buf() for _ in range(5)]

        def norm_transpose(src, srcb, rows, g, dst, dstb):
            i = rr["n"] % 2
            rr["n"] += 1
            st = stat[i]
            act(junk[0:rows, :], src, AF.Square, r=[srcb], w=[junkb, statb[i]], accum_out=st[0:rows, 0:1])
            act(st[0:rows, 1:2], st[0:rows, 0:1], AF.Sqrt, r=[statb[i]], w=[statb[i]], scale=1.0 / D, bias=EPS)
            S.emit("dve", lambda e: e.reciprocal(st[0:rows, 2:3], st[0:rows, 1:2]), reads=[statb[i]], writes=[statb[i]])
            ts("dve", xn[i][0:rows, :], src, st[0:rows, 2:3], None, ALU.mult, r=[srcb, statb[i]], w=[xnb[i]])
            for c in range(8):
                tr(PT[:, c, 0:rows], xn[i][0:rows, c * 128:(c + 1) * 128], ident_b[0:rows, 0:rows],
                   r=[xnb[i]], w=[PTb])
            tt("dve", dst, PT[:, :, 0:rows], g.unsqueeze(2).broadcast_to([128, 8, rows]), ALU.mult,
               r=[PTb], w=[dstb])
            return i

        S.enabled = "P1" in phases
        for i in range(NT + 1):
            k = i % 2
            if i < NT:
                rows = 128
                dma("sp", xt[k], x_in[i * 128:(i + 1) * 128, :], w=[xtb[k]])
                dst = hT[:, :, i * 128:(i + 1) * 128]
                db = hTb[i // 4]
            else:
                rows = NM + ND
                dma("sp", xt[k][0:rows, :], xe_in, w=[xtb[k]])
                dst = hT[:, :, 2048:2068]
                db = hTb[4]
            norm_transpose(xt[k][0:rows, :], xtb[k], rows, gmix, dst, db)

        S.enabled = "P2" in phases or "P2a" in phases
        P2 = Reg(big_start, big_end)
        qT = P2.take([128, 4, 2048], BF16)
        kT = P2.take([128, 4, 2064], BF16)
        v_sb = P2.take([128, 16, 512], BF16)
        vmeta = P2.take([NM, 512], BF16)
        qTb = [Buf() for _ in range(4)]
        kTb = [Buf() for _ in range(5)]
        vb = [Buf() for _ in range(17)]
        p3_start = P2.cur
        wsl = [P2.take([128, 8, 512], BF16) for _ in range(3)]
        wslb = [Buf() for _ in range(3)]
        ubuf = [P2.take([128, 2 + NM + T]) for _ in range(2)]
        ubufb = [Buf(), Buf()]
        cgt = [P2.take([128, 512]) for _ in range(2)]
        cgtb = [Buf(), Buf()]
        cvt = [P2.take([128, 512]) for _ in range(2)]
        cvtb = [Buf(), Buf()]
        w_in_v = w_in.rearrange("(c p) n -> p c n", p=128)

        def load_w(slot, g):
            dma("pool", wsl[slot], w_in_v[:, :, g * 512:(g + 1) * 512], w=[wslb[slot]])

        CH = [(c * 512, 512, c) for c in range(4)]

        def feat_mm(slot, nsub, col0, n, hb):
            pb, pbb = bank()
            for c in range(8):
                mm(pb[:, 0:n], wsl[slot][:, c, nsub * 128:(nsub + 1) * 128], hT[:, c, col0:col0 + n],
                   c == 0, c == 7, r=[wslb[slot], hb], w=[pbb])
            return pb, pbb

        def tok_mm(slot, col0, rows, hb):
            pb, pbb = bank()
            for c in range(8):
                mm(pb[0:rows, :], hT[:, c, col0:col0 + rows], wsl[slot][:, c, :], c == 0, c == 7,
                   r=[wslb[slot], hb], w=[pbb])
            return pb, pbb

        def stg():
            i = rr["st"] % 2
            rr["st"] += 1
            return stage[i], stageb[i]

        load_w(0, 1)
        load_w(1, 2)
        load_w(2, 0)
        import os
        DBG = os.environ.get("KDBG", "")
        if DBG == "loadonly":
            S.enabled = False
        for nsub in range(4):
            for (c0, n, ci) in CH + [(2048, NM, 4)]:
                pb, pbb = feat_mm(0, nsub, c0, n, hTb[ci])
                cpy("act", kT[:, nsub, c0:c0 + n], pb[:, 0:n], r=[pbb], w=[kTb[ci]])
        if DBG == "featonly":
            S.enabled = False
        for i in range(NT + 1):
            if DBG == "noext" and i == NT:
                continue
            if DBG == "extonly" and i < NT:
                continue
            rows = 128 if i < NT else NM + ND
            c0 = i * 128 if i < NT else 2048
            hb = hTb[i // 4] if i < NT else hTb[4]
            pb, pbb = tok_mm(0, c0, rows, hb)
            sg, sgb = stg()
            cpy("act", sg[0:rows, :], pb[0:rows, :], r=[pbb], w=[sgb])
            if i < NT:
                dma("sp", kp_out[NM + i * 128:NM + (i + 1) * 128, :], sg, r=[sgb])
            else:
                dma("sp", kp_out[0:NM, :], sg[0:NM, :], r=[sgb])
                dma("sp", ks_out, sg[NM:NM + ND, :], r=[sgb])
            pb, pbb = tok_mm(1, c0, rows, hb)
            sg, sgb = stg()
            cpy("act", sg[0:rows, :], pb[0:rows, :], r=[pbb], w=[sgb])
            if i < NT:
                cpy("dve", v_sb[:, i, :], pb, r=[pbb], w=[vb[i]])
                dma("sp", vp_out[NM + i * 128:NM + (i + 1) * 128, :], sg, r=[sgb])
            else:
                cpy("dve", vmeta, pb[0:NM, :], r=[pbb], w=[vb[16]])
                dma("sp", vp_out[0:NM, :], sg[0:NM, :], r=[sgb])
                dma("sp", vs_out, sg[NM:NM + ND, :], r=[sgb])
        if DBG in ("noq", "noext", "extonly", "noout"):
            S.enabled = False
        for nsub in range(4):
            for (c0, n, ci) in CH:
                pb, pbb = feat_mm(2, nsub, c0, n, hTb[ci])
                S.emit("act", (lambda o, i_: (lambda e: e.mul(o, i_, 0.125)))(qT[:, nsub, c0:c0 + n], pb[:, 0:n]),
                       reads=[pbb], writes=[qTb[ci]])
        qtokb = Buf()
        pb, pbb = tok_mm(2, 2048, NM + ND, hTb[4])
        cpy("act", qtok, pb[0:NM + ND, :], r=[pbb], w=[qtokb])

        S.enabled = "P2" in phases or "P2b" in phases
        load_w(0, 3)
        load_w(1, 4)
        load_w(2, 5)
        udecb = Buf()
        cpb = Buf()
        for nsub in range(4):
            ui = nsub % 2
            u = ubuf[ui]
            ub = ubufb[ui]
            mset("pool", u[:, 0:2], 0.0, w=[ub])
            w0 = cw[:, nsub * 3 + 0:nsub * 3 + 1]
            w1 = cw[:, nsub * 3 + 1:nsub * 3 + 2]
            w2 = cw[:, nsub * 3 + 2:nsub * 3 + 3]
            for (c0, n, ci) in [(2048, NM + ND, 4)] + CH:
                k = rr["n"] % 2
                rr["n"] += 1
                pcg, pcgb = feat_mm(1, nsub, c0, n, hTb[ci])
                pxc, pxcb = feat_mm(2, nsub, c0, n, hTb[ci])
                pbg, pbgb = feat_mm(0, nsub, c0, n, hTb[ci])
                cpy("act", cgt[k][:, 0:n], pcg[:, 0:n], r=[pcgb], w=[cgtb[k]])
                if ci == 4:
                    tt("dve", u[:, 2:2 + NM], pxc[:, 0:NM], cgt[k][:, 0:NM], ALU.mult, r=[pxcb, cgtb[k]], w=[ub])
                    tt("dve", udec[:, nsub, :], pxc[:, NM:NM + ND], cgt[k][:, NM:NM + ND], ALU.mult,
                       r=[pxcb, cgtb[k]], w=[udecb])
                    cv = cvt[k]
                    ts("dve", cv[:, 0:ND], scT[:, nsub, :, 0], w0, None, ALU.mult, r=[], w=[cvtb[k]])
                    stt(cv[:, 0:ND], scT[:, nsub, :, 1], w1, cv[:, 0:ND], ALU.mult, ALU.add, r=[cvtb[k]], w=[cvtb[k]])
                    stt(cv[:, 0:ND], udec[:, nsub, :], w2, cv[:, 0:ND], ALU.mult, ALU.add, r=[cvtb[k], udecb], w=[cvtb[k]])
                    tt("dve", ycT[:, nsub, 2048:2052], pbg[:, NM:NM + ND], cv[:, 0:ND], ALU.mult,
                       r=[pbgb, cvtb[k]], w=[ycTb[4]])
                else:
                    uo = 2 + NM + c0
                    tt("dve", u[:, uo:uo + n], pxc[:, 0:n], cgt[k][:, 0:n], ALU.mult, r=[pxcb, cgtb[k]], w=[ub])
                    cv = cvt[k]
                    ts("pool", cv[:, 0:n], u[:, uo - 2:uo - 2 + n], w0, None, ALU.mult, r=[ub], w=[cvtb[k]])
                    stt(cv[:, 0:n], u[:, uo - 1:uo - 1 + n], w1, cv[:, 0:n], ALU.mult, ALU.add, r=[ub, cvtb[k]], w=[cvtb[k]])
                    stt(cv[:, 0:n], u[:, uo:uo + n], w2, cv[:, 0:n], ALU.mult, ALU.add, r=[ub, cvtb[k]], w=[cvtb[k]])
                    tt("dve", ycT[:, nsub, c0:c0 + n], pbg[:, 0:n], cv[:, 0:n], ALU.mult, r=[pbgb, cvtb[k]], w=[ycTb[ci]])
            if "P2" in phases or "P2c" in phases:
              dma("sp", cp_out[:, nsub * 128:(nsub + 1) * 128].rearrange("r p -> p r"),
                u[:, 2 + NM + T - 2:2 + NM + T], r=[ub], w=[cpb], allow_slow_non_contiguous=True)
        S.enabled = "P2" in phases or "P2c" in phases
        dma("sp", cs_out[:, 0, :], sc_in[:, 1, :], w=[cpb])
        for n_ in range(4):
            dma("sp", cs_out[:, 1, n_ * 128:(n_ + 1) * 128].rearrange("s p -> p s"), udec[:, n_, :], r=[udecb], w=[cpb],
                allow_slow_non_contiguous=True)

        S.enabled = True
        S.barrier()
        run_p3 = "P3" in phases
        run_pd = "PD" in phases
        P3 = Reg(p3_start, big_end)
        e_t = [P3.take([128, 512]) for _ in range(2)]
        e_b = [Buf(), Buf()]
        sp_t = [P3.take([128, 512], BF16) for _ in range(3)]
        sp_b = [Buf() for _ in range(3)]
        tmp_t = [P3.take([128, 512]) for _ in range(2)]
        tmp_b = [Buf(), Buf()]
        w_t = [P3.take([128, 512], BF16) for _ in range(3)]
        w_b = [Buf() for _ in range(3)]
        R_t = [P3.take([128, 512]) for _ in range(3)]
        R_b = [Buf() for _ in range(3)]

        TK = 2
        NCH = 128 // TK
        GA = Reg(GEN.start, GEN.end)
        z_t = GA.take([128, 8, 128])
        e2_t = GA.take([128, 8, 128])
        s2_t = GA.take([128, 8, 128])
        pf_t = GA.take([128, 8, 128])
        zb, e2b, s2b, pfb = Buf(), Buf(), Buf(), Buf()
        kch = [P3.take([128, TK * 512]) for _ in range(3)]
        kchb = [Buf(), Buf(), Buf()]
        vch = [P3.take([128, TK * 512], BF16) for _ in range(3)]
        vchb = [Buf(), Buf(), Buf()]
        idxa = P3.take([128, 2, NCH], I32)
        idxb = Buf()
        qb_t = P3.take([128, 512])
        qbb = Buf()
        w2_t = P3.take([128, 8, 128])
        w2b = Buf()
        wz = P3.take([128, 128, 16], BF16)
        wzb = Buf()
        tot_t = P3.take([128, 16])
        totb = Buf()
        od_t = P3.take([16, 512])
        odb = Buf()
        odT = P3.take([128, 4, 16])
        odTb = Buf()
        ones_f = P3.take([128, 128])
        onesb = Buf()
        PD_BANK, PD_BANKb = PB[7], PBb[7]

        def gather(out, off, src, ob):
            return S.emit("pool", lambda e: e.indirect_dma_start(
                out=out, out_offset=None, in_=src,
                in_offset=bass.IndirectOffsetOnAxis(ap=off, axis=0)), reads=[idxb], writes=[ob], dma=True)

        def pd_gen():
            for pr in range(2):
                for ch in range(NCH):
                    ts("dve", idxa[:, pr, ch:ch + 1], ptab[:, pr:pr + 1], float(NCH), float(ch), ALU.mult, ALU.add, w=[idxb])
            mset("dve", ones_f, 1.0, w=[onesb])
            mset("pool", wz, 0.0, w=[wzb])
            yield
            for pr in range(2):
                mm(PD_BANK, sel_f[:, pr * 128:(pr + 1) * 128], qtok, True, True, r=[qtokb], w=[PD_BANKb])
                cpy("act", qb_t, PD_BANK, r=[PD_BANKb], w=[qbb])
                def k_reduce(ch_):
                    kk = ch_ % 3
                    S.emit("dve", (lambda o, i_: (lambda e: e.tensor_reduce(o, i_, AX.X, ALU.add)))(
                        z_t[:, :, ch_ * TK:(ch_ + 1) * TK].rearrange("p h t -> p t h"),
                        kch[kk].rearrange("p (t h d) -> p t h d", t=TK, h=8)), reads=[kchb[kk]], writes=[zb])

                gather(kch[0], idxa[:, pr, 0:1], ck_in, kchb[0])
                gather(kch[1], idxa[:, pr, 1:2], ck_in, kchb[1])
                for ch in range(NCH):
                    k = ch % 3
                    if ch >= 1:
                        k_reduce(ch - 1)
                    k3 = kch[k].rearrange("p (t n) -> p t n", t=TK)
                    tt("pool", k3, k3, qb_t.unsqueeze(1).broadcast_to([128, TK, 512]), ALU.mult,
                       r=[kchb[k], qbb], w=[kchb[k]])
                    if ch + 2 < NCH:
                        gather(kch[(ch + 2) % 3], idxa[:, pr, ch + 2:ch + 3], ck_in, kchb[(ch + 2) % 3])
                    yield "K"
                k_reduce(NCH - 1)
                stt(z_t, z_t, 0.125, sbb.unsqueeze(2).broadcast_to([128, 8, 128]), ALU.mult, ALU.add, r=[zb], w=[zb])
                act(e2_t, z_t, AF.Exp, r=[zb], w=[e2b])
                act(s2_t, e2_t, AF.Ln, r=[e2b], w=[s2b], bias=1.0)
                for h in range(8):
                    S.emit("dve", (lambda o, d0, d1: (lambda e: e.tensor_tensor_scan(o, d0, d1, 0.0, ALU.mult, ALU.add)))(
                        pf_t[:, h, :], ones_f, s2_t[:, h, :]), reads=[s2b, onesb], writes=[pfb])
                cpy("dve", tot_t[:, 0:8], pf_t[:, :, 127], r=[pfb], w=[totb])
                yield
                mm(PD_BANK[:, 0:8], ustr_f, tot_t[:, 0:8], True, True, r=[totb], w=[PD_BANKb])
                tt("dve", tot_t[:, 8:16], tot_t[:, 0:8], PD_BANK[:, 0:8], ALU.add, r=[totb, PD_BANKb], w=[totb])
                tt("dve", w2_t, pf_t, s2_t, ALU.subtract, r=[pfb, s2b], w=[w2b])
                tt("dve", w2_t, w2_t, z_t, ALU.add, r=[w2b, zb], w=[w2b])
                tt("dve", w2_t, w2_t, tot_t[:, 8:16].unsqueeze(2).broadcast_to([128, 8, 128]), ALU.subtract,
                   r=[w2b, totb], w=[w2b])
                act(e2_t, w2_t, AF.Exp, r=[w2b], w=[e2b])
                cpy("dve", wz[0:64, :, 0:8], e2_t[0:64].rearrange("p h t -> p t h"), r=[e2b], w=[wzb])
                cpy("dve", wz[64:128, :, 8:16], e2_t[64:128].rearrange("p h t -> p t h"), r=[e2b], w=[wzb])
                yield
                gather(vch[0], idxa[:, pr, 0:1], cv_in, vchb[0])
                gather(vch[1], idxa[:, pr, 1:2], cv_in, vchb[1])
                for ch in range(NCH):
                    k = ch % 3
                    for t in range(TK):
                        tok = ch * TK + t
                        mm(PD_BANK[0:16, :], wz[:, tok, :], vch[k][:, t * 512:(t + 1) * 512], tok == 0, tok == 127,
                           r=[wzb, vchb[k]], w=[PD_BANKb])
                    if ch + 2 < NCH:
                        gather(vch[(ch + 2) % 3], idxa[:, pr, ch + 2:ch + 3], cv_in, vchb[(ch + 2) % 3])
                    yield "V"
                cpy("act", od_t, PD_BANK[0:16, :], r=[PD_BANKb], w=[odb])
                for hq in range(4):
                    S.emit("pe", (lambda o, i_: (lambda e: e.transpose(o, i_, ident_f[0:16, 0:16])))(
                        PD_BANK[:, hq * 16:(hq + 1) * 16], od_t[:, hq * 128:(hq + 1) * 128]), reads=[odb], writes=[PD_BANKb])
                cpy("act", odT.rearrange("p a b -> p (a b)"), PD_BANK[:, 0:64], r=[PD_BANKb], w=[odTb])
                for h in range(8):
                    hq, hp = h // 2, h % 2
                    src = odT[hp * 64:(hp + 1) * 64, hq, h:h + 9:8]
                    cpy("dve", oT[hp * 64:(hp + 1) * 64, hq, 2048 + 2 * pr:2048 + 2 * pr + 2], src,
                        r=[odTb], w=[oTb[4]])
                yield

        pdg = pd_gen() if run_pd else iter(())

        pd_state = {"ph": "K", "n": 0}

        def pd_step(force=False):
            pd_state["n"] += 1
            if not force and pd_state["ph"] == "K" and pd_state["n"] % 3 == 0:
                return True
            S.enabled = True
            try:
                pd_state["ph"] = next(pdg)
            except StopIteration:
                return False
            finally:
                S.enabled = run_p3
            return True

        pairs = []
        for h in range(8):
            for c in range(4):
                blocks = list(range(4 * c + 3, -1, -1)) + [-1]
                for bi, i in enumerate(blocks):
                    pairs.append(dict(h=h, c=c, i=i, first=(bi == 0), last=(i == -1), g=h * 4 + c))
        NP = len(pairs)
        state = {"R": None, "ri": 0}

        def geom(p):
            h, c, i = p["h"], p["c"], p["i"]
            hq, hp = h // 2, h % 2
            if i >= 0:
                ns, kcol, kb = 128, i * 128, kTb[i // 4]
                vap, vbb = v_sb[:, i, h * 64:(h + 1) * 64], vb[i]
            else:
                ns, kcol, kb = NM, 2048, kTb[4]
                vap, vbb = vmeta[:, h * 64:(h + 1) * 64], vb[16]
            j = i - 4 * c
            t0 = j * 128 if j > 0 else 0
            return hq, hp * 64, ns, kcol, kb, vap, vbb, t0, (j >= 0)

        def S1(k):
            p = pairs[k]
            h, c = p["h"], p["c"]
            hq, p0, ns, kcol, kb, vap, vbb, t0, diag = geom(p)
            ZB, ZBb = PB[k % 3], PBb[k % 3]
            Z = ZB[0:ns, t0:512]
            mm(Z, kT[p0:p0 + 64, hq, kcol:kcol + ns], qT[p0:p0 + 64, hq, c * 512 + t0:(c + 1) * 512],
               True, True, r=[kb, qTb[c]], w=[ZBb])
            if diag:
                mm(ZB[:, t0:t0 + 128], ident_b, dmask_b, False, True, w=[ZBb], skip=True)
            e = e_t[k % 2][0:ns, t0:512]
            act(e, Z, AF.Exp, r=[ZBb], w=[e_b[k % 2]], bias=sbb[0:ns, h:h + 1])
            spv = sp_t[k % 3][0:ns, t0:512]
            act(spv, e, AF.Ln, r=[e_b[k % 2]], w=[sp_b[k % 3]], bias=1.0)

        def S2(k):
            p = pairs[k]
            h, c = p["h"], p["c"]
            hq, p0, ns, kcol, kb, vap, vbb, t0, diag = geom(p)
            ZB, ZBb = PB[k % 3], PBb[k % 3]
            g = p["g"]
            CB, CBb = PB[3 + g % 2], PBb[3 + g % 2]
            Z = ZB[0:ns, t0:512]
            spv = sp_t[k % 3][0:ns, t0:512]
            mm(Z, ntri_b[0:ns, 0:ns], spv, False, True, r=[sp_b[k % 3]], w=[ZBb], skip=True)
            if not p["last"]:
                mm(CB[:, t0:512], nones_b[0:ns, :], spv, p["first"], True, r=[sp_b[k % 3]], w=[CBb],
                   skip=not p["first"])
            wv = w_t[k % 3][0:ns, t0:512]
            Rcur = state["R"]
            if p["first"]:
                act(wv, Z, AF.Exp, r=[ZBb], w=[w_b[k % 3]], bias=sbb[0:ns, h:h + 1])
            else:
                tm = tmp_t[k % 2][0:ns, t0:512]
                tt("dve", tm, Z, Rcur[0][0:ns, t0:512], ALU.add, r=[ZBb, Rcur[1]], w=[tmp_b[k % 2]])
                act(wv, tm, AF.Exp, r=[tmp_b[k % 2]], w=[w_b[k % 3]], bias=sbb[0:ns, h:h + 1])
            if not p["last"]:
                ri = state["ri"]
                state["ri"] = (ri + 1) % 3
                Rn = (R_t[ri], R_b[ri])
                if t0 > 0:
                    mset("dve", Rn[0][:, 0:t0], 0.0, w=[Rn[1]])
                cpy("act" if k % 4 == 0 else "dve", Rn[0][:, t0:512], CB[:, t0:512], r=[CBb], w=[Rn[1]])
                state["R"] = Rn
            else:
                state["R"] = None

        def S3(k):
            p = pairs[k]
            h, c, g = p["h"], p["c"], p["g"]
            hq, p0, ns, kcol, kb, vap, vbb, t0, diag = geom(p)
            OB, OBb = PB[5 + g % 2], PBb[5 + g % 2]
            if p["first"]:
                mm(OB[p0:p0 + 64, :], zeros_b[:, 0:64], zeros_b[:, 0:512], True, False, w=[OBb])
            wv = w_t[k % 3][0:ns, t0:512]
            mm(OB[p0:p0 + 64, t0:512], vap, wv, False, p["last"], r=[vbb, w_b[k % 3]], w=[OBb])
            if p["last"]:
                cpy("act", oT[p0:p0 + 64, hq, c * 512:(c + 1) * 512], OB[p0:p0 + 64, :], r=[OBb], w=[oTb[c]])

        S.enabled = run_p3
        for s in range(NP + 2):
            if s < NP:
                S1(s)
            if 0 <= s - 1 < NP:
                S2(s - 1)
            if 0 <= s - 2 < NP:
                S3(s - 2)
            pd_step()
        while pd_step(True):
            pass

        S.enabled = True
        S.barrier()
        S.enabled = "P4" in phases
        P4 = Reg(big_start, big_end)
        wga = P4.take([128, 8, 1024], BF16)
        wgb = P4.take([128, 8, 1024], BF16)
        wao = P4.take([128, 4, 1024], BF16)
        wco = P4.take([128, 4, 1024], BF16)
        wo = P4.take([128, 8, 1024], BF16)
        w4b = [Buf() for _ in range(5)]
        dma("pool", wao, w_ao.rearrange("(c p) n -> p c n", p=128), w=[w4b[2]])
        dma("pool", wco, w_co.rearrange("(c p) n -> p c n", p=128), w=[w4b[3]])
        for hf in range(2):
            dma("pool", wga[:, :, hf * 512:(hf + 1) * 512], w_in_v[:, :, 3072 + hf * 512:3072 + (hf + 1) * 512], w=[w4b[0]])
            dma("pool", wgb[:, :, hf * 512:(hf + 1) * 512], w_in_v[:, :, 4096 + hf * 512:4096 + (hf + 1) * 512], w=[w4b[1]])
        w_o_v = w_o.rearrange("(c p) n -> p c n", p=128)
        for hf in range(2):
            dma("pool", wo[:, :, hf * 512:(hf + 1) * 512], w_o_v[:, :, hf * 512:(hf + 1) * 512], w=[w4b[4]])
        mT = [P4.take([128, 8, 512], BF16) for _ in range(2)]
        mTb = [Buf(), Buf()]
        sga = [P4.take([128, 512]) for _ in range(2)]
        sgab = [Buf(), Buf()]
        sgb_ = [P4.take([128, 512]) for _ in range(2)]
        sgbb = [Buf(), Buf()]
        t1 = [P4.take([128, 512]) for _ in range(2)]
        t1b = [Buf(), Buf()]
        x1t = [P4.take([128, D]) for _ in range(2)]
        x1tb = [Buf(), Buf()]
        x1sb = Buf()
        it = [0]
        for (hc0, oc0, n, ci) in [(c * 512, c * 512, 512, c) for c in range(4)] + [(2064, 2048, ND, 4)]:
            mi = ci % 2
            for dmc in range(8):
                k = it[0] % 2
                it[0] += 1
                dsl = slice(dmc * 128, (dmc + 1) * 128)
                A, Ab = bank()
                for c in range(4):
                    mm(A[:, 0:n], wao[:, c, dsl], oT[:, c, oc0:oc0 + n], c == 0, c == 3, r=[w4b[2], oTb[ci]], w=[Ab])
                Bk, Bb = bank()
                for c in range(4):
                    mm(Bk[:, 0:n], wco[:, c, dsl], ycT[:, c, oc0:oc0 + n], c == 0, c == 3, r=[w4b[3], ycTb[ci]], w=[Bb])
                G, Gb = bank()
                for c in range(8):
                    mm(G[:, 0:n], wga[:, c, dsl], hT[:, c, hc0:hc0 + n], c == 0, c == 7, r=[w4b[0], hTb[ci]], w=[Gb])
                H, Hb = bank()
                for c in range(8):
                    mm(H[:, 0:n], wgb[:, c, dsl], hT[:, c, hc0:hc0 + n], c == 0, c == 7, r=[w4b[1], hTb[ci]], w=[Hb])
                act(sga[k][:, 0:n], G[:, 0:n], AF.Sigmoid, r=[Gb], w=[sgab[k]])
                act(sgb_[k][:, 0:n], H[:, 0:n], AF.Sigmoid, r=[Hb], w=[sgbb[k]])
                tt("dve", t1[k][:, 0:n], A[:, 0:n], sga[k][:, 0:n], ALU.mult, r=[Ab, sgab[k]], w=[t1b[k]])
                tt("dve", sgb_[k][:, 0:n], Bk[:, 0:n], sgb_[k][:, 0:n], ALU.mult, r=[Bb, sgbb[k]], w=[sgbb[k]])
                tt("pool", mT[mi][:, dmc, 0:n], t1[k][:, 0:n], sgb_[k][:, 0:n], ALU.add, r=[t1b[k], sgbb[k]], w=[mTb[mi]])
            ntile = 4 if ci < 4 else 1
            for tl in range(ntile):
                rows = 128 if ci < 4 else ND
                k = it[0] % 2
                it[0] += 1
                if ci < 4:
                    row0 = ci * 512 + tl * 128
                    dma("sp", x1t[k], x_in[row0:row0 + 128, :], w=[x1tb[k]])
                else:
                    row0 = T
                    dma("sp", x1t[k][0:ND, :], xe_in[NM:NM + ND, :], w=[x1tb[k]])
                for hf in range(2):
                    Pk, Pb_ = bank()
                    for c in range(8):
                        mm(Pk[0:rows, :], mT[mi][:, c, tl * 128:tl * 128 + rows], wo[:, c, hf * 512:(hf + 1) * 512],
                           c == 0, c == 7, r=[mTb[mi], w4b[4]], w=[Pb_])
                    tt("dve", x1t[k][0:rows, hf * 512:(hf + 1) * 512], Pk[0:rows, :],
                       x1t[k][0:rows, hf * 512:(hf + 1) * 512], ALU.add, r=[Pb_, x1tb[k]], w=[x1tb[k]])
                dma("sp", x1s[row0:row0 + rows, :], x1t[k][0:rows, :], r=[x1tb[k]], w=[x1sb])

        S.enabled = True
        S.barrier()
        S.enabled = "P5" in phases
        P5 = Reg(big_start, TOTAL)
        wd = P5.take([128, NF, 1024], BF16)
        wdb = Buf()
        w_d_v = w_d.rearrange("(c p) n -> p c n", p=128)
        def load_wd():
            for hf in range(2):
                for fh in range(2):
                    dma("pool", wd[:, fh * 11:(fh + 1) * 11, hf * 512:(hf + 1) * 512],
                        w_d_v[:, fh * 11:(fh + 1) * 11, hf * 512:(hf + 1) * 512], w=[wdb])
        HC = 1024 + ND
        aT = P5.take([128, NF, HC], BF16)
        aTb = [Buf() for _ in range(3)]
        h2 = P5.take([128, 8, HC], BF16)
        h2b = [Buf() for _ in range(3)]
        x1q = P5.take([128, 9, D])
        x1qb = [Buf() for _ in range(9)]
        wgs = [P5.take([128, 8, 256], BF16) for _ in range(2)]
        wgsb = [Buf(), Buf()]
        wus = [P5.take([128, 8, 256], BF16) for _ in range(2)]
        wusb = [Buf(), Buf()]
        sg_t = [P5.take([128, 512]) for _ in range(2)]
        sg_b = [Buf(), Buf()]
        yst = [P5.take([128, D]) for _ in range(2)]
        ystb = [Buf(), Buf()]
        w_g_v = w_g.rearrange("(c p) n -> p c n", p=128)
        w_u_v = w_u.rearrange("(c p) n -> p c n", p=128)
        wl = [0]
        for hh in range(2):
            ntl = 9 if hh == 1 else 8
            for tl in range(ntl):
                rows = 128 if tl < 8 else ND
                row0 = hh * 1024 + tl * 128 if tl < 8 else T
                dma("sp", x1q[0:rows, tl, :], x1s[row0:row0 + rows, :], r=[x1sb], w=[x1qb[tl]])
                norm_transpose(x1q[0:rows, tl, :], x1qb[tl], rows, gffn,
                               h2[:, :, tl * 128:tl * 128 + rows], h2b[tl // 4])
            cols = [(0, 512, 0), (512, 512, 1)] + ([(1024, ND, 2)] if hh == 1 else [])
            for fg in range(11):
                s = wl[0] % 2
                wl[0] += 1
                dma("pool", wgs[s], w_g_v[:, :, fg * 256:(fg + 1) * 256], w=[wgsb[s]])
                dma("pool", wus[s], w_u_v[:, :, fg * 256:(fg + 1) * 256], w=[wusb[s]])
                if hh == 0 and fg == 1:
                    load_wd()
                for fs in range(2):
                    f = fg * 2 + fs
                    for (c0, n, cb) in cols:
                        k = it[0] % 2
                        it[0] += 1
                        G, Gb = bank()
                        for c in range(8):
                            mm(G[:, 0:n], wgs[s][:, c, fs * 128:(fs + 1) * 128], h2[:, c, c0:c0 + n],
                               c == 0, c == 7, r=[wgsb[s], h2b[cb]], w=[Gb])
                        U, Ub = bank()
                        for c in range(8):
                            mm(U[:, 0:n], wus[s][:, c, fs * 128:(fs + 1) * 128], h2[:, c, c0:c0 + n],
                               c == 0, c == 7, r=[wusb[s], h2b[cb]], w=[Ub])
                        act(sg_t[k][:, 0:n], G[:, 0:n], AF.Silu, r=[Gb], w=[sg_b[k]])
                        tt("dve", aT[:, f, c0:c0 + n], U[:, 0:n], sg_t[k][:, 0:n], ALU.mult, r=[Ub, sg_b[k]], w=[aTb[cb]])
            for tl in range(ntl):
                rows = 128 if tl < 8 else ND
                row0 = hh * 1024 + tl * 128 if tl < 8 else T
                xq = x1q[0:rows, tl, :]
                xqb = x1qb[tl]
                for hf in range(2):
                    Pk, Pb_ = bank()
                    for f in range(NF):
                        mm(Pk[0:rows, :], aT[:, f, tl * 128:tl * 128 + rows], wd[:, f, hf * 512:(hf + 1) * 512],
                           f == 0, f == NF - 1, r=[aTb[tl // 4], wdb], w=[Pb_])
                    tt("dve", xq[:, hf * 512:(hf + 1) * 512], Pk[0:rows, :], xq[:, hf * 512:(hf + 1) * 512], ALU.add,
                       r=[Pb_, xqb], w=[xqb])
                i = rr["n"] % 2
                rr["n"] += 1
                st = stat[i]
                act(junk[0:rows, :], xq, AF.Square, r=[xqb], w=[junkb, statb[i]], accum_out=st[0:rows, 0:1])
                act(st[0:rows, 1:2], st[0:rows, 0:1], AF.Sqrt, r=[statb[i]], w=[statb[i]], scale=1.0 / D, bias=EPS)
                S.emit("dve", (lambda o, i_: (lambda e: e.reciprocal(o, i_)))(st[0:rows, 2:3], st[0:rows, 1:2]),
                       reads=[statb[i]], writes=[statb[i]])
                yk = it[0] % 2
                it[0] += 1
                stt(yst[yk][0:rows, :], xq, st[0:rows, 2:3], gfin[0:rows, :], ALU.mult, ALU.mult,
                    r=[xqb, statb[i]], w=[ystb[yk]])
                if tl < 8:
                    dma("sp", y_out[row0:row0 + 128, :], yst[yk], r=[ystb[yk]])
                else:
                    dma("sp", ys_out, yst[yk][0:ND, :], r=[ystb[yk]])

        S.enabled = True
        replay = S.finalize(sems, dsems)
        with nc.Block() as block:
            @block.tensor
            def _(e):
                replay("pe", e)

            @block.scalar
            def _(e):
                replay("act", e)

            @block.vector
            def _(e):
                replay("dve", e)

            @block.gpsimd
            def _(e):
                replay("pool", e)

            @block.sync
            def _(e):
                replay("sp", e)
    return nc


_CACHE = {}


def _consts():
    ident = np.eye(128, dtype=np.float32)
    j = np.arange(128)[:, None]
    s = np.arange(128)[None, :]
    ntri = np.where(j >= s, -1.0, 0.0).astype(np.float32)
    dmask = np.where(j < s, 0.0, -30000.0).astype(np.float32)
    sel = np.zeros((NM + ND, 256), np.float32)
    for pr in range(2):
        for p in range(128):
            sel[NM + 2 * pr + p // 64, pr * 128 + p] = 1.0
    ustr = np.where((j > s) & ((j // 64) == (s // 64)), 1.0, 0.0).astype(np.float32)
    return ident, ntri, dmask, sel, ustr


def _prepare(x_prompt, x_sample, cache_k, cache_v, state_conv, page_table, meta_tokens,
             norm_mix, w_in, sb_bias, conv_w, w_att_out, w_conv_out, w_o, norm_ffn,
             w_gate, w_up, w_down, norm_final, cores=range(NCORES)):
    f = lambda a: np.ascontiguousarray(np.asarray(a, dtype=np.float32))
    x_prompt = f(x_prompt)
    x_sample = f(x_sample)
    ck = f(cache_k).reshape(NPHYS * 64, 1024)
    cv = f(cache_v).reshape(NPHYS * 64, 1024)
    state_conv = f(state_conv)
    page_table = np.asarray(page_table, dtype=np.int32)
    meta = f(meta_tokens)
    ident, ntri, dmask, sel, ustr = _consts()
    col = lambda v: np.ascontiguousarray(f(v).reshape(8, 128).T)
    shared = {
        "cache_k": ck, "cache_v": cv,
        "w_in": f(w_in)[0], "w_att_out": f(w_att_out)[0], "w_conv_out": f(w_conv_out)[0], "w_o": f(w_o)[0],
        "w_gate": f(w_gate)[0], "w_up": f(w_up)[0], "w_down": f(w_down)[0],
        "nmix": col(norm_mix[0]), "nffn": col(norm_ffn[0]), "nfin": f(norm_final).reshape(1, D),
        "sbb": f(sb_bias).reshape(1, 8),
        "cw": np.ascontiguousarray(f(conv_w)[0].reshape(3, 4, 128).transpose(2, 1, 0).reshape(128, 12)),
        "ident": ident, "ntri": ntri, "dmask": dmask, "sel": sel, "ustr": ustr,
    }
    in_maps = []
    for c in cores:
        pt = page_table[4 * c:4 * c + 4]
        ptc = np.ascontiguousarray(pt.reshape(2, 128).T).astype(np.int32)
        m = dict(shared)
        m["x"] = x_prompt[c]
        m["xe"] = np.ascontiguousarray(np.concatenate([meta, x_sample[4 * c:4 * c + 4, 0, :]], axis=0))
        m["state_conv"] = np.ascontiguousarray(state_conv[0, 4 * c:4 * c + 4])
        m["pt"] = ptc
        in_maps.append(m)
    return in_maps


def kernel(**inputs):
    in_maps = _prepare(**inputs)
    if "nc" not in _CACHE:
        _CACHE["nc"] = build_program()
    nc = _CACHE["nc"]
    res = run_bass_kernel_spmd(nc, in_maps, core_ids=list(range(NCORES)))
    R = res.results
    y = np.stack([R[c]["y"] for c in range(NCORES)], axis=0)
    ys = np.concatenate([R[c]["ys"] for c in range(NCORES)], axis=0).reshape(32, 1, D)
    kp = np.stack([R[c]["kp"] for c in range(NCORES)], axis=0).reshape(1, 8, NM + T, 8, 64)
    vp = np.stack([R[c]["vp"] for c in range(NCORES)], axis=0).reshape(1, 8, NM + T, 8, 64)
    cp = np.stack([R[c]["cp"] for c in range(NCORES)], axis=0).reshape(1, 8, 2, 512)
    ks = np.concatenate([R[c]["ks"] for c in range(NCORES)], axis=0).reshape(1, 32, 1, 8, 64)
    vs = np.concatenate([R[c]["vs"] for c in range(NCORES)], axis=0).reshape(1, 32, 1, 8, 64)
    cs = np.concatenate([R[c]["cs"] for c in range(NCORES)], axis=0).reshape(1, 32, 2, 512)
    return (y.astype(np.float32), ys.astype(np.float32), kp.astype(np.float32), vp.astype(np.float32),
            cp.astype(np.float32), ks.astype(np.float32), vs.astype(np.float32), cs.astype(np.float32))
```
